# Optimizing a Trainium2 kernel written in Bass

```python
import math
import jax, jax.numpy as jnp
from jax import lax
import numpy as np

D_MODEL = 1024
BATCH = 4
SEQ = 8192
DEPTH = 1
DEC_BATCH = 128
DEC_SEQ = 1
PAST_LEN = 8192
PAGE_SIZE = 128

HEAD_DIM = 64
MIX_W = D_MODEL
ATT_W = MIX_W // 2
RWKV_W = MIX_W - ATT_W
H_ATT = ATT_W // HEAD_DIM
H_RWKV = RWKV_W // HEAD_DIM
DILATED_BRANCHES = ((128, 1), (512, 4), (2048, 16))
MAX_WINDOW = max(w for w, _ in DILATED_BRANCHES)
N_BUCKETS = 32
BUCKET_MAX_DIST = MAX_WINDOW
LORA_W = 64
LORA_A = 64
SHIFT_W = 3 * RWKV_W + LORA_W + LORA_A
IN_W = 4 * ATT_W + SHIFT_W + RWKV_W
Q_BLOCK = 128
NORM_EPS = 1e-6
GN_EPS = HEAD_DIM * 1e-5

kernel_name = 'dilated_rwkv7_hybrid_step'


def rmsnorm(x, w):
    xf = x.astype(jnp.float32)
    xf = xf * lax.rsqrt(jnp.mean(xf * xf, axis=-1, keepdims=True) + NORM_EPS)
    return (xf * w.astype(jnp.float32)).astype(x.dtype)


def split_cols(z, widths):
    return jnp.split(z, [int(i) for i in np.cumsum(widths)[:-1]], axis=-1)


def t5_bucket(dist):
    max_exact = N_BUCKETS // 2
    nf = jnp.maximum(dist, max_exact).astype(jnp.float32)
    large = max_exact + (jnp.log(nf / max_exact) / math.log(BUCKET_MAX_DIST / max_exact)
                         * (N_BUCKETS - max_exact)).astype(jnp.int32)
    large = jnp.minimum(large, N_BUCKETS - 1)
    return jnp.where(dist < max_exact, dist, large)


def dilated_attention(q, k_ext, v_ext, q_idx, rel_bias):
    scale = HEAD_DIM ** -0.5
    lses, outs = [], []
    for window, dil in DILATED_BRANCHES:
        dist = dil * jnp.arange(window // dil + 1, dtype=jnp.int32)
        bias = rel_bias[t5_bucket(dist)].T.astype(jnp.float32)
        idx = q_idx[:, None] - dist[None, :]
        valid = idx >= 0
        idx = jnp.maximum(idx, 0)
        kg = jnp.take(k_ext, idx, axis=1)
        vg = jnp.take(v_ext, idx, axis=1)
        logits = jnp.einsum('bqhd,bqkhd->bhqk', q, kg).astype(jnp.float32) * scale + bias[None, :, None, :]
        logits = jnp.where(valid[None, None], logits, -jnp.inf)
        lse = jax.nn.logsumexp(logits, axis=-1)
        p = jnp.exp(logits - lse[..., None])
        outs.append(jnp.einsum('bhqk,bqkhd->bqhd', p.astype(vg.dtype), vg).astype(jnp.float32))
        lses.append(lse)
    wts = jax.nn.softmax(jnp.stack(lses), axis=0)
    out = jnp.einsum('nbhq,nbqhd->bqhd', wts, jnp.stack(outs))
    return out.astype(q.dtype)


def attention_mixer(q, k_ext, v_ext, q_start, rel_bias):
    B, T, H, Dh = q.shape
    if T > Q_BLOCK and T % Q_BLOCK == 0:
        def blk(j):
            s = j * Q_BLOCK
            qb = lax.dynamic_slice_in_dim(q, s, Q_BLOCK, axis=1)
            q_idx = q_start + s + jnp.arange(Q_BLOCK, dtype=jnp.int32)
            return dilated_attention(qb, k_ext, v_ext, q_idx, rel_bias)
        out = lax.map(blk, jnp.arange(T // Q_BLOCK, dtype=jnp.int32))
        return jnp.moveaxis(out, 0, 1).reshape(B, T, H, Dh)
    q_idx = q_start + jnp.arange(T, dtype=jnp.int32)
    return dilated_attention(q, k_ext, v_ext, q_idx, rel_bias)


def rwkv7_mixer(u, prev_row, wkv0, mu_shift, w0, w_lora_b, a0, a_lora_b, k_k, k_a, r_k, ln_x_w, ln_x_b):
    B, T, _ = u.shape
    f32 = jnp.float32
    u_prev = jnp.concatenate([prev_row[:, None].astype(u.dtype), u[:, :-1]], axis=1)
    um = u + (u_prev - u) * mu_shift
    r, k, v, xw, xa = split_cols(um, (RWKV_W, RWKV_W, RWKV_W, LORA_W, LORA_A))
    w_log = -jax.nn.softplus(-(w0 + jnp.tanh(xw) @ w_lora_b).astype(f32)) - 0.5
    decay = jnp.exp(-jnp.exp(w_log))
    a = jax.nn.sigmoid((a0 + xa @ a_lora_b).astype(f32))
    hd = lambda t: t.astype(f32).reshape(B, T, H_RWKV, HEAD_DIM)
    kk = hd(k * k_k)
    kk = kk / jnp.maximum(jnp.sqrt(jnp.sum(kk * kk, axis=-1, keepdims=True)), 1e-12)
    k = k.astype(f32) * (1 + (a - 1) * k_a.astype(f32))
    r, k, v, decay, a = hd(r), hd(k), hd(v), hd(decay), hd(a)

    def step(S, inp):
        r_t, k_t, v_t, w_t, kk_t, a_t = inp
        sa = jnp.einsum('bhij,bhj->bhi', S, -kk_t)
        S = S * w_t[:, :, None, :] + sa[..., None] * (kk_t * a_t)[:, :, None, :] + v_t[..., None] * k_t[:, :, None, :]
        return S, jnp.einsum('bhij,bhj->bhi', S, r_t)

    xs = tuple(jnp.moveaxis(t, 1, 0) for t in (r, k, v, decay, kk, a))
    wkv_T, ys = lax.scan(step, wkv0.astype(f32), xs)
    y = jnp.moveaxis(ys, 0, 1)
    mean = jnp.mean(y, axis=-1, keepdims=True)
    var = jnp.mean(jnp.square(y - mean), axis=-1, keepdims=True)
    y = ((y - mean) * lax.rsqrt(var + GN_EPS)).reshape(B, T, RWKV_W) * ln_x_w.astype(f32) + ln_x_b.astype(f32)
    bonus = jnp.sum(r * k * r_k.astype(f32), axis=-1, keepdims=True) * v
    y = y + bonus.reshape(B, T, RWKV_W)
    return y.astype(u.dtype), wkv_T.astype(wkv0.dtype), u[:, -1]


def hybrid_layer(x, c, k_past, v_past, wkv0, prev_row, rel_bias, norm_w, ada_w, ada_b, w_in, mu_shift,
                 w0, w_lora_b, a0, a_lora_b, k_k, k_a, r_k, ln_x_w, ln_x_b, w_out):
    B, T, _ = x.shape
    mod = jnp.einsum('bd,de->be', jax.nn.silu(c), ada_w) + ada_b
    shift, scale, gate = jnp.split(mod, 3, axis=-1)
    h = rmsnorm(x, norm_w) * (1 + scale[:, None]) + shift[:, None]
    z = jnp.einsum('btd,de->bte', h, w_in)
    q, k, v, g_att, u, g_rwkv = split_cols(z, (ATT_W, ATT_W, ATT_W, ATT_W, SHIFT_W, RWKV_W))
    heads = lambda t: t.reshape(B, T, H_ATT, HEAD_DIM)
    q, k, v = heads(q), heads(k), heads(v)
    k_ext = k if k_past is None else jnp.concatenate([k_past.astype(k.dtype), k], axis=1)
    v_ext = v if v_past is None else jnp.concatenate([v_past.astype(v.dtype), v], axis=1)
    q_start = k_ext.shape[1] - T
    att = attention_mixer(q, k_ext, v_ext, q_start, rel_bias).reshape(B, T, ATT_W)
    rw, wkv_T, last_row = rwkv7_mixer(u, prev_row, wkv0, mu_shift, w0, w_lora_b, a0, a_lora_b,
                                      k_k, k_a, r_k, ln_x_w, ln_x_b)
    mixed = jnp.concatenate([att * jax.nn.silu(g_att), rw * jax.nn.silu(g_rwkv)], axis=-1)
    x = x + gate[:, None] * jnp.einsum('bte,ed->btd', mixed, w_out)
    return x, k, v, wkv_T, last_row


def setup_inputs(seed: int = 0) -> dict:
    key = jax.random.key(seed)
    ks = jax.random.split(key, 26)
    f32 = jnp.float32
    nrm = lambda k, shape, s: s * jax.random.normal(k, shape, f32)
    wbuf = min(MAX_WINDOW, PAST_LEN)
    return {
        'x_prompt': nrm(ks[0], (BATCH, SEQ, D_MODEL), 1.0),
        'x_sample': nrm(ks[1], (DEC_BATCH, DEC_SEQ, D_MODEL), 1.0),
        'cache_win_k': nrm(ks[2], (DEPTH, DEC_BATCH, wbuf, H_ATT, HEAD_DIM), 1.0),
        'cache_win_v': nrm(ks[3], (DEPTH, DEC_BATCH, wbuf, H_ATT, HEAD_DIM), 1.0),
        'state_wkv': nrm(ks[4], (DEPTH, DEC_BATCH, H_RWKV, HEAD_DIM, HEAD_DIM), 0.3),
        'state_shift': nrm(ks[5], (DEPTH, DEC_BATCH, SHIFT_W), 1.0),
        'c_prompt': nrm(ks[6], (BATCH, D_MODEL), 1.0),
        'c_sample': nrm(ks[7], (DEC_BATCH, D_MODEL), 1.0),
        'rel_bias': nrm(ks[8], (N_BUCKETS, H_ATT), 0.5),
        'norm_w': 1.0 + nrm(ks[9], (DEPTH, D_MODEL), 0.1),
        'ada_w': nrm(ks[10], (DEPTH, D_MODEL, 3 * D_MODEL), 0.5 * D_MODEL ** -0.5),
        'ada_b': nrm(ks[11], (DEPTH, 3 * D_MODEL), 0.02),
        'w_in': nrm(ks[12], (DEPTH, D_MODEL, IN_W), D_MODEL ** -0.5),
        'mu_shift': jax.random.uniform(ks[13], (DEPTH, SHIFT_W), f32),
        'w0': jax.random.uniform(ks[14], (DEPTH, RWKV_W), f32, -2.0, 1.0),
        'w_lora_b': nrm(ks[15], (DEPTH, LORA_W, RWKV_W), 0.5 * LORA_W ** -0.5),
        'a0': nrm(ks[16], (DEPTH, RWKV_W), 0.1),
        'a_lora_b': nrm(ks[17], (DEPTH, LORA_A, RWKV_W), 0.5 * LORA_A ** -0.5),
        'k_k': 0.85 + nrm(ks[18], (DEPTH, RWKV_W), 0.05),
        'k_a': 1.0 + nrm(ks[19], (DEPTH, RWKV_W), 0.05),
        'r_k': nrm(ks[20], (DEPTH, H_RWKV, HEAD_DIM), 0.1),
        'ln_x_w': 1.0 + nrm(ks[21], (DEPTH, RWKV_W), 0.1),
        'ln_x_b': nrm(ks[22], (DEPTH, RWKV_W), 0.02),
        'w_out': nrm(ks[23], (DEPTH, MIX_W, D_MODEL), MIX_W ** -0.5),
        'final_norm_w': 1.0 + nrm(ks[24], (D_MODEL,), 0.1),
    }


def reference(x_prompt, x_sample, cache_win_k, cache_win_v, state_wkv, state_shift, c_prompt, c_sample,
              rel_bias, norm_w, ada_w, ada_b, w_in, mu_shift, w0, w_lora_b, a0, a_lora_b, k_k, k_a, r_k,
              ln_x_w, ln_x_b, w_out, final_norm_w):
    keep = min(MAX_WINDOW, x_prompt.shape[1])
    xp, xs = x_prompt, x_sample
    wkp, wvp, wks, wvs, svp, svs, shp, shs = [], [], [], [], [], [], [], []
    for l in range(DEPTH):
        p_l = (norm_w[l], ada_w[l], ada_b[l], w_in[l], mu_shift[l], w0[l], w_lora_b[l], a0[l], a_lora_b[l],
               k_k[l], k_a[l], r_k[l], ln_x_w[l], ln_x_b[l], w_out[l])
        wkv_zero = jnp.zeros((xp.shape[0], H_RWKV, HEAD_DIM, HEAD_DIM), jnp.float32)
        prev_zero = jnp.zeros((xp.shape[0], SHIFT_W), xp.dtype)
        xp, kp, vp, Sp, lp = hybrid_layer(xp, c_prompt, None, None, wkv_zero, prev_zero, rel_bias, *p_l)
        xs, ksm, vsm, Ss, ls = hybrid_layer(xs, c_sample, cache_win_k[l], cache_win_v[l], state_wkv[l],
                                            state_shift[l], rel_bias, *p_l)
        wkp.append(kp[:, -keep:])
        wvp.append(vp[:, -keep:])
        wks.append(ksm)
        wvs.append(vsm)
        svp.append(Sp)
        svs.append(Ss)
        shp.append(lp)
        shs.append(ls)
    y_prompt = rmsnorm(xp, final_norm_w)
    y_sample = rmsnorm(xs, final_norm_w)
    return (y_prompt, y_sample, jnp.stack(wkp), jnp.stack(wvp), jnp.stack(wks), jnp.stack(wvs),
            jnp.stack(svp), jnp.stack(svs), jnp.stack(shp), jnp.stack(shs))
```

```python
import math
import numpy as np
import concourse.bass as bass
import concourse.mybir as mybir
from concourse.ap import AP
from concourse.bass_utils import run_bass_kernel_spmd
from contextlib import ExitStack

F32 = mybir.dt.float32
BF16 = mybir.dt.bfloat16
AF = mybir.ActivationFunctionType
ALU = mybir.AluOpType
AX = mybir.AxisListType

D = 1024
NH = 8
HD = 64
INW = 4224
SHW = 1664
WIN = 2048
NKT = 17
STRIPW = NKT * 128
FLEN = STRIPW + 128
C0 = math.exp(-0.5)
NEG = -30000.0
NSQ = 6
SBW = 256
TPS = SBW // 128
NORM_EPS = 1e-6
GN_EPS = HD * 1e-5


class Buf:
    __slots__ = ("name", "w", "rs")

    def __init__(self, name):
        self.name = name
        self.w = None
        self.rs = []


class Op:
    __slots__ = ("eng", "fn", "deps", "sig", "val", "dma", "idx")


class Sched:
    ENGS = ["pe", "act", "dve", "pool", "sp"]

    def __init__(self, nc, stack, n_dma=40):
        self.nc = nc
        self.ops = {e: [] for e in self.ENGS}
        self.sem = {e: stack.enter_context(nc.semaphore("s_" + e)) for e in ["pe", "act", "dve", "pool"]}
        self.dsem = [stack.enter_context(nc.semaphore("d%d" % i)) for i in range(n_dma)]
        self.dcnt = [0] * n_dma
        self.dlast = [None] * n_dma
        self.drr = 0
        self.n = 0

    def add(self, eng, fn, reads=(), writes=(), dma=False):
        op = Op()
        op.eng = eng
        op.fn = fn
        op.sig = False
        op.val = None
        op.dma = None
        op.idx = self.n
        self.n += 1
        deps = {}
        for b in reads:
            if b.w is not None:
                deps[b.w.idx] = b.w
        for b in writes:
            if b.w is not None:
                deps[b.w.idx] = b.w
            for r in b.rs:
                deps[r.idx] = r
        if dma:
            k = self.drr
            self.drr = (self.drr + 1) % len(self.dsem)
            if self.dlast[k] is not None:
                deps[self.dlast[k].idx] = self.dlast[k]
            self.dcnt[k] += 16
            op.dma = k
            op.val = self.dcnt[k]
            self.dlast[k] = op
        dl = []
        for d in deps.values():
            if d is op:
                continue
            if d.dma is None and d.eng == "pe" and eng == "pe":
                continue
            if d.dma is None:
                d.sig = True
            dl.append(d)
        op.deps = dl
        for b in reads:
            b.rs.append(op)
        for b in writes:
            b.w = op
            b.rs = []
        self.ops[eng].append(op)
        return op

    def emit(self, block):
        for e in ["pe", "act", "dve", "pool"]:
            c = 0
            for op in self.ops[e]:
                if op.sig:
                    c += 1
                    op.val = c
        fin = [(self.dsem[k], self.dcnt[k]) for k in range(len(self.dsem)) if self.dcnt[k] > 0]

        def run(e, engobj, extra=None):
            waited = {}
            for op in self.ops[e]:
                need = {}
                for d in op.deps:
                    if d.dma is not None:
                        s = self.dsem[d.dma]
                        key = ("d", d.dma)
                    else:
                        s = self.sem[d.eng]
                        key = ("e", d.eng)
                    if need.get(key, (None, 0))[1] < d.val:
                        need[key] = (s, d.val)
                for key, (s, v) in need.items():
                    if waited.get(key, 0) >= v:
                        continue
                    engobj.wait_ge(s, v)
                    waited[key] = v
                ins = op.fn(engobj)
                if op.dma is not None:
                    ins.then_inc(self.dsem[op.dma], 16)
                elif op.sig:
                    ins.then_inc(self.sem[e], 1)
            if extra:
                for s, v in extra:
                    engobj.wait_ge(s, v)

        @block.tensor
        def _(eng):
            run("pe", eng)

        @block.scalar
        def _(eng):
            run("act", eng)

        @block.vector
        def _(eng):
            run("dve", eng)

        @block.gpsimd
        def _(eng):
            run("pool", eng)

        @block.sync
        def _(eng):
            run("sp", eng, fin)


class T:
    def __init__(self, t, name):
        self.t = t
        self.b = Buf(name)

    def __getitem__(self, k):
        return self.t[k]

    @property
    def ap(self):
        return self.t[:]


class View:
    def __init__(self, ap, buf):
        self._ap = ap
        self.b = buf

    def __getitem__(self, k):
        return self._ap[k]

    @property
    def ap(self):
        return self._ap


def bs(*ts):
    return [x.b if hasattr(x, "b") else x for x in ts]


def build(TP, NS, KEEP, dbg=None):
    assert TP % SBW == 0
    NSB = TP // SBW
    nc = bass.Bass("TRN2", target_bir_lowering=False)

    def din(name, shape, dt=F32):
        return nc.dram_tensor(name, list(shape), dt, kind="ExternalInput").ap()

    def dout(name, shape, dt=F32):
        return nc.dram_tensor(name, list(shape), dt, kind="ExternalOutput").ap()

    x_d = din("x", [TP, D])
    cv_d = din("cv", [1 + NS, D])
    wperm_d = din("wperm", [D, INW])
    adaw_d = din("adaw", [D, 3 * D])
    adab_d = din("adab", [1, 3 * D])
    normw_d = din("normw", [1, D])
    fnw_d = din("fnw", [1, D])
    wout_d = din("wout", [D, D])
    relb_d = din("relb", [32, NH])
    oh_d = din("oh", [32, FLEN])
    fc_d = din("fc", [NH, FLEN])
    cst_d = din("cst", [128, 1280])
    pvec_d = din("pvec", [128, 40])
    wlora_d = din("wlora", [64, 512])
    alora_d = din("alora", [64, 512])
    lnwb_d = din("lnwb", [1, 1024])
    xs_d = din("xs", [NS, D])
    ck_d = din("ck", [NS, WIN, 512])
    cvv_d = din("cvv", [NS, WIN, 512])
    swkv_d = din("swkv", [NS * NH, HD * HD])
    ssh_d = din("ssh", [NS, SHW])
    prow_d = din("prow", [8, 512])
    muperm_d = din("muperm", [1, SHW])
    ohs_d = din("ohs", [32, 384])
    scst_d = din("scst", [8, 1024])
    cvs_d = din("cvs", [NS, D])

    y_d = dout("y", [TP, D])
    wk_d = dout("wk", [KEEP, 512])
    wv_d = dout("wv", [KEEP, 512])
    wkv_d = dout("wkv", [NH, HD, HD])
    shp_d = dout("shp", [13, 128])
    ys_d = dout("ys", [NS, D])
    ks_d = dout("ks", [NS, 512])
    vs_d = dout("vs", [NS, 512])
    wkvs_d = dout("wkvs", [NS * NH, HD * HD])
    shs_d = dout("shs", [NS, SHW])
    dbg_d = dout("dbgmix", [TP, D], BF16) if dbg == "mix" else None

    wbf_d = nc.dram_tensor("wbf", [D, INW], BF16, kind="Internal").ap()
    wobf_d = nc.dram_tensor("wobf", [D, D], BF16, kind="Internal").ap()
    fscr_d = nc.dram_tensor("fscr", [NH, FLEN], F32, kind="Internal").ap()
    qscr_d = nc.dram_tensor("qscr", [NS, 512], F32, kind="Internal").ap()
    mscr_d = nc.dram_tensor("mscr", [NS, 3 * D], F32, kind="Internal").ap()
    vscr_d = nc.dram_tensor("vscr", [NS, NH, 6, HD], F32, kind="Internal").ap()
    yscr_d = nc.dram_tensor("yscr", [NS * NH, HD], F32, kind="Internal").ap()
    fscr2_d = nc.dram_tensor("fscr2", [NH * 128, FLEN], F32, kind="Internal").ap()

    with ExitStack() as st:
        S = Sched(nc, st)
        add = S.add

        def sb(name, shape, dt):
            return T(st.enter_context(nc.sbuf_tensor("sb_" + name, list(shape), dt)), name)

        def psb(name, shape, dt):
            return T(st.enter_context(nc.psum_tensor("ps_" + name, list(shape), dt)), name)

        rr = [0]

        def anyeng():
            rr[0] += 1
            return "dve" if rr[0] % 2 else "pool"

        banks = [psb("bk%d" % i, [128, 512], F32) for i in range(8)]
        cst = sb("cst", [128, 1280], F32)
        pvec = sb("pvec", [128, 40], F32)
        identb = sb("identb", [128, 128], BF16)
        blkb = sb("blkb", [128, 128], BF16)
        zerob = sb("zerob", [128, 128], BF16)
        ind2b = sb("ind2b", [128, 2], BF16)
        xb = sb("xb", [128, TPS, D], F32)
        hb = sb("hb", [128, TPS, D], BF16)
        tmpf = sb("tmpf", [128, D], F32)
        hT = sb("hT", [128, 8, SBW], BF16)
        wst = [sb("wst%d" % i, [128, 8, 512], BF16) for i in range(2)]
        Gb = sb("Gb", [128, D], F32)
        SHb = sb("SHb", [128, D], F32)
        GATEb = sb("GATEb", [128, D], F32)
        FNWb = sb("FNWb", [128, D], F32)
        LNWb = sb("LNWb", [128, 512], F32)
        LNBb = sb("LNBb", [128, 512], F32)
        strips = sb("strips", [128, NH, STRIPW], BF16)
        NRING = 18
        kTr = sb("kTr", [128, 4, NRING * 128], BF16)
        Vr = sb("Vr", [128, NRING, NH, 66], BF16)
        qTz = sb("qTz", [128, NH, SBW], BF16)
        gat = sb("gat", [128, TPS, 512], BF16)
        grw = sb("grw", [128, TPS, 512], BF16)
        stat = sb("stat", [128, 64], F32)
        uraw = sb("uraw", [128, SBW + 8], BF16)
        carry = sb("carry", [128, 16], BF16)
        ulast = sb("ulast", [128, 16], F32)
        dtmp = sb("dtmp", [128, SBW], BF16)
        um = [sb("um%d" % i, [128, SBW], BF16) for i in range(3)]
        txw = sb("txw", [128, SBW], BF16)
        wlora = sb("wlora", [128, 512], BF16)
        alora = sb("alora", [128, 512], BF16)
        f1 = sb("f1", [128, 512 if 1 <= 3 else SBW], F32)
        f2 = sb("f2", [128, 512 if 2 <= 3 else SBW], F32)
        f3 = sb("f3", [128, 512 if 3 <= 3 else SBW], F32)
        f4 = sb("f4", [128, 512 if 4 <= 3 else SBW], F32)
        f5 = sb("f5", [128, 512 if 5 <= 3 else SBW], F32)
        f6 = sb("f6", [128, 512 if 6 <= 3 else SBW], F32)
        f7 = sb("f7", [128, 512 if 7 <= 3 else SBW], F32)
        f8 = sb("f8", [128, 512 if 8 <= 3 else SBW], F32)
        alT = sb("alT", [128, 4, SBW], BF16)
        nbT = sb("nbT", [128, 4, SBW], BF16)
        gaT = sb("gaT", [128, 4, SBW], BF16)
        rhT = sb("rhT", [128, 4, SBW], BF16)
        rkT = sb("rkT", [128, 4, SBW], BF16)
        WC = sb("WC", [128, 4, TPS], F32)
        nbK = sb("nbK", [128, TPS, 512], BF16)
        gaK = sb("gaK", [128, TPS, 512], BF16)
        vK = sb("vK", [128, TPS, 512], BF16)
        Pw = sb("Pw", [128, 1024], BF16)
        PTw = sb("PTw", [128, 1024], BF16)
        Zw = sb("Zw", [128, 1024], BF16)
        AagT = sb("AagT", [128, 1024], BF16)
        BrbT = sb("BrbT", [128, 1024], BF16)
        BrgT = sb("BrgT", [128, 1024], BF16)
        Rb = sb("Rb", [128, 512], BF16)
        Ub = sb("Ub", [128, 512], BF16)
        N0f = sb("N0f", [128, 512], F32)
        N0b = [sb("N0b%d" % i, [128, 512], BF16) for i in range(2)]
        identf = sb("identf", [128, 128], F32)
        mixed = sb("mixed", [128, TPS, D], BF16)
        mixT = sb("mixT", [128, TPS, 8, 128], BF16)
        expT = [sb("expT%d" % i, [128, 4, 128], BF16) for i in range(2)]
        attf = f3
        xo = tmpf
        yo = [sb("yo0", [128, D], F32)]
        kvo = [View(yo[0][:, 0:512], yo[0].b)]
        csil = View(mixed[0:32, 0, :], mixed.b)
        csT = sb("csT", [128, 8, 32], BF16)
        hsT = sb("hsT", [128, 8, 16], BF16)
        xxT = sb("xxT", [128, 16], BF16)
        relb = sb("relb", [32, NH], F32)
        onesf = sb("onesf", [1, 128], F32)
        modsb = View(kTr.ap.rearrange("p a b -> p (a b)").bitcast(F32)[0:32, 0:3 * D], kTr.b)
        frow = View(strips.ap.rearrange("p a b -> p (a b)").bitcast(F32)[0:8, 0:FLEN], strips.b)
        rowtmp = View(yo[0][0:1, :], yo[0].b)

        C_I = 0
        C_ML = 128
        C_MLI = 256
        C_MU = 384
        C_MUI = 512
        C_BLK = 640
        C_IND = 768
        C_SEG = 770
        C_ONE = 1026
        V_MU = 0
        V_W0 = 13
        V_A0 = 17
        V_KK = 21
        V_KA = 25
        V_OMKA = 29
        V_RK = 33

        def dma(out, in_, reads=(), writes=()):
            return add("sp", lambda e: e.dma_start(out=out, in_=in_), reads=bs(*reads), writes=bs(*writes), dma=True)

        def mm(out, lhsT, rhs, start, stop, reads, writes):
            return add("pe", lambda e: e.matmul(out, lhsT=lhsT, rhs=rhs, start=start, stop=stop, skip_group_check=True),
                       reads=bs(*reads), writes=bs(*writes))

        def tr(out, in_, reads, writes, idn=None):
            idt = identb
            return add("pe", lambda e: e.transpose(out, in_, idt[0:in_.shape[0], 0:in_.shape[0]]),
                       reads=bs(idt, *reads), writes=bs(*writes))

        def act(out, in_, func, reads, writes, scale=1.0, bias=0.0, accum=None):
            kw = {}
            if accum is not None:
                kw["accum_out"] = accum
            return add("act", lambda e: e.activation(out=out, in_=in_, func=func, scale=scale, bias=bias, **kw),
                       reads=bs(*reads), writes=bs(*writes))

        def tt(eng, out, in0, in1, op, reads, writes):
            return add(eng, lambda e: e.tensor_tensor(out=out, in0=in0, in1=in1, op=op), reads=bs(*reads), writes=bs(*writes))

        def ts(eng, out, in0, s1, s2, op0, op1, reads, writes):
            return add(eng, lambda e: e.tensor_scalar(out=out, in0=in0, scalar1=s1, scalar2=s2, op0=op0, op1=op1),
                       reads=bs(*reads), writes=bs(*writes))

        def stt(out, in0, scalar, in1, op0, op1, reads, writes):
            return add("dve", lambda e: e.scalar_tensor_tensor(out=out, in0=in0, scalar=scalar, in1=in1, op0=op0, op1=op1),
                       reads=bs(*reads), writes=bs(*writes))

        def cp(eng, out, in_, reads, writes):
            if eng == "act":
                return act(out, in_, AF.Copy, reads, writes)
            return add(eng, lambda e: e.tensor_copy(out=out, in_=in_), reads=bs(*reads), writes=bs(*writes))

        def bview(bank, dt=BF16):
            return bank.t[:].bitcast(dt)

        dma(cst.ap, cst_d, writes=[cst])
        dma(pvec.ap, pvec_d, writes=[pvec])
        dma(relb.ap, relb_d, writes=[relb])
        cp("dve", identb.ap, cst[:, C_I:C_I + 128], [cst], [identb])
        cp("dve", identf.ap, cst[:, C_I:C_I + 128], [cst], [identf])
        cp("pool", blkb.ap, cst[:, C_BLK:C_BLK + 128], [cst], [blkb])
        cp("pool", ind2b.ap, cst[:, C_IND:C_IND + 2], [cst], [ind2b])
        cp("pool", onesf.ap, cst[0:1, C_ONE:C_ONE + 128], [cst], [onesf])
        add("pool", lambda e: e.memset(carry.ap, 0.0), writes=bs(carry))
        add("pool", lambda e: e.memset(zerob.ap, 0.0), writes=bs(zerob))
        ts("dve", pvec[:, V_OMKA:V_OMKA + 4], pvec[:, V_KA:V_KA + 4], -1.0, 1.0, ALU.mult, ALU.add, [pvec], [pvec])
        add("pool", lambda e: e.memset(N0f.ap, 0.0), writes=bs(N0f))
        add("pool", lambda e: e.memset(N0b[0].ap, 0.0), writes=bs(N0b[0]))
        add("pool", lambda e: e.memset(qTz.ap, 0.0), writes=bs(qTz))
        dma(f1[0:64, :], wlora_d, writes=[f1])
        dma(f1[64:128, :], alora_d, writes=[f1])
        cp("dve", wlora[0:64, :], f1[0:64, :], [f1], [wlora])
        cp("dve", alora[64:128, :], f1[64:128, :], [f1], [alora])

        xbf = xb.ap.rearrange("p a b -> p (a b)")
        hbf = hb.ap.rearrange("p a b -> p (a b)")
        NSL = TPS
        xslot = [Buf("xs%d" % i) for i in range(NSL)]
        hslot = [Buf("hs%d" % i) for i in range(NSL)]
        pieces = []
        wstores = []
        for kc in range(8):
            for c0 in range(0, INW, 1024):
                pieces.append((kc, c0, min(1024, INW - c0)))
        for i, (kc, c0, w) in enumerate(pieces):
            sl = i % NSL
            dma(xbf[:, sl * 1024:sl * 1024 + w], wperm_d[kc * 128:(kc + 1) * 128, c0:c0 + w], writes=[xslot[sl]])
            e = ["act", "dve", "pool"][i % 3]
            cp(e, hbf[:, sl * 1024:sl * 1024 + w], xbf[:, sl * 1024:sl * 1024 + w], [xslot[sl]], [hslot[sl]])
            wstores.append(dma(wbf_d[kc * 128:(kc + 1) * 128, c0:c0 + w], hbf[:, sl * 1024:sl * 1024 + w], reads=[hslot[sl]], writes=[]))
        for kc in range(8):
            sl = kc % NSL
            dma(xbf[:, sl * 1024:(sl + 1) * 1024], wout_d[kc * 128:(kc + 1) * 128, :], writes=[xslot[sl]])
            e = ["act", "dve", "pool"][kc % 3]
            cp(e, hbf[:, sl * 1024:(sl + 1) * 1024], xbf[:, sl * 1024:(sl + 1) * 1024], [xslot[sl]], [hslot[sl]])
            wstores.append(dma(wobf_d[kc * 128:(kc + 1) * 128, :], hbf[:, sl * 1024:(sl + 1) * 1024], reads=[hslot[sl]], writes=[]))

        def compute_mod(cv_ap, NR):
            dma(f2[0:NR, :], cv_ap[:, 0:512], writes=[f2])
            dma(f3[0:NR, :], cv_ap[:, 512:1024], writes=[f3])
            act(csil[0:NR, 0:512], f2[0:NR, :], AF.Silu, [f2], [csil])
            act(csil[0:NR, 512:1024], f3[0:NR, :], AF.Silu, [f3], [csil])
            pb = banks[0]
            for c in range(8):
                tr(bview(pb)[:, c * 32:c * 32 + NR], csil[0:NR, c * 128:(c + 1) * 128], [csil], [pb])
            cp("dve", csT.ap.rearrange("p a b -> p (a b)")[:, 0:256], bview(pb)[:, 0:256], [pb], [csT])
            for g in range(6):
                for kc in range(8):
                    sl = kc % NSL
                    dma(xbf[:, sl * 1024:sl * 1024 + 512], adaw_d[kc * 128:(kc + 1) * 128, g * 512:(g + 1) * 512], writes=[xslot[sl]])
                    e = ["act", "dve", "pool"][kc % 3]
                    cp(e, hbf[:, sl * 1024:sl * 1024 + 512], xbf[:, sl * 1024:sl * 1024 + 512], [xslot[sl]], [hslot[sl]])
                    mm(banks[1][0:NR, :], csT[:, kc, 0:NR], hbf[:, sl * 1024:sl * 1024 + 512], kc == 0, kc == 7,
                       [csT, hslot[sl]], [banks[1]])
                cp("dve", modsb[0:NR, g * 512:(g + 1) * 512], banks[1][0:NR, :], [banks[1]], [modsb])
            for g in range(6):
                dma(rowtmp[0:1, 0:512], adab_d[0:1, g * 512:(g + 1) * 512], writes=[rowtmp])
                mm(banks[0][0:NR, :], onesf[0:1, 0:NR], rowtmp[0:1, 0:512], True, True, [onesf, rowtmp], [banks[0]])
                tt("dve", modsb[0:NR, g * 512:(g + 1) * 512], banks[0][0:NR, :], modsb[0:NR, g * 512:(g + 1) * 512], ALU.add,
                   [banks[0], modsb], [modsb])
        compute_mod(cv_d, 1 + NS)
        mscr_w = Buf("mscr")
        dma(mscr_d, modsb[1:1 + NS, :], reads=[modsb], writes=[mscr_w])
        def bcast_row(dst, src_row_ap, srcbufs, bank):
            for hlf in range(2):
                mm(bank[:, :], onesf[0:1, :], src_row_ap[:, hlf * 512:(hlf + 1) * 512], True, True, [onesf] + srcbufs, [bank])
                cp("dve", dst[:, hlf * 512:(hlf + 1) * 512], bank[:, :], [bank], [dst])
        bcast_row(SHb, modsb[0:1, 0:D], [modsb], banks[0])
        bcast_row(Gb, modsb[0:1, D:2 * D], [modsb], banks[1])
        bcast_row(GATEb, modsb[0:1, 2 * D:3 * D], [modsb], banks[0])
        dma(rowtmp.ap, normw_d, writes=[rowtmp])
        bcast_row(tmpf, rowtmp.ap, [rowtmp], banks[1])
        stt(Gb.ap, Gb.ap, 1.0, tmpf.ap, ALU.add, ALU.mult, [Gb, tmpf], [Gb])
        dma(rowtmp.ap, fnw_d, writes=[rowtmp])
        bcast_row(FNWb, rowtmp.ap, [rowtmp], banks[0])
        dma(rowtmp.ap, lnwb_d, writes=[rowtmp])
        mm(banks[1][:, :], onesf[0:1, :], rowtmp[0:1, 0:512], True, True, [onesf, rowtmp], [banks[1]])
        cp("dve", LNWb.ap, banks[1][:, :], [banks[1]], [LNWb])
        mm(banks[0][:, :], onesf[0:1, :], rowtmp[0:1, 512:1024], True, True, [onesf, rowtmp], [banks[0]])
        cp("dve", LNBb.ap, banks[0][:, :], [banks[0]], [LNBb])

        if dbg == 'nostrip':
            pass
        for pc in range(0, FLEN, 512):
            w = min(512, FLEN - pc)
            dma(f1[0:32, 0:w], oh_d[:, pc:pc + w], writes=[f1])
            dma(f2[0:8, 0:w], fc_d[:, pc:pc + w], writes=[f2])
            mm(banks[1][0:8, 0:w], relb[0:32, :], f1[0:32, 0:w], True, True, [relb, f1], [banks[1]])
            tt("dve", frow[0:8, pc:pc + w], banks[1][0:8, 0:w], f2[0:8, 0:w], ALU.add, [banks[1], f2], [frow])
        fs_w = Buf("fs_w")
        dma(fscr_d, frow.ap, reads=[frow], writes=[fs_w])
        for h in range(NH):
            dma(fscr2_d[h * 128:(h + 1) * 128, :], AP(fscr_d.tensor, h * FLEN, [[0, 128], [1, FLEN]]), reads=[fs_w], writes=[fs_w])
        pi = 0
        for h in range(NH):
            for q4 in range(4):
                src = AP(fscr2_d.tensor, h * 128 * FLEN + 127 + q4 * 544, [[FLEN - 1, 128], [1, 544]])
                sl = pi % NSL
                pi += 1
                stg = xbf[:, sl * 1024:sl * 1024 + 544]
                dma(stg, src, reads=[fs_w], writes=[xslot[sl]])
                cp(["act", "dve"][pi % 2], strips[:, h, q4 * 544:(q4 + 1) * 544], stg, [xslot[sl]], [strips])
        add("dve", lambda e: e.memset(stat[:, 60:61], 0.0), reads=xslot + hslot, writes=bs(xb, hb, stat))

        add("pool", lambda e: e.memset(kTr.ap, 0.0), writes=bs(kTr))
        add("pool", lambda e: e.memset(Vr.ap, 0.0), writes=bs(Vr))
        add("pool", lambda e: e.memset(Vr[:, :, :, 64:65], 1.0), reads=bs(Vr), writes=bs(Vr))
        bank_rr = [0]

        def nb(lst):
            bank_rr[0] += 1
            return lst[bank_rr[0] % len(lst)]

        GRP = [("q", 0, 512), ("k", 512, 512), ("gf", 2560, 128)] + [("gr%d" % c, 2688 + 384 * c, 384) for c in range(4)] + \
              [("v", 1024, 512), ("ga", 1536, 512), ("gw", 2048, 512)]
        wsti = [0]
        PB = [banks[0], banks[1]]

        def load_w(c0, w):
            wsti[0] += 1
            wt = wst[wsti[0] % 2]
            op = dma(wt[:, :, 0:w], wbf_d[:, c0:c0 + w].rearrange("(kc p) c -> p kc c", p=128), reads=[], writes=[wt])
            op.deps.extend(wstores)
            return wt

        def proj_fm(wt, cc):
            bk = nb(PB)
            for kc in range(8):
                mm(bk[:, 0:SBW], wt[:, kc, cc * 128:(cc + 1) * 128], hT[:, kc, :], kc == 0, kc == 7, [wt, hT], [bk])
            return bk

        n0i = [0]
        for sbi in range(NSB if dbg != 'setup' else 0):
            t0 = sbi * SBW
            last = sbi == NSB - 1
            dma(xb.ap, x_d[t0:t0 + SBW, :].rearrange("(j p) d -> p j d", p=128), writes=[xb])
            for j in range(TPS):
                act(tmpf.ap, xb[:, j, :], AF.Square, [xb], [tmpf, stat], accum=stat[:, j:j + 1])
            ts("dve", stat[:, 4:8], stat[:, 0:4], 1.0 / D, NORM_EPS, ALU.mult, ALU.add, [stat], [stat])
            act(stat[:, 4:8], stat[:, 4:8], AF.Sqrt, [stat], [stat])
            add("dve", lambda e: e.reciprocal(out=stat[:, 8:12], in_=stat[:, 4:8]), reads=bs(stat), writes=bs(stat))
            for j in range(TPS):
                stt(tmpf.ap, xb[:, j, :], stat[:, 8 + j:9 + j], Gb.ap, ALU.mult, ALU.mult, [xb, stat, Gb], [tmpf])
                tt("pool", hb[:, j, :], tmpf.ap, SHb.ap, ALU.add, [tmpf, SHb], [hb])
                bk = nb(PB)
                for c in range(8):
                    tr(bview(bk)[:, c * 128:(c + 1) * 128], hb[:, j, c * 128:(c + 1) * 128], [hb], [bk])
                cp("act", hT[:, :, j * 128:(j + 1) * 128], bview(bk).rearrange("p (c t) -> p c t", c=8), [bk], [hT])
            slot = (sbi * TPS) % NRING
            def proj_gen(GL):
                for (gname, c0, w) in GL:
                    if dbg == 'norm':
                        break
                    if dbg == 'pqk' and gname not in ('q', 'k'):
                        continue
                    if dbg == 'pgf' and gname not in ('q', 'k', 'gf'):
                        continue
                    if dbg == 'ptm' and gname not in ('q', 'k', 'v', 'ga', 'gw'):
                        continue
                    if dbg == 'ptmv' and gname not in ('q', 'k', 'v'):
                        continue
                    if dbg == 'ptmg' and gname not in ('q', 'k', 'ga'):
                        continue
                    wt = load_w(c0, w)
                    if gname == "q":
                        for cc in range(4):
                            bk = proj_fm(wt, cc)
                            act(qTz[0:64, 2 * cc, :], bk[0:64, 0:SBW], AF.Copy, [bk], [qTz], scale=0.125)
                            act(qTz[64:128, 2 * cc + 1, :], bk[64:128, 0:SBW], AF.Copy, [bk], [qTz], scale=0.125)
                    elif gname == "k":
                        for cc in range(4):
                            bk = proj_fm(wt, cc)
                            cp("act", kTr[:, cc, slot * 128:slot * 128 + SBW], bk[:, 0:SBW], [bk], [kTr])
                        if t0 + SBW > TP - KEEP:
                            for j in range(TPS):
                                bk = nb(PB)
                                for kc in range(8):
                                    mm(bk[:, :], hT[:, kc, j * 128:(j + 1) * 128], wt[:, kc, :], kc == 0, kc == 7, [wt, hT], [bk])
                                ko = nb(kvo)
                                cp("dve", ko.ap, bk[:, :], [bk], [ko])
                                r0 = t0 + j * 128 - (TP - KEEP)
                                dma(wk_d[r0:r0 + 128, :], ko.ap, reads=[ko])
                    elif gname in ("v", "ga", "gw"):
                        for j in range(TPS):
                            bk = nb(PB)
                            for kc in range(8):
                                mm(bk[:, :], hT[:, kc, j * 128:(j + 1) * 128], wt[:, kc, :], kc == 0, kc == 7, [wt, hT], [bk])
                            if gname == "v":
                                rs_ = (sbi * TPS + j) % NRING
                                cp("dve", Vr[:, rs_, :, 0:64], bk[:, :].rearrange("p (h d) -> p h d", h=NH), [bk], [Vr])
                                if t0 + SBW > TP - KEEP:
                                    ko = nb(kvo)
                                    cp("dve", ko.ap, bk[:, :], [bk], [ko])
                                    r0 = t0 + j * 128 - (TP - KEEP)
                                    dma(wv_d[r0:r0 + 128, :], ko.ap, reads=[ko])
                            elif gname == "ga":
                                act(gat[:, j, :], bk[:, :], AF.Silu, [bk], [gat])
                            else:
                                act(grw[:, j, :], bk[:, :], AF.Silu, [bk], [grw])
                    elif gname == "gf":
                        bk = proj_fm(wt, 0)
                        _tokshift(add, act, tt, stt, cp, bs, bk, 0, um[0], uraw, carry, ulast, dtmp, pvec, V_MU, last)
                        act(txw[0:64, :], um[0][0:64, :], AF.Tanh, [um[0]], [txw])
                        cp("pool", txw[64:128, :], um[0][64:128, :], [um[0]], [txw])
                    else:
                        c = int(gname[2])
                        for q3 in range(3):
                            bk = proj_fm(wt, q3)
                            _tokshift(add, act, tt, stt, cp, bs, bk, 1 + 3 * c + q3, um[q3], uraw, carry, ulast, dtmp, pvec, V_MU, last)
                            yield
                        umr, umk, umv = um
                        W_ = slice(0, SBW)
                        g1, g2, g3 = f1[:, W_], f2[:, W_], f3[:, W_]
                        bw = banks[6]
                        ba = banks[7]
                        mm(bw[:, W_], wlora[0:64, c * 128:(c + 1) * 128], txw[0:64, :], True, True, [wlora, txw], [bw])
                        mm(ba[:, W_], alora[64:128, c * 128:(c + 1) * 128], txw[64:128, :], True, True, [alora, txw], [ba])
                        act(g1, bw[:, W_], AF.Sigmoid, [bw, pvec], [f1], bias=pvec[:, V_W0 + c:V_W0 + c + 1])
                        act(g2, ba[:, W_], AF.Sigmoid, [ba, pvec], [f2], bias=pvec[:, V_A0 + c:V_A0 + c + 1])
                        add("dve", lambda e, g1=g1, g3=g3: e.tensor_tensor_scan(out=g3, data0=cst[:, C_SEG:C_SEG + SBW], data1=g1, initial=0.0,
                                                                   op0=ALU.mult, op1=ALU.add), reads=bs(cst, f1), writes=bs(f3))
                        tt("pool", f4.ap, g3, g1, ALU.subtract, [f3, f1], [f4])
                        act(f5.ap, g3, AF.Exp, [f3], [f5], scale=-C0)
                        act(f6.ap, g3, AF.Exp, [f3], [f6], scale=C0)
                        act(f4.ap, f4.ap, AF.Exp, [f4], [f4], scale=-C0)
                        act(WC[:, c, 0:TPS], AP(f3.ap.tensor, 127, [[512, 128], [128, TPS]]), AF.Exp, [f3], [WC], scale=-C0)
                        yield
                        tt("pool", rhT[:, c, :], umr.ap, f5.ap, ALU.mult, [umr, f5], [rhT])
                        act(dtmp.ap, umk.ap, AF.Square, [umk, pvec], [dtmp], scale=pvec[:, V_KK + c:V_KK + c + 1])
                        bn_ = banks[6]
                        mm(bn_[:, W_], blkb.ap, dtmp.ap, True, True, [blkb, dtmp], [bn_])
                        ts("dve", f7.ap, bn_[:, W_], 1e-24, None, ALU.add, ALU.bypass, [bn_], [f7])
                        act(f7.ap, f7.ap, AF.Sqrt, [f7], [f7])
                        add("dve", lambda e: e.reciprocal(out=f7.ap, in_=f7.ap), reads=bs(f7), writes=bs(f7))
                        stt(f8.ap, umk.ap, pvec[:, V_KK + c:V_KK + c + 1], f7.ap, ALU.mult, ALU.mult, [umk, pvec, f7], [f8])
                        yield
                        tt("pool", alT[:, c, :], f8.ap, f4.ap, ALU.mult, [f8, f4], [alT])
                        tt("dve", f8.ap, f8.ap, g2, ALU.mult, [f8, f2], [f8])
                        stt(nbT[:, c, :], f8.ap, -1.0, f6.ap, ALU.mult, ALU.mult, [f8, f6], [nbT])
                        yield
                        ts("pool", f7.ap, g2, pvec[:, V_KA + c:V_KA + c + 1], pvec[:, V_OMKA + c:V_OMKA + c + 1], ALU.mult, ALU.add,
                           [f2, pvec], [f7])
                        tt("dve", f7.ap, f7.ap, umk.ap, ALU.mult, [f7, umk], [f7])
                        tt("pool", gaT[:, c, :], f7.ap, f6.ap, ALU.mult, [f7, f6], [gaT])
                        stt(rkT[:, c, :], umr.ap, pvec[:, V_RK + c:V_RK + c + 1], f7.ap, ALU.mult, ALU.mult, [umr, pvec, f7], [rkT])
                        yield
                        for (srcap, srcb, dstK, ee) in ((nbT[:, c, :], nbT, nbK, "dve"), (gaT[:, c, :], gaT, gaK, "dve"), (umv.ap, umv, vK, "dve")):
                            bk = nb(PB)
                            for j in range(TPS):
                                tr(bview(bk)[:, j * 128:(j + 1) * 128], srcap[:, j * 128:(j + 1) * 128], [srcb], [bk])
                            cp(ee, dstK[:, :, c * 128:(c + 1) * 128], bview(bk)[:, 0:TPS * 128].rearrange("p (j t) -> p j t", j=TPS), [bk], [dstK])
                    yield
            def rwkv_gen(j, ti):
                    tc = slice(j * 128, (j + 1) * 128)
                    mask4 = lambda c0_: AP(cst.ap.tensor, c0_, [[1280, 128], [0, 4], [1, 128]])
                    par = lambda t_, hh_: AP(t_.ap.tensor, hh_ * 128, [[1024, 128], [256, 4], [1, 128]])
                    hb128 = lambda h_: slice(h_ * 128, (h_ + 1) * 128)
                    hs64 = lambda h_: slice(h_ * 64, (h_ + 1) * 64)
                    specs = [(gaT, alT, C_MU, AagT), (gaT, rhT, C_MUI, BrgT), (nbT, rhT, C_MUI, BrbT), (nbT, alT, C_MU, PTw), (alT, nbT, C_ML, Pw)]
                    RP = [(banks[6], banks[7]), (banks[0], banks[1])]
                    for (La, Ra, mk, dst) in specs:
                        bp = nb(RP)
                        for hh in range(2):
                            for c in range(4):
                                mm(bp[hh][:, c * 128:(c + 1) * 128], La[64 * hh:64 * hh + 64, c, tc], Ra[64 * hh:64 * hh + 64, c, tc], True, True,
                                   [La, Ra], [bp[hh]])
                        yield
                        for hh in range(2):
                            tt("dve", par(dst, hh), bp[hh][:, :].rearrange("p (c s) -> p c s", c=4), mask4(mk), ALU.mult, [bp[hh], cst], [dst])
                    tt("pool", Zw.ap.rearrange("p (h s) -> p h s", h=8), PTw.ap.rearrange("p (h s) -> p h s", h=8),
                       AP(cst.ap.tensor, C_I, [[1280, 128], [0, 8], [1, 128]]), ALU.add, [PTw, cst], [Zw])
                    for lv in range(NSQ):
                        bP = nb(RP)
                        for h in range(8):
                            mm(bP[h // 4][:, hb128(h % 4)], PTw[:, hb128(h)], Pw[:, hb128(h)], True, True, [PTw, Pw], [bP[h // 4]])
                        if lv < NSQ - 1:
                            bT = nb(RP)
                            for h in range(8):
                                mm(bT[h // 4][:, hb128(h % 4)], Pw[:, hb128(h)], PTw[:, hb128(h)], True, True, [PTw, Pw], [bT[h // 4]])
                        yield
                        for q2 in range(2):
                            cp("act", Pw[:, q2 * 512:(q2 + 1) * 512], bP[q2][:, :], [bP[q2]], [Pw])
                        if lv < NSQ - 1:
                            for q2 in range(2):
                                cp("dve", PTw[:, q2 * 512:(q2 + 1) * 512], bT[q2][:, :], [bT[q2]], [PTw])
                        bZ = nb(RP)
                        for h in range(8):
                            mm(bZ[h // 4][:, hb128(h % 4)], Pw[:, hb128(h)], Zw[:, hb128(h)], True, True, [Pw, Zw], [bZ[h // 4]])
                        for q2 in range(2):
                            tt("dve", Zw[:, q2 * 512:(q2 + 1) * 512], bZ[q2][:, :], Zw[:, q2 * 512:(q2 + 1) * 512], ALU.add, [bZ[q2], Zw], [Zw])
                        yield
                    No, Nn = N0b[n0i[0] % 2], N0b[(n0i[0] + 1) % 2]
                    n0i[0] += 1
                    bR = banks[6]
                    mm(bR[:, :], zerob.ap, identb.ap.to_broadcast([128, 128]) if False else strips[:, 0, 0:512], True, False, [zerob, strips], [bR])
                    for c in range(4):
                        mm(bR[:, hb128(c)], alT[:, c, tc], No[:, hb128(c)], False, False, [alT, No], [bR])
                    for h in range(8):
                        mm(bR[:, hs64(h)], AagT[:, hb128(h)], vK[:, j, hs64(h)], False, True, [AagT, vK], [bR])
                    cp("act", Rb.ap, bR[:, :], [bR], [Rb])
                    yield
                    bU = banks[7]
                    for h in range(8):
                        mm(bU[:, hs64(h)], Zw[:, hb128(h)], Rb[:, hs64(h)], True, True, [Zw, Rb], [bU])
                    cp("act", Ub.ap, bU[:, :], [bU], [Ub])
                    yield
                    bY = banks[1]
                    mm(bY[:, :], zerob.ap, strips[:, 0, 0:512], True, False, [zerob, strips], [bY])
                    for c in range(4):
                        mm(bY[:, hb128(c)], rhT[:, c, tc], No[:, hb128(c)], False, False, [rhT, No], [bY])
                    for h in range(8):
                        mm(bY[:, hs64(h)], BrbT[:, hb128(h)], Ub[:, hs64(h)], False, False, [BrbT, Ub], [bY])
                        mm(bY[:, hs64(h)], BrgT[:, hb128(h)], vK[:, j, hs64(h)], False, True, [BrgT, vK], [bY])
                    bN = banks[6]
                    mm(bN[:, :], identf.ap, N0f.ap, True, False, [identf, N0f], [bN])
                    for c in range(4):
                        mm(bN[:, hb128(c)], nbK[:, j, hb128(c)], Ub[:, hb128(c)], False, False, [nbK, Ub], [bN])
                        mm(bN[:, hb128(c)], gaK[:, j, hb128(c)], vK[:, j, hb128(c)], False, True, [gaK, vK], [bN])
                    tt("dve", f1.ap.rearrange("p (c q) -> p c q", c=4), bN[:, :].rearrange("p (c q) -> p c q", c=4),
                       AP(WC.ap.tensor, j, [[4 * TPS, 128], [TPS, 4], [0, 128]]), ALU.mult, [bN, WC], [f1])
                    tt("pool", N0f.ap.rearrange("p (c q) -> p c q", c=4), f1.ap.rearrange("p (c q) -> p c q", c=4), mask4(C_BLK), ALU.mult,
                       [f1, cst], [N0f])
                    cp("act", Nn.ap, N0f.ap, [N0f], [Nn])
                    yield
                    Y3 = bY[:, :].rearrange("p (h i) -> p h i", h=8)
                    add("dve", lambda e: e.tensor_reduce(out=stat[:, 16:24], in_=Y3, axis=AX.X, op=ALU.add), reads=bs(bY), writes=bs(stat))
                    act(f1.ap, bY[:, :], AF.Square, [bY], [f1])
                    add("dve", lambda e: e.tensor_reduce(out=stat[:, 24:32], in_=f1.ap.rearrange("p (h i) -> p h i", h=8), axis=AX.X, op=ALU.add),
                        reads=bs(f1), writes=bs(stat))
                    ts("dve", stat[:, 16:24], stat[:, 16:24], 1.0 / HD, None, ALU.mult, ALU.bypass, [stat], [stat])
                    tt("dve", stat[:, 32:40], stat[:, 16:24], stat[:, 16:24], ALU.mult, [stat], [stat])
                    stt(stat[:, 24:32], stat[:, 24:32], 1.0 / HD, stat[:, 32:40], ALU.mult, ALU.subtract, [stat], [stat])
                    ts("dve", stat[:, 24:32], stat[:, 24:32], GN_EPS, None, ALU.add, ALU.bypass, [stat], [stat])
                    act(stat[:, 24:32], stat[:, 24:32], AF.Sqrt, [stat], [stat])
                    add("dve", lambda e: e.reciprocal(out=stat[:, 24:32], in_=stat[:, 24:32]), reads=bs(stat), writes=bs(stat))
                    b8 = lambda lo: stat[:, lo:lo + 8].unsqueeze(2).to_broadcast([128, 8, 64])
                    f2_3 = f2.ap.rearrange("p (h i) -> p h i", h=8)
                    tt("dve", f2_3, Y3, b8(16), ALU.subtract, [bY, stat], [f2])
                    tt("pool", f2_3, f2_3, b8(24), ALU.mult, [f2, stat], [f2])
                    tt("dve", f2.ap, f2.ap, LNWb.ap, ALU.mult, [f2, LNWb], [f2])
                    tt("pool", f2.ap, f2.ap, LNBb.ap, ALU.add, [f2, LNBb], [f2])
                    bB = banks[7]
                    for c in range(4):
                        mm(bB[:, c * 2:c * 2 + 2], rkT[:, c, j * 128:(j + 1) * 128], ind2b.ap, True, True, [rkT, ind2b], [bB])
                    cp("act", stat[:, 40:48], bB[:, 0:8], [bB], [stat])
                    f3_3 = f3.ap.rearrange("p (h i) -> p h i", h=8)
                    tt("dve", f3_3, vK[:, j, :].rearrange("p (h i) -> p h i", h=8), b8(40), ALU.mult, [vK, stat], [f3])
                    tt("pool", f2.ap, f2.ap, f3.ap, ALU.add, [f2, f3], [f2])
                    tt("dve", mixed[:, j, 512:1024], f2.ap, grw[:, j, :], ALU.mult, [f2, grw], [mixed])
                    yield
            def attn_gen(j, ti):
                    bN0, bN1 = banks[4], banks[5]
                    SB_ = [banks[2], banks[3]]
                    kts = [kt for kt in range(ti - 16, ti + 1) if kt >= 0]
                    mm(bN0[:, 0:260], zerob.ap, strips[:, 0, 0:260], True, False, [zerob, strips], [bN0])
                    mm(bN1[:, 0:260], zerob.ap, strips[:, 0, 0:260], True, False, [zerob, strips], [bN1])
                    units = [(ki, kt, half) for ki, kt in enumerate(kts) for half in range(2)]

                    def emit_pv(u_, ex_):
                        ki_, kt_, half_ = u_
                        bNh_ = bN0 if half_ == 0 else bN1
                        for hq in range(4):
                            h = half_ * 4 + hq
                            mm(bNh_[:, hq * 65:(hq + 1) * 65], ex_[:, hq, :], Vr[:, kt_ % NRING, h, 0:65], False, ki_ == len(kts) - 1, [ex_, Vr], [bNh_])
                    prev = None
                    for idx, (ki, kt, half) in enumerate(units):
                        o = ti - kt
                        rs_ = kt % NRING
                        ksl = slice(rs_ * 128, (rs_ + 1) * 128)
                        bk = SB_[idx % 2]
                        ex = expT[idx % 2]
                        mm(bk[:, :].rearrange("p (h q) -> p h q", h=4), identb.ap, strips[:, half * 4:half * 4 + 4, o * 128:(o + 1) * 128],
                           True, False, [identb, strips], [bk])
                        for cc in range(2):
                            c = half * 2 + cc
                            mm(bk[:, cc * 256:(cc + 1) * 256].rearrange("p (h q) -> p h q", h=2), kTr[:, c, ksl],
                               qTz[:, 2 * c:2 * c + 2, j * 128:(j + 1) * 128], False, cc == 1, [kTr, qTz], [bk])
                        act(ex[:, 0:4, :], bk[:, :].rearrange("p (h q) -> p h q", h=4), AF.Exp, [bk], [ex])
                        if prev is not None:
                            emit_pv(*prev)
                            yield
                        prev = ((ki, kt, half), ex)
                    emit_pv(*prev)
                    yield
                    for half in range(2):
                        bNh = bN0 if half == 0 else bN1
                        N3 = bNh[:, 0:260].rearrange("p (h e) -> p h e", h=4)
                        add("dve", lambda e, N3=N3, half=half: e.reciprocal(out=stat[:, 48 + half * 4:52 + half * 4].unsqueeze(2), in_=N3[:, :, 64:65]),
                            reads=bs(bNh), writes=bs(stat))
                        tt("dve", mixed[:, j, half * 256:(half + 1) * 256].rearrange("p (h d) -> p h d", h=4), N3[:, :, 0:64],
                           stat[:, 48 + half * 4:52 + half * 4].unsqueeze(2).to_broadcast([128, 4, 64]), ALU.mult, [bNh, stat], [mixed])
                    tt("pool", mixed[:, j, 0:512], mixed[:, j, 0:512], gat[:, j, :], ALU.mult, [mixed, gat], [mixed])
                    yield
            def interleave(ga_, gb_, na_=1, nb_=1):
                a1 = a2 = True
                while a1 or a2:
                    for _ in range(na_):
                        if a1:
                            a1 = next(ga_, "end") != "end"
                    for _ in range(nb_):
                        if a2:
                            a2 = next(gb_, "end") != "end"

            def mix_transposes(j):
                bk = nb(PB)
                for c in range(8):
                    tr(bview(bk)[:, c * 128:(c + 1) * 128], mixed[:, j, c * 128:(c + 1) * 128], [mixed], [bk])
                cp("act", mixT[:, j, :, :].rearrange("p c t -> p (c t)"), bview(bk)[:, :], [bk], [mixT])
            assert TPS == 2
            ti0 = sbi * TPS
            for _ in proj_gen(GRP[0:2] + GRP[7:10]):
                pass
            interleave(proj_gen(GRP[2:7]), attn_gen(0, ti0), 2, 1)
            interleave(rwkv_gen(0, ti0), attn_gen(1, ti0 + 1), 2, 3)
            mix_transposes(0)
            for _ in rwkv_gen(1, ti0 + 1):
                pass
            mix_transposes(1)
            if dbg in ('proj', 'chain', 'sweep', 'ypost', 'attn', 'norm', 'pqk', 'pgf', 'ptm', 'ptmv', 'ptmg', 'mats', 'mats1', 'mats2'):
                continue
            wo = []
            for hf in range(2):
                wt = wst[hf]
                op = dma(wt.ap, wobf_d[:, hf * 512:(hf + 1) * 512].rearrange("(kc p) c -> p kc c", p=128), reads=[], writes=[wt])
                op.deps.extend(wstores)
                wo.append(wt)
            for j in range(TPS):
                bO = [banks[6], banks[7]]
                for hf in range(2):
                    for fc in range(8):
                        mm(bO[hf][:, :], mixT[:, j, fc, :], wo[hf][:, fc, :], fc == 0, fc == 7, [mixT, wo[hf]], [bO[hf]])
                for hf in range(2):
                    hs = slice(hf * 512, (hf + 1) * 512)
                    tt("dve", xo[:, hs], bO[hf][:, :], GATEb[:, hs], ALU.mult, [bO[hf], GATEb], [xo])
                tt("pool", xo.ap, xo.ap, xb[:, j, :], ALU.add, [xo, xb], [xo])
                act(hb[:, j, :], xo.ap, AF.Square, [xo], [hb, stat], accum=stat[:, 56:57])
                ts("dve", stat[:, 57:58], stat[:, 56:57], 1.0 / D, NORM_EPS, ALU.mult, ALU.add, [stat], [stat])
                act(stat[:, 57:58], stat[:, 57:58], AF.Sqrt, [stat], [stat])
                add("dve", lambda e: e.reciprocal(out=stat[:, 58:59], in_=stat[:, 57:58]), reads=bs(stat), writes=bs(stat))
                yy = yo[0]
                stt(yy.ap, xo.ap, stat[:, 58:59], FNWb.ap, ALU.mult, ALU.mult, [xo, stat, FNWb], [yy])
                dma(y_d[t0 + j * 128:t0 + (j + 1) * 128, :], yy.ap, reads=[yy])
        if dbg != 'nosample':
            P = slice(0, NS)
            add("dve", lambda e: e.memset(stat[:, 61:62], 0.0), reads=[], writes=xslot + hslot + bs(xb, hb, stat))
            dma(modsb[0:NS, :], mscr_d, reads=[mscr_w], writes=[modsb])
            ar = strips.ap.rearrange("p a b -> p (a b)").bitcast(F32)
            Sst = View(ar[:, 0:4096], strips.b)
            TMP = View(ar[:, 4096:8192], strips.b)
            vec6 = View(ar[:, 8192:8576], strips.b)
            vrf = Vr.ap.rearrange("p a b c -> p (a b c)")
            zs = View(AP(vrf.tensor, 0, [[NRING * NH * 66, 128], [1, 2 * 4224]]).bitcast(F32)[0:NS, :], Vr.b)
            xs = View(yo[0][0:NS, :], yo[0].b)
            hs = View(mixed[0:NS, 0, :], mixed.b)
            dma(xs.ap, xs_d, writes=[xs])
            act(tmpf[P, :], xs.ap, AF.Square, [xs], [tmpf, stat], accum=stat[P, 0:1])
            ts("dve", stat[P, 1:2], stat[P, 0:1], 1.0 / D, NORM_EPS, ALU.mult, ALU.add, [stat], [stat])
            act(stat[P, 1:2], stat[P, 1:2], AF.Sqrt, [stat], [stat])
            add("dve", lambda e: e.reciprocal(out=stat[P, 2:3], in_=stat[P, 1:2]), reads=bs(stat), writes=bs(stat))
            dma(f1[0:1, :], normw_d[0:1, 0:512], writes=[f1])
            dma(f2[0:1, :], normw_d[0:1, 512:1024], writes=[f2])
            for hlf, ft in enumerate((f1, f2)):
                mm(banks[0][P, :], onesf[0:1, 0:NS], ft[0:1, :], True, True, [onesf, ft], [banks[0]])
                stt(Gb[P, hlf * 512:(hlf + 1) * 512], modsb[P, D + hlf * 512:D + (hlf + 1) * 512], 1.0, banks[0][P, :], ALU.add, ALU.mult,
                    [modsb, banks[0]], [Gb])
            stt(tmpf[P, :], xs.ap, stat[P, 2:3], Gb[P, :], ALU.mult, ALU.mult, [xs, stat, Gb], [tmpf])
            tt("pool", hs.ap, tmpf[P, :], modsb[P, 0:D], ALU.add, [tmpf, modsb], [hs])
            bk = banks[1]
            for c in range(8):
                tr(bview(bk)[:, c * 16:(c + 1) * 16], hs[:, c * 128:(c + 1) * 128], [hs], [bk])
            cp("dve", hsT.ap.rearrange("p c t -> p (c t)"), bview(bk)[:, 0:128], [bk], [hsT])
            for gi, c0 in enumerate(range(0, INW, 512)):
                w = min(512, INW - c0)
                wt = load_w(c0, w)
                bk = PB[gi % 2]
                for kc in range(8):
                    mm(bk[P, 0:w], hsT[:, kc, :], wt[:, kc, 0:w], kc == 0, kc == 7, [hsT, wt], [bk])
                cp("dve", zs[:, c0:c0 + w], bk[P, 0:w], [bk], [zs])
            dma(ks_d, zs[:, 512:1024], reads=[zs])
            dma(vs_d, zs[:, 1024:1536], reads=[zs])
            dma(shs_d, zs[:, 2560:4224], reads=[zs])
            qw = Buf("qscr")
            dma(qscr_d, zs[:, 0:512], reads=[zs], writes=[qw])
            dma(f3[0:8, :], scst_d[:, 0:512], writes=[f3])
            dma(f2[0:8, :], scst_d[:, 512:1024], writes=[f2])
            dma(f1[0:32, 0:384], ohs_d, writes=[f1])
            bS = View(ar[:, 8576:8600], strips.b)
            for br in range(3):
                mm(banks[2][:, br * 8:(br + 1) * 8], f1[0:32, br * 128:(br + 1) * 128], relb[0:32, :], True, True, [f1, relb], [banks[2]])
            cp("dve", bS.ap, banks[2][:, 0:24], [banks[2]], [bS])
            f32v = lambda t_, pat: View(t_.ap.rearrange(pat).bitcast(F32), t_.b)
            kst = [f32v(alT, "p a b -> p (a b)"), f32v(nbT, "p a b -> p (a b)")]
            vst = [f32v(gaT, "p a b -> p (a b)"), f32v(rhT, "p a b -> p (a b)"), f32v(rkT, "p a b -> p (a b)")]
            qb_ = f32v(nbK, "p a b -> p (a b)")
            gk32 = gaK.ap.rearrange("p a b -> p (a b)").bitcast(F32)
            lraw = View(gk32[:, 0:24], gaK.b)
            esb = View(gk32[:, 32:56], gaK.b)
            vk32 = vK.ap.rearrange("p a b -> p (a b)").bitcast(F32)
            m1 = View(vk32[0:8, :], vK.b)
            m2 = View(gk32[0:8, 64:80], gaK.b)
            bATT, bDEN, bP1, bP2 = banks[4], banks[5], banks[6], banks[7]
            mm(bATT[P, :], zerob[:, 0:NS], wst[0][:, 0, :], True, False, [zerob, wst[0]], [bATT])
            mm(bDEN[P, 0:8], zerob[:, 0:NS], wst[0][:, 0, 0:8], True, False, [zerob, wst[0]], [bDEN])
            dist = [1, 4, 16]
            for s_ in range(NS):
                dma(qb_.ap, AP(qscr_d.tensor, s_ * 512, [[0, 128], [1, 512]]), reads=[qw], writes=[qb_])
                for br in range(3):
                    Dd = dist[br]
                    off = (s_ * WIN + (WIN - 128 * Dd)) * 512
                    kk_ = kst[br % 2]
                    vv_ = vst[br]
                    dma(kk_.ap, AP(ck_d.tensor, off, [[Dd * 512, 128], [1, 512]]), writes=[kk_])
                    dma(vv_.ap, AP(cvv_d.tensor, off, [[Dd * 512, 128], [1, 512]]), writes=[vv_])
                    tt("dve", kk_.ap, kk_.ap, qb_.ap, ALU.mult, [kk_, qb_], [kk_])
                    add("dve", lambda e, kk_=kk_, br=br: e.tensor_reduce(out=lraw[:, br * 8:(br + 1) * 8], in_=kk_.ap.rearrange("p (h d) -> p h d", h=8),
                                                                        axis=AX.X, op=ALU.add), reads=bs(kk_), writes=bs(lraw))
                stt(lraw.ap, lraw.ap, 0.125, bS.ap, ALU.mult, ALU.add, [lraw, bS], [lraw])
                act(esb.ap, lraw.ap, AF.Exp, [lraw], [esb])
                for br in range(3):
                    mm(bP1[0:8, :], esb[:, br * 8:(br + 1) * 8], vst[br].ap, br == 0, br == 2, [esb, vst[br]], [bP1])
                for br in range(3):
                    mm(bP2[0:8, 0:2], esb[:, br * 8:(br + 1) * 8], cst[:, C_ONE:C_ONE + 2], br == 0, br == 2, [esb, cst], [bP2])
                tt("dve", m1.ap, bP1[0:8, :], f3[0:8, :], ALU.mult, [bP1, f3], [m1])
                cp("act", m2[:, 8:9], bP2[0:8, 0:1], [bP2], [m2])
                ts("dve", m2[:, 0:8], f2[0:8, 256:264], m2[:, 8:9], None, ALU.mult, ALU.bypass, [f2, m2], [m2])
                mm(bATT[P, :], f2[0:8, s_ * 16:(s_ + 1) * 16], m1.ap, False, s_ == NS - 1, [f2, m1], [bATT])
                mm(bDEN[P, 0:8], f2[0:8, s_ * 16:(s_ + 1) * 16], m2[:, 0:8], False, s_ == NS - 1, [f2, m2], [bDEN])
            t1 = View(Pw.ap.bitcast(F32)[0:NS, :], Pw.b)
            tt("dve", t1.ap, zs[:, 0:512], zs[:, 512:1024], ALU.mult, [zs], [t1])
            add("dve", lambda e: e.tensor_reduce(out=stat[P, 8:16], in_=t1.ap.rearrange("p (h d) -> p h d", h=8), axis=AX.X, op=ALU.add),
                reads=bs(t1), writes=bs(stat))
            dma(stat[P, 16:24], AP(relb_d.tensor, 0, [[0, NS], [1, 8]]), writes=[stat])
            stt(stat[P, 8:16], stat[P, 8:16], 0.125, stat[P, 16:24], ALU.mult, ALU.add, [stat], [stat])
            act(stat[P, 8:16], stat[P, 8:16], AF.Exp, [stat], [stat])
            ts("dve", stat[P, 8:16], stat[P, 8:16], 3.0, None, ALU.mult, ALU.bypass, [stat], [stat])
            tt("dve", stat[P, 24:32], bDEN[P, 0:8], stat[P, 8:16], ALU.add, [bDEN, stat], [stat])
            add("dve", lambda e: e.reciprocal(out=stat[P, 24:32], in_=stat[P, 24:32]), reads=bs(stat), writes=bs(stat))
            t13 = t1.ap.rearrange("p (h d) -> p h d", h=8)
            tt("dve", t13, zs[:, 1024:1536].rearrange("p (h d) -> p h d", h=8), stat[P, 8:16].unsqueeze(2).to_broadcast([NS, 8, 64]), ALU.mult,
               [zs, stat], [t1])
            tt("dve", t1.ap, t1.ap, bATT[P, :], ALU.add, [t1, bATT], [t1])
            tt("dve", t13, t13, stat[P, 24:32].unsqueeze(2).to_broadcast([NS, 8, 64]), ALU.mult, [t1, stat], [t1])
            act(f4[P, :], zs[:, 1536:1792], AF.Silu, [zs], [f4])
            act(f5[P, :], zs[:, 1792:2048], AF.Silu, [zs], [f5])
            tt("dve", hs[:, 0:256], t1[:, 0:256], f4[P, :], ALU.mult, [t1, f4], [hs])
            tt("dve", hs[:, 256:512], t1[:, 256:512], f5[P, :], ALU.mult, [t1, f5], [hs])
            UO = 2560
            prv = View(ar[0:NS, 4096:4096 + SHW], strips.b)
            mub = View(ar[0:NS, 4096 + SHW:4096 + 2 * SHW], strips.b)
            dma(prv.ap, ssh_d, writes=[prv])
            dma(mub.ap, AP(muperm_d.tensor, 0, [[0, NS], [1, SHW]]), writes=[mub])
            tt("dve", prv.ap, prv.ap, zs[:, UO:UO + SHW], ALU.subtract, [prv, zs], [prv])
            tt("dve", prv.ap, prv.ap, mub.ap, ALU.mult, [prv, mub], [prv])
            tt("dve", prv.ap, prv.ap, zs[:, UO:UO + SHW], ALU.add, [prv, zs], [prv])
            u3 = prv[:, 128:SHW].rearrange("p (c q x) -> p c q x", c=4, q=3)
            r_v, k_v, v_v = u3[:, :, 0, :], u3[:, :, 1, :], u3[:, :, 2, :]
            nat = lambda t_: t_.rearrange("p (c x) -> p c x", c=4)
            hold = [Gb[P, 0:512], Gb[P, 512:1024], SHb[P, 0:512], SHb[P, 512:1024], GATEb[P, 0:512], GATEb[P, 512:1024], tmpf[P, 512:1024]]
            hbufs = [Gb, Gb, SHb, SHb, GATEb, GATEb, tmpf]
            for i_ in range(7):
                dma(hold[i_], AP(prow_d.tensor, i_ * 512, [[0, NS], [1, 512]]), writes=bs(hbufs[i_]))
            W0b, A0b, KKb, KAb, RKb, LWb, LBb = hold
            cp("dve", hs[:, 512:640], prv[:, 0:128], [prv], [hs])
            bk = banks[0]
            tr(bview(bk)[:, 0:NS], hs[:, 512:640], [hs], [bk])
            act(xxT[0:64, :], bview(bk)[0:64, 0:NS], AF.Tanh, [bk], [xxT])
            cp("dve", xxT[64:128, :], bview(bk)[64:128, 0:NS], [bk], [xxT])
            mm(banks[2][P, :], xxT[0:64, :], wlora[0:64, :], True, True, [xxT, wlora], [banks[2]])
            mm(banks[3][P, :], xxT[64:128, :], alora[64:128, :], True, True, [xxT, alora], [banks[3]])
            g1 = View(f1[P, :], f1.b); g2 = View(f2[P, :], f2.b)
            tt("dve", g1.ap, banks[2][P, :], W0b, ALU.add, [banks[2], Gb], [f1])
            act(g1.ap, g1.ap, AF.Sigmoid, [f1], [f1])
            act(g1.ap, g1.ap, AF.Exp, [f1], [f1], scale=-C0)
            tt("dve", g2.ap, banks[3][P, :], A0b, ALU.add, [banks[3], Gb], [f2])
            act(g2.ap, g2.ap, AF.Sigmoid, [f2], [f2])
            V6t = View(Sst[0:NS, 0:3072], strips.b)
            v6 = lambda q_: V6t[:, q_ * 512:(q_ + 1) * 512]
            cp("pool", v6(0), g1.ap, [f1], [V6t])
            kkt = View(f3[P, :], f3.b)
            tt("dve", nat(kkt.ap), k_v, nat(KKb), ALU.mult, [prv, SHb], [f3])
            tt("dve", t1.ap, kkt.ap, kkt.ap, ALU.mult, [f3], [t1])
            add("dve", lambda e: e.tensor_reduce(out=stat[P, 32:40], in_=t1.ap.rearrange("p (h d) -> p h d", h=8), axis=AX.X, op=ALU.add),
                reads=bs(t1), writes=bs(stat))
            ts("dve", stat[P, 32:40], stat[P, 32:40], 1e-24, None, ALU.add, ALU.bypass, [stat], [stat])
            act(stat[P, 32:40], stat[P, 32:40], AF.Sqrt, [stat], [stat])
            add("dve", lambda e: e.reciprocal(out=stat[P, 32:40], in_=stat[P, 32:40]), reads=bs(stat), writes=bs(stat))
            tt("dve", v6(1).rearrange("p (h d) -> p h d", h=8), kkt.ap.rearrange("p (h d) -> p h d", h=8),
               stat[P, 32:40].unsqueeze(2).to_broadcast([NS, 8, 64]), ALU.mult, [f3, stat], [V6t])
            tt("dve", v6(2), v6(1), g2.ap, ALU.mult, [V6t, f2], [V6t])
            tt("dve", t1.ap, g2.ap, KAb, ALU.mult, [f2, SHb], [t1])
            tt("dve", t1.ap, t1.ap, KAb, ALU.subtract, [t1, SHb], [t1])
            ts("dve", t1.ap, t1.ap, 1.0, None, ALU.add, ALU.bypass, [t1], [t1])
            tt("dve", nat(v6(3)), k_v, nat(t1.ap), ALU.mult, [prv, t1], [V6t])
            cp("dve", nat(v6(4)), r_v, [prv], [V6t])
            cp("dve", nat(v6(5)), v_v, [prv], [V6t])
            tt("dve", t1.ap, v6(4), v6(3), ALU.mult, [V6t], [t1])
            tt("dve", t1.ap, t1.ap, RKb, ALU.mult, [t1, GATEb], [t1])
            add("dve", lambda e: e.tensor_reduce(out=stat[P, 40:48], in_=t1.ap.rearrange("p (h d) -> p h d", h=8), axis=AX.X, op=ALU.add),
                reads=bs(t1), writes=bs(stat))
            tt("dve", kkt.ap.rearrange("p (h d) -> p h d", h=8), v6(5).rearrange("p (h d) -> p h d", h=8),
               stat[P, 40:48].unsqueeze(2).to_broadcast([NS, 8, 64]), ALU.mult, [V6t, stat], [f3])
            vw = Buf("vscr")
            for q_ in range(6):
                dma(vscr_d[:, :, q_, :], v6(q_).rearrange("p (h j) -> p h j", h=8), reads=[V6t], writes=[vw])
            dma(vec6.ap, vscr_d.rearrange("s h q j -> (s h) (q j)"), reads=[vw], writes=[vec6])
            dma(Sst.ap, swkv_d, reads=[V6t], writes=[Sst])
            S3 = Sst.ap.rearrange("p (i j) -> p i j", i=64)
            T3 = TMP.ap.rearrange("p (i j) -> p i j", i=64)
            vq = lambda q_: vec6[:, q_ * 64:(q_ + 1) * 64]
            rowb = lambda q_: vq(q_).unsqueeze(1).to_broadcast([128, 64, 64])
            colb = lambda ap_: ap_.unsqueeze(2).to_broadcast([128, 64, 64])
            tt("dve", T3, S3, rowb(1), ALU.mult, [Sst, vec6], [TMP])
            add("dve", lambda e: e.tensor_reduce(out=f6[:, 0:64], in_=T3, axis=AX.X, op=ALU.add),
                reads=bs(TMP), writes=bs(f6))
            tt("pool", S3, S3, rowb(0), ALU.mult, [Sst, vec6], [Sst])
            tt("dve", T3, colb(f6[:, 0:64]), rowb(2), ALU.mult, [f6, vec6], [TMP])
            tt("dve", Sst.ap, Sst.ap, TMP.ap, ALU.subtract, [Sst, TMP], [Sst])
            tt("pool", T3, colb(vq(5)), rowb(3), ALU.mult, [vec6], [TMP])
            tt("dve", Sst.ap, Sst.ap, TMP.ap, ALU.add, [Sst, TMP], [Sst])
            dma(wkvs_d, Sst.ap, reads=[Sst])
            tt("dve", T3, S3, rowb(4), ALU.mult, [Sst, vec6], [TMP])
            add("dve", lambda e: e.tensor_reduce(out=f6[:, 64:128], in_=T3, axis=AX.X, op=ALU.add), reads=bs(TMP), writes=bs(f6))
            yv = f6[:, 64:128]
            add("dve", lambda e: e.tensor_reduce(out=f6[:, 128:129], in_=yv, axis=AX.X, op=ALU.add), reads=bs(f6), writes=bs(f6))
            ts("dve", f6[:, 128:129], f6[:, 128:129], 1.0 / HD, None, ALU.mult, ALU.bypass, [f6], [f6])
            ts("dve", f6[:, 192:256], yv, f6[:, 128:129], None, ALU.subtract, ALU.bypass, [f6], [f6])
            tt("dve", f7[:, 0:64], f6[:, 192:256], f6[:, 192:256], ALU.mult, [f6], [f7])
            add("dve", lambda e: e.tensor_reduce(out=f6[:, 129:130], in_=f7[:, 0:64], axis=AX.X, op=ALU.add), reads=bs(f7), writes=bs(f6))
            ts("dve", f6[:, 129:130], f6[:, 129:130], 1.0 / HD, GN_EPS, ALU.mult, ALU.add, [f6], [f6])
            act(f6[:, 129:130], f6[:, 129:130], AF.Sqrt, [f6], [f6])
            add("dve", lambda e: e.reciprocal(out=f6[:, 130:131], in_=f6[:, 129:130]), reads=bs(f6), writes=bs(f6))
            ts("dve", f7[:, 64:128], f6[:, 192:256], f6[:, 130:131], None, ALU.mult, ALU.bypass, [f6], [f7])
            yw = Buf("yscr")
            dma(yscr_d, f7[:, 64:128], reads=[f7], writes=[yw])
            dma(t1.ap, yscr_d.rearrange("(s h) i -> s (h i)", h=8), reads=[yw], writes=[t1])
            tt("dve", t1.ap, t1.ap, LWb, ALU.mult, [t1, GATEb], [t1])
            tt("dve", t1.ap, t1.ap, LBb, ALU.add, [t1, tmpf], [t1])
            tt("dve", t1.ap, t1.ap, kkt.ap, ALU.add, [t1, f3], [t1])
            act(f4[P, :], zs[:, 2048:2304], AF.Silu, [zs], [f4])
            act(f5[P, :], zs[:, 2304:2560], AF.Silu, [zs], [f5])
            tt("dve", hs[:, 512:768], t1[:, 0:256], f4[P, :], ALU.mult, [t1, f4], [hs])
            tt("dve", hs[:, 768:1024], t1[:, 256:512], f5[P, :], ALU.mult, [t1, f5], [hs])
            bk = banks[1]
            for c in range(8):
                tr(bview(bk)[:, c * 16:(c + 1) * 16], hs[:, c * 128:(c + 1) * 128], [hs], [bk])
            cp("dve", hsT.ap.rearrange("p c t -> p (c t)"), bview(bk)[:, 0:128], [bk], [hsT])
            for hf in range(2):
                wt = wst[hf]
                op = dma(wt.ap, wobf_d[:, hf * 512:(hf + 1) * 512].rearrange("(kc p) c -> p kc c", p=128), reads=[], writes=[wt])
                op.deps.extend(wstores)
                bO_ = banks[2 + hf]
                for fc in range(8):
                    mm(bO_[P, :], hsT[:, fc, :], wt[:, fc, :], fc == 0, fc == 7, [hsT, wt], [bO_])
                hsl = slice(hf * 512, (hf + 1) * 512)
                tt("dve", tmpf[P, hsl] if hf == 0 else f1[P, :], bO_[P, :], modsb[P, 2 * D + hf * 512:2 * D + (hf + 1) * 512], ALU.mult,
                   [bO_, modsb], [tmpf if hf == 0 else f1])
            xo2 = View(Gb[P, :], Gb.b)
            tt("dve", xo2[:, 0:512], tmpf[P, 0:512], xs[:, 0:512], ALU.add, [tmpf, xs], [xo2])
            tt("dve", xo2[:, 512:1024], f1[P, :], xs[:, 512:1024], ALU.add, [f1, xs], [xo2])
            act(hb[P, 0, :], xo2.ap, AF.Square, [xo2], [hb, stat], accum=stat[P, 56:57])
            ts("dve", stat[P, 57:58], stat[P, 56:57], 1.0 / D, NORM_EPS, ALU.mult, ALU.add, [stat], [stat])
            act(stat[P, 57:58], stat[P, 57:58], AF.Sqrt, [stat], [stat])
            add("dve", lambda e: e.reciprocal(out=stat[P, 58:59], in_=stat[P, 57:58]), reads=bs(stat), writes=bs(stat))
            stt(xo2.ap, xo2.ap, stat[P, 58:59], FNWb[P, :], ALU.mult, ALU.mult, [xo2, stat, FNWb], [xo2])
            dma(ys_d, xo2.ap, reads=[xo2])
        bk = banks[0]
        for c in range(4):
            mm(bk[:, c * 128:(c + 1) * 128], N0f[:, c * 128:(c + 1) * 128], identf.ap, True, True, [N0f, identf], [bk])
        cp("dve", f1.ap, bk[:, :], [bk], [f1])
        for h in range(NH):
            c, hh = h // 2, h % 2
            dma(wkv_d[h], f1[64 * hh:64 * hh + 64, c * 128 + 64 * hh:c * 128 + 64 * hh + 64], reads=[f1])
        for ch in range(13):
            dma(shp_d[ch:ch + 1, :].rearrange("o p -> p o"), ulast[:, ch:ch + 1], reads=[ulast])

        with nc.Block() as block:
            S.emit(block)
    return nc


def _tokshift(add, act, tt, stt, cp, bs, bk, ch, dst, uraw, carry, ulast, dtmp, pvec, V_MU, last):
    cp("pool", uraw[:, 0:1], carry[:, ch:ch + 1], [carry], [uraw])
    act(uraw[:, 1:SBW + 1], bk[:, 0:SBW], AF.Copy, [bk], [uraw])
    if last:
        cp("dve", ulast[:, ch:ch + 1], bk[:, SBW - 1:SBW], [bk], [ulast])
    cp("pool", carry[:, ch:ch + 1], uraw[:, SBW:SBW + 1], [uraw], [carry])
    tt("pool", dtmp.ap, uraw[:, 0:SBW], uraw[:, 1:SBW + 1], ALU.subtract, [uraw], [dtmp])
    stt(dst.ap, dtmp.ap, pvec[:, V_MU + ch:V_MU + ch + 1], uraw[:, 1:SBW + 1], ALU.mult, ALU.add, [dtmp, pvec, uraw], [dst])


def _t5_bucket_np(dist):
    dist = np.asarray(dist, np.int32)
    nf = np.maximum(dist, 16).astype(np.float32)
    large = 16 + (np.log(nf / np.float32(16)) / np.float32(math.log(2048 / 16)) * np.float32(16)).astype(np.int32)
    large = np.minimum(large, 31)
    return np.where(dist < 16, dist, large)


def _consts():
    cst = np.zeros((128, 1280), np.float32)
    p = np.arange(128)[:, None]
    s = np.arange(64)[None, :]
    cst[:, 0:128] = np.eye(128, dtype=np.float32)
    s = np.arange(128)[None, :]
    cst[:, 128:256] = (p > s)
    cst[:, 256:384] = (p >= s)
    cst[:, 384:512] = (p < s)
    cst[:, 512:640] = (p <= s)
    cst[:, 640:768] = (p // 64 == (s // 64))
    cst[:, 768:770] = (p // 64 == np.arange(2)[None, :])
    cst[:, 770:1026] = (np.arange(256)[None, :] % 128 != 0)
    cst[:, 1026:1154] = 1.0
    d = np.arange(FLEN) - 127
    valid = (d >= 0) & (d <= 2048)
    m = ((d <= 128).astype(np.int32) + ((d % 4 == 0) & (d <= 512)).astype(np.int32)
         + ((d % 16 == 0) & (d <= 2048)).astype(np.int32))
    m = np.where(valid, m, 0)
    ok = m > 0
    bucket = _t5_bucket_np(np.clip(d, 0, 2048))
    oh = np.zeros((32, FLEN), np.float32)
    oh[bucket[ok], np.nonzero(ok)[0]] = 1.0
    fcr = np.where(ok, np.log(np.maximum(m, 1)).astype(np.float32), np.float32(NEG)).astype(np.float32)
    fc = np.tile(fcr[None, :], (NH, 1)).astype(np.float32)
    ohs = np.zeros((32, 384), np.float32)
    for br, Dd in enumerate((1, 4, 16)):
        dj = Dd * (128 - np.arange(128))
        ohs[_t5_bucket_np(dj), br * 128 + np.arange(128)] = 1.0
    scst = np.zeros((8, 1024), np.float32)
    hp = np.arange(8)[:, None]
    scst[:, 0:512] = (np.arange(512)[None, :] // 64 == hp)
    col = np.arange(256)[None, :]
    scst[:, 512:768] = ((col % 16) == (col // 16)) * np.ones((8, 1), np.float32)
    scst[:, 768:776] = (np.arange(8)[None, :] == hp)
    return cst, oh, fc, ohs, scst


def _perm():
    cols = list(range(0, 2048)) + list(range(3712, 4224)) + list(range(3584, 3712))
    for c in range(4):
        cols += list(range(2048 + 128 * c, 2048 + 128 * c + 128))
        cols += list(range(2560 + 128 * c, 2560 + 128 * c + 128))
        cols += list(range(3072 + 128 * c, 3072 + 128 * c + 128))
    return np.array(cols, np.int64)


def _uchunks():
    ch = [np.arange(1536, 1664)]
    for c in range(4):
        ch.append(np.arange(128 * c, 128 * c + 128))
        ch.append(np.arange(512 + 128 * c, 512 + 128 * c + 128))
        ch.append(np.arange(1024 + 128 * c, 1024 + 128 * c + 128))
    return ch


_NC_CACHE = {}
DBG_MODE = None


def _get_nc(TP, NS, KEEP):
    key = (TP, NS, KEEP)
    if key not in _NC_CACHE:
        _NC_CACHE[key] = build(TP, NS, KEEP, dbg=DBG_MODE)
    return _NC_CACHE[key]


def _core_inputs(inp, b, s0, NS, TP):
    f = lambda a: np.ascontiguousarray(np.asarray(a, dtype=np.float32))
    cst, oh, fc, ohs, scst = _consts()
    perm = _perm()
    uch = _uchunks()
    mu = f(inp["mu_shift"])[0]
    ucat = np.concatenate(uch)
    pvec = np.zeros((128, 40), np.float32)
    for i, idx in enumerate(uch):
        pvec[:, i] = mu[idx]
    for c in range(4):
        sl = slice(128 * c, 128 * c + 128)
        pvec[:, 13 + c] = f(inp["w0"])[0][sl]
        pvec[:, 17 + c] = f(inp["a0"])[0][sl]
        pvec[:, 21 + c] = f(inp["k_k"])[0][sl]
        pvec[:, 25 + c] = f(inp["k_a"])[0][sl]
        pvec[:, 33 + c] = f(inp["r_k"])[0].reshape(-1)[sl]
    m = {
        "x": f(inp["x_prompt"][b][:TP]),
        "cv": f(np.concatenate([np.asarray(inp["c_prompt"])[b:b + 1], np.asarray(inp["c_sample"])[s0:s0 + NS]], 0)),
        "wperm": f(np.asarray(inp["w_in"])[0][:, perm]),
        "adaw": f(inp["ada_w"])[0],
        "adab": f(inp["ada_b"])[0][None, :],
        "normw": f(inp["norm_w"])[0][None, :],
        "fnw": f(inp["final_norm_w"])[None, :],
        "wout": f(inp["w_out"])[0],
        "relb": f(inp["rel_bias"]),
        "oh": oh, "fc": fc, "cst": cst, "pvec": pvec,
        "wlora": f(inp["w_lora_b"])[0],
        "alora": f(inp["a_lora_b"])[0],
        "lnwb": f(np.concatenate([np.asarray(inp["ln_x_w"])[0], np.asarray(inp["ln_x_b"])[0]])[None, :]),
        "xs": f(np.asarray(inp["x_sample"])[s0:s0 + NS, 0]),
        "cvs": f(np.asarray(inp["c_sample"])[s0:s0 + NS]),
        "ck": f(np.asarray(inp["cache_win_k"])[0, s0:s0 + NS]).reshape(NS, WIN, 512),
        "cvv": f(np.asarray(inp["cache_win_v"])[0, s0:s0 + NS]).reshape(NS, WIN, 512),
        "swkv": f(np.asarray(inp["state_wkv"])[0, s0:s0 + NS]).reshape(NS * NH, HD * HD),
        "ssh": f(np.asarray(inp["state_shift"])[0, s0:s0 + NS][:, ucat]),
        "prow": f(np.stack([np.asarray(inp["w0"])[0], np.asarray(inp["a0"])[0], np.asarray(inp["k_k"])[0], np.asarray(inp["k_a"])[0],
                            np.asarray(inp["r_k"])[0].reshape(-1), np.asarray(inp["ln_x_w"])[0], np.asarray(inp["ln_x_b"])[0],
                            np.zeros(512, np.float32)])),
        "muperm": f(mu[ucat][None, :]),
        "ohs": ohs, "scst": scst,
    }
    return m


def kernel(**inp):
    B, TP, _ = np.asarray(inp["x_prompt"]).shape
    NSAMP = np.asarray(inp["x_sample"]).shape[0]
    NS = NSAMP // 8
    KEEP = min(WIN, TP)
    nc = _get_nc(TP, NS, KEEP)
    in_maps = [_core_inputs(inp, i % B, i * NS, NS, TP) for i in range(8)]
    res = run_bass_kernel_spmd(nc, in_maps, core_ids=list(range(8))).results
    uch = _uchunks()
    y_p = np.stack([res[b]["y"] for b in range(B)]).astype(np.float32)
    wk = np.stack([res[b]["wk"].reshape(KEEP, NH, HD) for b in range(B)])[None].astype(np.float32)
    wv = np.stack([res[b]["wv"].reshape(KEEP, NH, HD) for b in range(B)])[None].astype(np.float32)
    wkv = np.stack([res[b]["wkv"] for b in range(B)])[None].astype(np.float32)
    shp = np.zeros((1, B, SHW), np.float32)
    for b in range(B):
        for i, idx in enumerate(uch):
            shp[0, b, idx] = res[b]["shp"][i]
    ucat = np.concatenate(uch)
    y_s = np.concatenate([res[i]["ys"] for i in range(8)], 0).reshape(NSAMP, 1, D).astype(np.float32)
    wks = np.concatenate([res[i]["ks"] for i in range(8)], 0).reshape(1, NSAMP, 1, NH, HD).astype(np.float32)
    wvs = np.concatenate([res[i]["vs"] for i in range(8)], 0).reshape(1, NSAMP, 1, NH, HD).astype(np.float32)
    wkvs = np.concatenate([res[i]["wkvs"] for i in range(8)], 0).reshape(1, NSAMP, NH, HD, HD).astype(np.float32)
    shs = np.zeros((1, NSAMP, SHW), np.float32)
    shs[0][:, ucat] = np.concatenate([res[i]["shs"] for i in range(8)], 0)
    return (y_p, y_s, wk, wv, wks, wvs, wkv, wkvs, shp, shs)
```

```python
import math
import numpy as np
import concourse.bass as bass
import concourse.mybir as mybir
from concourse.ap import AP
from concourse.bass_utils import run_bass_kernel_spmd
from contextlib import ExitStack

F32 = mybir.dt.float32
BF16 = mybir.dt.bfloat16
AF = mybir.ActivationFunctionType
ALU = mybir.AluOpType
AX = mybir.AxisListType

D = 1024
NH = 8
HD = 64
INW = 4224
SHW = 1664
WIN = 2048
NKT = 17
STRIPW = NKT * 128
FLEN = STRIPW + 128
C0 = math.exp(-0.5)
NEG = -30000.0
NSQ = 6
SBW = 256
TPS = SBW // 128
NORM_EPS = 1e-6
GN_EPS = HD * 1e-5


class Buf:
    __slots__ = ("name", "w", "rs")

    def __init__(self, name):
        self.name = name
        self.w = None
        self.rs = []


class Op:
    __slots__ = ("eng", "fn", "deps", "sig", "val", "dma", "idx")


class Sched:
    ENGS = ["pe", "act", "dve", "pool", "sp"]

    def __init__(self, nc, stack, n_dma=40):
        self.nc = nc
        self.ops = {e: [] for e in self.ENGS}
        self.sem = {e: stack.enter_context(nc.semaphore("s_" + e)) for e in ["pe", "act", "dve", "pool"]}
        self.dsem = [stack.enter_context(nc.semaphore("d%d" % i)) for i in range(n_dma)]
        self.dcnt = [0] * n_dma
        self.dlast = [None] * n_dma
        self.drr = 0
        self.n = 0

    def add(self, eng, fn, reads=(), writes=(), dma=False):
        op = Op()
        op.eng = eng
        op.fn = fn
        op.sig = False
        op.val = None
        op.dma = None
        op.idx = self.n
        self.n += 1
        deps = {}
        for b in reads:
            if b.w is not None:
                deps[b.w.idx] = b.w
        for b in writes:
            if b.w is not None:
                deps[b.w.idx] = b.w
            for r in b.rs:
                deps[r.idx] = r
        if dma:
            k = self.drr
            self.drr = (self.drr + 1) % len(self.dsem)
            if self.dlast[k] is not None:
                deps[self.dlast[k].idx] = self.dlast[k]
            self.dcnt[k] += 16
            op.dma = k
            op.val = self.dcnt[k]
            self.dlast[k] = op
        dl = []
        for d in deps.values():
            if d is op:
                continue
            if d.dma is None and d.eng == "pe" and eng == "pe":
                continue
            if d.dma is None:
                d.sig = True
            dl.append(d)
        op.deps = dl
        for b in reads:
            b.rs.append(op)
        for b in writes:
            b.w = op
            b.rs = []
        self.ops[eng].append(op)
        return op

    def emit(self, block):
        for e in ["pe", "act", "dve", "pool"]:
            c = 0
            for op in self.ops[e]:
                if op.sig:
                    c += 1
                    op.val = c
        fin = [(self.dsem[k], self.dcnt[k]) for k in range(len(self.dsem)) if self.dcnt[k] > 0]

        def run(e, engobj, extra=None):
            waited = {}
            for op in self.ops[e]:
                need = {}
                for d in op.deps:
                    if d.dma is not None:
                        s = self.dsem[d.dma]
                        key = ("d", d.dma)
                    else:
                        s = self.sem[d.eng]
                        key = ("e", d.eng)
                    if need.get(key, (None, 0))[1] < d.val:
                        need[key] = (s, d.val)
                for key, (s, v) in need.items():
                    if waited.get(key, 0) >= v:
                        continue
                    engobj.wait_ge(s, v)
                    waited[key] = v
                ins = op.fn(engobj)
                if op.dma is not None:
                    ins.then_inc(self.dsem[op.dma], 16)
                elif op.sig:
                    ins.then_inc(self.sem[e], 1)
            if extra:
                for s, v in extra:
                    engobj.wait_ge(s, v)

        @block.tensor
        def _(eng):
            run("pe", eng)

        @block.scalar
        def _(eng):
            run("act", eng)

        @block.vector
        def _(eng):
            run("dve", eng)

        @block.gpsimd
        def _(eng):
            run("pool", eng)

        @block.sync
        def _(eng):
            run("sp", eng, fin)


class T:
    def __init__(self, t, name):
        self.t = t
        self.b = Buf(name)

    def __getitem__(self, k):
        return self.t[k]

    @property
    def ap(self):
        return self.t[:]


class View:
    def __init__(self, ap, buf):
        self._ap = ap
        self.b = buf

    def __getitem__(self, k):
        return self._ap[k]

    @property
    def ap(self):
        return self._ap


def bs(*ts):
    return [x.b if hasattr(x, "b") else x for x in ts]


def build(TP, NS, KEEP, dbg=None):
    assert TP % SBW == 0
    NSB = TP // SBW
    nc = bass.Bass("TRN2", target_bir_lowering=False)

    def din(name, shape, dt=F32):
        return nc.dram_tensor(name, list(shape), dt, kind="ExternalInput").ap()

    def dout(name, shape, dt=F32):
        return nc.dram_tensor(name, list(shape), dt, kind="ExternalOutput").ap()

    x_d = din("x", [TP, D])
    cv_d = din("cv", [1 + NS, D])
    wperm_d = din("wperm", [D, INW])
    adaw_d = din("adaw", [D, 3 * D])
    adab_d = din("adab", [1, 3 * D])
    normw_d = din("normw", [1, D])
    fnw_d = din("fnw", [1, D])
    wout_d = din("wout", [D, D])
    relb_d = din("relb", [32, NH])
    oh_d = din("oh", [32, FLEN])
    fc_d = din("fc", [NH, FLEN])
    cst_d = din("cst", [128, 1280])
    pvec_d = din("pvec", [128, 40])
    wlora_d = din("wlora", [64, 512])
    alora_d = din("alora", [64, 512])
    lnwb_d = din("lnwb", [1, 1024])
    xs_d = din("xs", [NS, D])
    ck_d = din("ck", [NS, WIN, 512])
    cvv_d = din("cvv", [NS, WIN, 512])
    swkv_d = din("swkv", [NS * NH, HD * HD])
    ssh_d = din("ssh", [NS, SHW])
    prow_d = din("prow", [8, 512])
    muperm_d = din("muperm", [1, SHW])
    ohs_d = din("ohs", [32, 384])
    scst_d = din("scst", [8, 1024])
    cvs_d = din("cvs", [NS, D])

    y_d = dout("y", [TP, D])
    wk_d = dout("wk", [KEEP, 512])
    wv_d = dout("wv", [KEEP, 512])
    wkv_d = dout("wkv", [NH, HD, HD])
    shp_d = dout("shp", [13, 128])
    ys_d = dout("ys", [NS, D])
    ks_d = dout("ks", [NS, 512])
    vs_d = dout("vs", [NS, 512])
    wkvs_d = dout("wkvs", [NS * NH, HD * HD])
    shs_d = dout("shs", [NS, SHW])
    dbg_d = dout("dbgmix", [TP, D], BF16) if dbg == "mix" else None

    wbf_d = nc.dram_tensor("wbf", [D, INW], BF16, kind="Internal").ap()
    wobf_d = nc.dram_tensor("wobf", [D, D], BF16, kind="Internal").ap()
    fscr_d = nc.dram_tensor("fscr", [NH, FLEN], F32, kind="Internal").ap()
    qscr_d = nc.dram_tensor("qscr", [NS, 512], F32, kind="Internal").ap()
    mscr_d = nc.dram_tensor("mscr", [NS, 3 * D], F32, kind="Internal").ap()
    vscr_d = nc.dram_tensor("vscr", [NS, NH, 6, HD], F32, kind="Internal").ap()
    yscr_d = nc.dram_tensor("yscr", [NS * NH, HD], F32, kind="Internal").ap()
    fscr2_d = nc.dram_tensor("fscr2", [NH * 128, FLEN], F32, kind="Internal").ap()

    with ExitStack() as st:
        S = Sched(nc, st)
        add = S.add

        def sb(name, shape, dt):
            return T(st.enter_context(nc.sbuf_tensor("sb_" + name, list(shape), dt)), name)

        def psb(name, shape, dt):
            return T(st.enter_context(nc.psum_tensor("ps_" + name, list(shape), dt)), name)

        rr = [0]

        def anyeng():
            rr[0] += 1
            return "dve" if rr[0] % 2 else "pool"

        banks = [psb("bk%d" % i, [128, 512], F32) for i in range(8)]
        cst = sb("cst", [128, 1280], F32)
        pvec = sb("pvec", [128, 40], F32)
        identb = sb("identb", [128, 128], BF16)
        blkb = sb("blkb", [128, 128], BF16)
        zerob = sb("zerob", [128, 128], BF16)
        ind2b = sb("ind2b", [128, 2], BF16)
        xb = sb("xb", [128, TPS, D], F32)
        hb = sb("hb", [128, TPS, D], BF16)
        tmpf = sb("tmpf", [128, D], F32)
        hT = sb("hT", [128, 8, SBW], BF16)
        wst = [sb("wst%d" % i, [128, 8, 512], BF16) for i in range(2)]
        Gb = sb("Gb", [128, D], F32)
        SHb = sb("SHb", [128, D], F32)
        GATEb = sb("GATEb", [128, D], F32)
        FNWb = sb("FNWb", [128, D], F32)
        LNWb = sb("LNWb", [128, 512], F32)
        LNBb = sb("LNBb", [128, 512], F32)
        strips = sb("strips", [128, NH, STRIPW], BF16)
        NRING = 18
        kTr = sb("kTr", [128, 4, NRING * 128], BF16)
        Vr = sb("Vr", [128, NRING, NH, 66], BF16)
        qTz = sb("qTz", [128, NH, SBW], BF16)
        gat = sb("gat", [128, TPS, 512], BF16)
        grw = sb("grw", [128, TPS, 512], BF16)
        stat = sb("stat", [128, 64], F32)
        uraw = sb("uraw", [128, SBW + 8], BF16)
        carry = sb("carry", [128, 16], BF16)
        ulast = sb("ulast", [128, 16], F32)
        dtmp = sb("dtmp", [128, SBW], BF16)
        um = [sb("um%d" % i, [128, SBW], BF16) for i in range(3)]
        txw = sb("txw", [128, SBW], BF16)
        wlora = sb("wlora", [128, 512], BF16)
        alora = sb("alora", [128, 512], BF16)
        f1 = sb("f1", [128, 512 if 1 <= 3 else SBW], F32)
        f2 = sb("f2", [128, 512 if 2 <= 3 else SBW], F32)
        f3 = sb("f3", [128, 512 if 3 <= 3 else SBW], F32)
        f4 = sb("f4", [128, 512 if 4 <= 3 else SBW], F32)
        f5 = sb("f5", [128, 512 if 5 <= 3 else SBW], F32)
        f6 = sb("f6", [128, 512 if 6 <= 3 else SBW], F32)
        f7 = sb("f7", [128, 512 if 7 <= 3 else SBW], F32)
        f8 = sb("f8", [128, 512 if 8 <= 3 else SBW], F32)
        alT = sb("alT", [128, 4, SBW], BF16)
        nbT = sb("nbT", [128, 4, SBW], BF16)
        gaT = sb("gaT", [128, 4, SBW], BF16)
        rhT = sb("rhT", [128, 4, SBW], BF16)
        rkT = sb("rkT", [128, 4, SBW], BF16)
        WC = sb("WC", [128, 4, TPS], F32)
        nbK = sb("nbK", [128, TPS, 512], BF16)
        gaK = sb("gaK", [128, TPS, 512], BF16)
        vK = sb("vK", [128, TPS, 512], BF16)
        Pw = sb("Pw", [128, 1024], BF16)
        PTw = sb("PTw", [128, 1024], BF16)
        Zw = sb("Zw", [128, 1024], BF16)
        AagT = sb("AagT", [128, 1024], BF16)
        BrbT = sb("BrbT", [128, 1024], BF16)
        BrgT = sb("BrgT", [128, 1024], BF16)
        Rb = sb("Rb", [128, 512], BF16)
        Ub = sb("Ub", [128, 512], BF16)
        N0f = sb("N0f", [128, 512], F32)
        N0b = [sb("N0b%d" % i, [128, 512], BF16) for i in range(2)]
        identf = sb("identf", [128, 128], F32)
        mixed = sb("mixed", [128, TPS, D], BF16)
        mixT = sb("mixT", [128, TPS, 8, 128], BF16)
        expT = [sb("expT%d" % i, [128, 4, 128], BF16) for i in range(2)]
        attf = f3
        xo = tmpf
        yo = [sb("yo0", [128, D], F32)]
        kvo = [View(yo[0][:, 0:512], yo[0].b)]
        csil = View(mixed[0:32, 0, :], mixed.b)
        csT = sb("csT", [128, 8, 32], BF16)
        hsT = sb("hsT", [128, 8, 16], BF16)
        xxT = sb("xxT", [128, 16], BF16)
        relb = sb("relb", [32, NH], F32)
        onesf = sb("onesf", [1, 128], F32)
        modsb = View(kTr.ap.rearrange("p a b -> p (a b)").bitcast(F32)[0:32, 0:3 * D], kTr.b)
        frow = View(strips.ap.rearrange("p a b -> p (a b)").bitcast(F32)[0:8, 0:FLEN], strips.b)
        rowtmp = View(yo[0][0:1, :], yo[0].b)

        C_I = 0
        C_ML = 128
        C_MLI = 256
        C_MU = 384
        C_MUI = 512
        C_BLK = 640
        C_IND = 768
        C_SEG = 770
        C_ONE = 1026
        V_MU = 0
        V_W0 = 13
        V_A0 = 17
        V_KK = 21
        V_KA = 25
        V_OMKA = 29
        V_RK = 33

        def dma(out, in_, reads=(), writes=()):
            return add("sp", lambda e: e.dma_start(out=out, in_=in_), reads=bs(*reads), writes=bs(*writes), dma=True)

        def mm(out, lhsT, rhs, start, stop, reads, writes):
            return add("pe", lambda e: e.matmul(out, lhsT=lhsT, rhs=rhs, start=start, stop=stop, skip_group_check=True),
                       reads=bs(*reads), writes=bs(*writes))

        def tr(out, in_, reads, writes, idn=None):
            idt = identb
            return add("pe", lambda e: e.transpose(out, in_, idt[0:in_.shape[0], 0:in_.shape[0]]),
                       reads=bs(idt, *reads), writes=bs(*writes))

        def act(out, in_, func, reads, writes, scale=1.0, bias=0.0, accum=None):
            kw = {}
            if accum is not None:
                kw["accum_out"] = accum
            return add("act", lambda e: e.activation(out=out, in_=in_, func=func, scale=scale, bias=bias, **kw),
                       reads=bs(*reads), writes=bs(*writes))

        def tt(eng, out, in0, in1, op, reads, writes):
            return add(eng, lambda e: e.tensor_tensor(out=out, in0=in0, in1=in1, op=op), reads=bs(*reads), writes=bs(*writes))

        def ts(eng, out, in0, s1, s2, op0, op1, reads, writes):
            return add(eng, lambda e: e.tensor_scalar(out=out, in0=in0, scalar1=s1, scalar2=s2, op0=op0, op1=op1),
                       reads=bs(*reads), writes=bs(*writes))

        def stt(out, in0, scalar, in1, op0, op1, reads, writes):
            return add("dve", lambda e: e.scalar_tensor_tensor(out=out, in0=in0, scalar=scalar, in1=in1, op0=op0, op1=op1),
                       reads=bs(*reads), writes=bs(*writes))

        def cp(eng, out, in_, reads, writes):
            if eng == "act":
                return act(out, in_, AF.Copy, reads, writes)
            return add(eng, lambda e: e.tensor_copy(out=out, in_=in_), reads=bs(*reads), writes=bs(*writes))

        def bview(bank, dt=BF16):
            return bank.t[:].bitcast(dt)

        dma(cst.ap, cst_d, writes=[cst])
        dma(pvec.ap, pvec_d, writes=[pvec])
        dma(relb.ap, relb_d, writes=[relb])
        cp("dve", identb.ap, cst[:, C_I:C_I + 128], [cst], [identb])
        cp("dve", identf.ap, cst[:, C_I:C_I + 128], [cst], [identf])
        cp("pool", blkb.ap, cst[:, C_BLK:C_BLK + 128], [cst], [blkb])
        cp("pool", ind2b.ap, cst[:, C_IND:C_IND + 2], [cst], [ind2b])
        cp("pool", onesf.ap, cst[0:1, C_ONE:C_ONE + 128], [cst], [onesf])
        add("pool", lambda e: e.memset(carry.ap, 0.0), writes=bs(carry))
        add("pool", lambda e: e.memset(zerob.ap, 0.0), writes=bs(zerob))
        ts("dve", pvec[:, V_OMKA:V_OMKA + 4], pvec[:, V_KA:V_KA + 4], -1.0, 1.0, ALU.mult, ALU.add, [pvec], [pvec])
        add("pool", lambda e: e.memset(N0f.ap, 0.0), writes=bs(N0f))
        add("pool", lambda e: e.memset(N0b[0].ap, 0.0), writes=bs(N0b[0]))
        add("pool", lambda e: e.memset(qTz.ap, 0.0), writes=bs(qTz))
        dma(f1[0:64, :], wlora_d, writes=[f1])
        dma(f1[64:128, :], alora_d, writes=[f1])
        cp("dve", wlora[0:64, :], f1[0:64, :], [f1], [wlora])
        cp("dve", alora[64:128, :], f1[64:128, :], [f1], [alora])

        xbf = xb.ap.rearrange("p a b -> p (a b)")
        hbf = hb.ap.rearrange("p a b -> p (a b)")
        NSL = TPS
        xslot = [Buf("xs%d" % i) for i in range(NSL)]
        hslot = [Buf("hs%d" % i) for i in range(NSL)]
        pieces = []
        wstores = []
        for kc in range(8):
            for c0 in range(0, INW, 1024):
                pieces.append((kc, c0, min(1024, INW - c0)))
        for i, (kc, c0, w) in enumerate(pieces):
            sl = i % NSL
            dma(xbf[:, sl * 1024:sl * 1024 + w], wperm_d[kc * 128:(kc + 1) * 128, c0:c0 + w], writes=[xslot[sl]])
            e = ["act", "dve", "pool"][i % 3]
            cp(e, hbf[:, sl * 1024:sl * 1024 + w], xbf[:, sl * 1024:sl * 1024 + w], [xslot[sl]], [hslot[sl]])
            wstores.append(dma(wbf_d[kc * 128:(kc + 1) * 128, c0:c0 + w], hbf[:, sl * 1024:sl * 1024 + w], reads=[hslot[sl]], writes=[]))
        for kc in range(8):
            sl = kc % NSL
            dma(xbf[:, sl * 1024:(sl + 1) * 1024], wout_d[kc * 128:(kc + 1) * 128, :], writes=[xslot[sl]])
            e = ["act", "dve", "pool"][kc % 3]
            cp(e, hbf[:, sl * 1024:(sl + 1) * 1024], xbf[:, sl * 1024:(sl + 1) * 1024], [xslot[sl]], [hslot[sl]])
            wstores.append(dma(wobf_d[kc * 128:(kc + 1) * 128, :], hbf[:, sl * 1024:(sl + 1) * 1024], reads=[hslot[sl]], writes=[]))

        def compute_mod(cv_ap, NR):
            dma(f2[0:NR, :], cv_ap[:, 0:512], writes=[f2])
            dma(f3[0:NR, :], cv_ap[:, 512:1024], writes=[f3])
            act(csil[0:NR, 0:512], f2[0:NR, :], AF.Silu, [f2], [csil])
            act(csil[0:NR, 512:1024], f3[0:NR, :], AF.Silu, [f3], [csil])
            pb = banks[0]
            for c in range(8):
                tr(bview(pb)[:, c * 32:c * 32 + NR], csil[0:NR, c * 128:(c + 1) * 128], [csil], [pb])
            cp("dve", csT.ap.rearrange("p a b -> p (a b)")[:, 0:256], bview(pb)[:, 0:256], [pb], [csT])
            for g in range(6):
                for kc in range(8):
                    sl = kc % NSL
                    dma(xbf[:, sl * 1024:sl * 1024 + 512], adaw_d[kc * 128:(kc + 1) * 128, g * 512:(g + 1) * 512], writes=[xslot[sl]])
                    e = ["act", "dve", "pool"][kc % 3]
                    cp(e, hbf[:, sl * 1024:sl * 1024 + 512], xbf[:, sl * 1024:sl * 1024 + 512], [xslot[sl]], [hslot[sl]])
                    mm(banks[1][0:NR, :], csT[:, kc, 0:NR], hbf[:, sl * 1024:sl * 1024 + 512], kc == 0, kc == 7,
                       [csT, hslot[sl]], [banks[1]])
                cp("dve", modsb[0:NR, g * 512:(g + 1) * 512], banks[1][0:NR, :], [banks[1]], [modsb])
            for g in range(6):
                dma(rowtmp[0:1, 0:512], adab_d[0:1, g * 512:(g + 1) * 512], writes=[rowtmp])
                mm(banks[0][0:NR, :], onesf[0:1, 0:NR], rowtmp[0:1, 0:512], True, True, [onesf, rowtmp], [banks[0]])
                tt("dve", modsb[0:NR, g * 512:(g + 1) * 512], banks[0][0:NR, :], modsb[0:NR, g * 512:(g + 1) * 512], ALU.add,
                   [banks[0], modsb], [modsb])
        compute_mod(cv_d, 1 + NS)
        mscr_w = Buf("mscr")
        dma(mscr_d, modsb[1:1 + NS, :], reads=[modsb], writes=[mscr_w])
        def bcast_row(dst, src_row_ap, srcbufs, bank):
            for hlf in range(2):
                mm(bank[:, :], onesf[0:1, :], src_row_ap[:, hlf * 512:(hlf + 1) * 512], True, True, [onesf] + srcbufs, [bank])
                cp("dve", dst[:, hlf * 512:(hlf + 1) * 512], bank[:, :], [bank], [dst])
        bcast_row(SHb, modsb[0:1, 0:D], [modsb], banks[0])
        bcast_row(Gb, modsb[0:1, D:2 * D], [modsb], banks[1])
        bcast_row(GATEb, modsb[0:1, 2 * D:3 * D], [modsb], banks[0])
        dma(rowtmp.ap, normw_d, writes=[rowtmp])
        bcast_row(tmpf, rowtmp.ap, [rowtmp], banks[1])
        stt(Gb.ap, Gb.ap, 1.0, tmpf.ap, ALU.add, ALU.mult, [Gb, tmpf], [Gb])
        dma(rowtmp.ap, fnw_d, writes=[rowtmp])
        bcast_row(FNWb, rowtmp.ap, [rowtmp], banks[0])
        dma(rowtmp.ap, lnwb_d, writes=[rowtmp])
        mm(banks[1][:, :], onesf[0:1, :], rowtmp[0:1, 0:512], True, True, [onesf, rowtmp], [banks[1]])
        cp("dve", LNWb.ap, banks[1][:, :], [banks[1]], [LNWb])
        mm(banks[0][:, :], onesf[0:1, :], rowtmp[0:1, 512:1024], True, True, [onesf, rowtmp], [banks[0]])
        cp("dve", LNBb.ap, banks[0][:, :], [banks[0]], [LNBb])

        if dbg == 'nostrip':
            pass
        for pc in range(0, FLEN, 512):
            w = min(512, FLEN - pc)
            dma(f1[0:32, 0:w], oh_d[:, pc:pc + w], writes=[f1])
            dma(f2[0:8, 0:w], fc_d[:, pc:pc + w], writes=[f2])
            mm(banks[1][0:8, 0:w], relb[0:32, :], f1[0:32, 0:w], True, True, [relb, f1], [banks[1]])
            tt("dve", frow[0:8, pc:pc + w], banks[1][0:8, 0:w], f2[0:8, 0:w], ALU.add, [banks[1], f2], [frow])
        fs_w = Buf("fs_w")
        dma(fscr_d, frow.ap, reads=[frow], writes=[fs_w])
        for h in range(NH):
            dma(fscr2_d[h * 128:(h + 1) * 128, :], AP(fscr_d.tensor, h * FLEN, [[0, 128], [1, FLEN]]), reads=[fs_w], writes=[fs_w])
        pi = 0
        for h in range(NH):
            for q4 in range(4):
                src = AP(fscr2_d.tensor, h * 128 * FLEN + 127 + q4 * 544, [[FLEN - 1, 128], [1, 544]])
                sl = pi % NSL
                pi += 1
                stg = xbf[:, sl * 1024:sl * 1024 + 544]
                dma(stg, src, reads=[fs_w], writes=[xslot[sl]])
                cp(["act", "dve"][pi % 2], strips[:, h, q4 * 544:(q4 + 1) * 544], stg, [xslot[sl]], [strips])
        add("dve", lambda e: e.memset(stat[:, 60:61], 0.0), reads=xslot + hslot, writes=bs(xb, hb, stat))

        add("pool", lambda e: e.memset(kTr.ap, 0.0), writes=bs(kTr))
        add("pool", lambda e: e.memset(Vr.ap, 0.0), writes=bs(Vr))
        add("pool", lambda e: e.memset(Vr[:, :, :, 64:65], 1.0), reads=bs(Vr), writes=bs(Vr))
        bank_rr = [0]

        def nb(lst):
            bank_rr[0] += 1
            return lst[bank_rr[0] % len(lst)]

        GRP = [("q", 0, 512), ("k", 512, 512), ("gf", 2560, 128)] + [("gr%d" % c, 2688 + 384 * c, 384) for c in range(4)] + \
              [("v", 1024, 512), ("ga", 1536, 512), ("gw", 2048, 512)]
        wsti = [0]
        PB = [banks[0], banks[1]]

        def load_w(c0, w):
            wsti[0] += 1
            wt = wst[wsti[0] % 2]
            op = dma(wt[:, :, 0:w], wbf_d[:, c0:c0 + w].rearrange("(kc p) c -> p kc c", p=128), reads=[], writes=[wt])
            op.deps.extend(wstores)
            return wt

        def proj_fm(wt, cc):
            bk = nb(PB)
            for kc in range(8):
                mm(bk[:, 0:SBW], wt[:, kc, cc * 128:(cc + 1) * 128], hT[:, kc, :], kc == 0, kc == 7, [wt, hT], [bk])
            return bk

        n0i = [0]
        for sbi in range(NSB if dbg != 'setup' else 0):
            t0 = sbi * SBW
            last = sbi == NSB - 1
            def norm_gen(nsb):
                for j in range(TPS):
                    act(tmpf.ap, xb[:, j, :], AF.Square, [xb], [tmpf, stat], accum=stat[:, j:j + 1])
                    yield
                ts("dve", stat[:, 4:8], stat[:, 0:4], 1.0 / D, NORM_EPS, ALU.mult, ALU.add, [stat], [stat])
                act(stat[:, 4:8], stat[:, 4:8], AF.Sqrt, [stat], [stat])
                add("dve", lambda e: e.reciprocal(out=stat[:, 8:12], in_=stat[:, 4:8]), reads=bs(stat), writes=bs(stat))
                yield
                for j in range(TPS):
                    stt(tmpf.ap, xb[:, j, :], stat[:, 8 + j:9 + j], Gb.ap, ALU.mult, ALU.mult, [xb, stat, Gb], [tmpf])
                    tt("pool", hb[:, j, :], tmpf.ap, SHb.ap, ALU.add, [tmpf, SHb], [hb])
                    yield
                if nsb + 1 < NSB:
                    dma(xb.ap, x_d[(nsb + 1) * SBW:(nsb + 2) * SBW, :].rearrange("(j p) d -> p j d", p=128), writes=[xb])
                yield

            def hT_transposes():
                for j in range(TPS):
                    bk = nb(PB)
                    for c in range(8):
                        tr(bview(bk)[:, c * 128:(c + 1) * 128], hb[:, j, c * 128:(c + 1) * 128], [hb], [bk])
                    cp("act", hT[:, :, j * 128:(j + 1) * 128], bview(bk).rearrange("p (c t) -> p c t", c=8), [bk], [hT])
            if sbi == 0:
                dma(xb.ap, x_d[t0:t0 + SBW, :].rearrange("(j p) d -> p j d", p=128), writes=[xb])
                for _ in norm_gen(0):
                    pass
                hT_transposes()
            slot = (sbi * TPS) % NRING
            def proj_gen(GL):
                for (gname, c0, w) in GL:
                    if dbg == 'norm':
                        break
                    if dbg == 'pqk' and gname not in ('q', 'k'):
                        continue
                    if dbg == 'pgf' and gname not in ('q', 'k', 'gf'):
                        continue
                    if dbg == 'ptm' and gname not in ('q', 'k', 'v', 'ga', 'gw'):
                        continue
                    if dbg == 'ptmv' and gname not in ('q', 'k', 'v'):
                        continue
                    if dbg == 'ptmg' and gname not in ('q', 'k', 'ga'):
                        continue
                    wt = load_w(c0, w)
                    if gname == "q":
                        for cc in range(4):
                            bk = proj_fm(wt, cc)
                            act(qTz[0:64, 2 * cc, :], bk[0:64, 0:SBW], AF.Copy, [bk], [qTz], scale=0.125)
                            act(qTz[64:128, 2 * cc + 1, :], bk[64:128, 0:SBW], AF.Copy, [bk], [qTz], scale=0.125)
                    elif gname == "k":
                        for cc in range(4):
                            bk = proj_fm(wt, cc)
                            cp("act", kTr[:, cc, slot * 128:slot * 128 + SBW], bk[:, 0:SBW], [bk], [kTr])
                        if t0 + SBW > TP - KEEP:
                            for j in range(TPS):
                                bk = nb(PB)
                                for kc in range(8):
                                    mm(bk[:, :], hT[:, kc, j * 128:(j + 1) * 128], wt[:, kc, :], kc == 0, kc == 7, [wt, hT], [bk])
                                ko = nb(kvo)
                                cp("dve", ko.ap, bk[:, :], [bk], [ko])
                                r0 = t0 + j * 128 - (TP - KEEP)
                                dma(wk_d[r0:r0 + 128, :], ko.ap, reads=[ko])
                    elif gname in ("v", "ga", "gw"):
                        for j in range(TPS):
                            bk = nb(PB)
                            for kc in range(8):
                                mm(bk[:, :], hT[:, kc, j * 128:(j + 1) * 128], wt[:, kc, :], kc == 0, kc == 7, [wt, hT], [bk])
                            if gname == "v":
                                rs_ = (sbi * TPS + j) % NRING
                                cp("dve", Vr[:, rs_, :, 0:64], bk[:, :].rearrange("p (h d) -> p h d", h=NH), [bk], [Vr])
                                if t0 + SBW > TP - KEEP:
                                    ko = nb(kvo)
                                    cp("dve", ko.ap, bk[:, :], [bk], [ko])
                                    r0 = t0 + j * 128 - (TP - KEEP)
                                    dma(wv_d[r0:r0 + 128, :], ko.ap, reads=[ko])
                            elif gname == "ga":
                                act(gat[:, j, :], bk[:, :], AF.Silu, [bk], [gat])
                            else:
                                act(grw[:, j, :], bk[:, :], AF.Silu, [bk], [grw])
                    elif gname == "gf":
                        bk = proj_fm(wt, 0)
                        _tokshift(add, act, tt, stt, cp, bs, bk, 0, um[0], uraw, carry, ulast, dtmp, pvec, V_MU, last)
                        act(txw[0:64, :], um[0][0:64, :], AF.Tanh, [um[0]], [txw])
                        cp("pool", txw[64:128, :], um[0][64:128, :], [um[0]], [txw])
                    else:
                        c = int(gname[2])
                        for q3 in range(3):
                            bk = proj_fm(wt, q3)
                            _tokshift(add, act, tt, stt, cp, bs, bk, 1 + 3 * c + q3, um[q3], uraw, carry, ulast, dtmp, pvec, V_MU, last)
                            yield
                        umr, umk, umv = um
                        W_ = slice(0, SBW)
                        g1, g2, g3 = f1[:, W_], f2[:, W_], f3[:, W_]
                        bw = banks[6]
                        ba = banks[7]
                        mm(bw[:, W_], wlora[0:64, c * 128:(c + 1) * 128], txw[0:64, :], True, True, [wlora, txw], [bw])
                        mm(ba[:, W_], alora[64:128, c * 128:(c + 1) * 128], txw[64:128, :], True, True, [alora, txw], [ba])
                        act(g1, bw[:, W_], AF.Sigmoid, [bw, pvec], [f1], bias=pvec[:, V_W0 + c:V_W0 + c + 1])
                        act(g2, ba[:, W_], AF.Sigmoid, [ba, pvec], [f2], bias=pvec[:, V_A0 + c:V_A0 + c + 1])
                        add("dve", lambda e, g1=g1, g3=g3: e.tensor_tensor_scan(out=g3, data0=cst[:, C_SEG:C_SEG + SBW], data1=g1, initial=0.0,
                                                                   op0=ALU.mult, op1=ALU.add), reads=bs(cst, f1), writes=bs(f3))
                        tt("pool", f4.ap, g3, g1, ALU.subtract, [f3, f1], [f4])
                        act(f5.ap, g3, AF.Exp, [f3], [f5], scale=-C0)
                        act(f6.ap, g3, AF.Exp, [f3], [f6], scale=C0)
                        act(f4.ap, f4.ap, AF.Exp, [f4], [f4], scale=-C0)
                        act(WC[:, c, 0:TPS], AP(f3.ap.tensor, 127, [[512, 128], [128, TPS]]), AF.Exp, [f3], [WC], scale=-C0)
                        yield
                        tt("pool", rhT[:, c, :], umr.ap, f5.ap, ALU.mult, [umr, f5], [rhT])
                        act(dtmp.ap, umk.ap, AF.Square, [umk, pvec], [dtmp], scale=pvec[:, V_KK + c:V_KK + c + 1])
                        bn_ = banks[6]
                        mm(bn_[:, W_], blkb.ap, dtmp.ap, True, True, [blkb, dtmp], [bn_])
                        ts("dve", f7.ap, bn_[:, W_], 1e-24, None, ALU.add, ALU.bypass, [bn_], [f7])
                        act(f7.ap, f7.ap, AF.Sqrt, [f7], [f7])
                        add("dve", lambda e: e.reciprocal(out=f7.ap, in_=f7.ap), reads=bs(f7), writes=bs(f7))
                        stt(f8.ap, umk.ap, pvec[:, V_KK + c:V_KK + c + 1], f7.ap, ALU.mult, ALU.mult, [umk, pvec, f7], [f8])
                        yield
                        tt("pool", alT[:, c, :], f8.ap, f4.ap, ALU.mult, [f8, f4], [alT])
                        tt("dve", f8.ap, f8.ap, g2, ALU.mult, [f8, f2], [f8])
                        stt(nbT[:, c, :], f8.ap, -1.0, f6.ap, ALU.mult, ALU.mult, [f8, f6], [nbT])
                        yield
                        ts("pool", f7.ap, g2, pvec[:, V_KA + c:V_KA + c + 1], pvec[:, V_OMKA + c:V_OMKA + c + 1], ALU.mult, ALU.add,
                           [f2, pvec], [f7])
                        tt("dve", f7.ap, f7.ap, umk.ap, ALU.mult, [f7, umk], [f7])
                        tt("pool", gaT[:, c, :], f7.ap, f6.ap, ALU.mult, [f7, f6], [gaT])
                        stt(rkT[:, c, :], umr.ap, pvec[:, V_RK + c:V_RK + c + 1], f7.ap, ALU.mult, ALU.mult, [umr, pvec, f7], [rkT])
                        yield
                        for (srcap, srcb, dstK, ee) in ((nbT[:, c, :], nbT, nbK, "dve"), (gaT[:, c, :], gaT, gaK, "dve"), (umv.ap, umv, vK, "dve")):
                            bk = nb(PB)
                            for j in range(TPS):
                                tr(bview(bk)[:, j * 128:(j + 1) * 128], srcap[:, j * 128:(j + 1) * 128], [srcb], [bk])
                            cp(ee, dstK[:, :, c * 128:(c + 1) * 128], bview(bk)[:, 0:TPS * 128].rearrange("p (j t) -> p j t", j=TPS), [bk], [dstK])
                    yield
            def rwkv_gen(j, ti):
                    tc = slice(j * 128, (j + 1) * 128)
                    mask4 = lambda c0_: AP(cst.ap.tensor, c0_, [[1280, 128], [0, 4], [1, 128]])
                    par = lambda t_, hh_: AP(t_.ap.tensor, hh_ * 128, [[1024, 128], [256, 4], [1, 128]])
                    hb128 = lambda h_: slice(h_ * 128, (h_ + 1) * 128)
                    hs64 = lambda h_: slice(h_ * 64, (h_ + 1) * 64)
                    specs = [(gaT, alT, C_MU, AagT), (gaT, rhT, C_MUI, BrgT), (nbT, rhT, C_MUI, BrbT), (nbT, alT, C_MU, PTw), (alT, nbT, C_ML, Pw)]
                    RP = [(banks[6], banks[7]), (banks[0], banks[1])]
                    for (La, Ra, mk, dst) in specs:
                        bp = nb(RP)
                        for hh in range(2):
                            for c in range(4):
                                mm(bp[hh][:, c * 128:(c + 1) * 128], La[64 * hh:64 * hh + 64, c, tc], Ra[64 * hh:64 * hh + 64, c, tc], True, True,
                                   [La, Ra], [bp[hh]])
                        yield
                        for hh in range(2):
                            tt("dve", par(dst, hh), bp[hh][:, :].rearrange("p (c s) -> p c s", c=4), mask4(mk), ALU.mult, [bp[hh], cst], [dst])
                    tt("pool", Zw.ap.rearrange("p (h s) -> p h s", h=8), PTw.ap.rearrange("p (h s) -> p h s", h=8),
                       AP(cst.ap.tensor, C_I, [[1280, 128], [0, 8], [1, 128]]), ALU.add, [PTw, cst], [Zw])
                    for lv in range(NSQ):
                        bP = nb(RP)
                        for h in range(8):
                            mm(bP[h // 4][:, hb128(h % 4)], PTw[:, hb128(h)], Pw[:, hb128(h)], True, True, [PTw, Pw], [bP[h // 4]])
                        if lv < NSQ - 1:
                            bT = nb(RP)
                            for h in range(8):
                                mm(bT[h // 4][:, hb128(h % 4)], Pw[:, hb128(h)], PTw[:, hb128(h)], True, True, [PTw, Pw], [bT[h // 4]])
                        yield
                        for q2 in range(2):
                            cp("act", Pw[:, q2 * 512:(q2 + 1) * 512], bP[q2][:, :], [bP[q2]], [Pw])
                        if lv < NSQ - 1:
                            for q2 in range(2):
                                cp("dve", PTw[:, q2 * 512:(q2 + 1) * 512], bT[q2][:, :], [bT[q2]], [PTw])
                        bZ = nb(RP)
                        for h in range(8):
                            mm(bZ[h // 4][:, hb128(h % 4)], Pw[:, hb128(h)], Zw[:, hb128(h)], True, True, [Pw, Zw], [bZ[h // 4]])
                        for q2 in range(2):
                            tt("dve", Zw[:, q2 * 512:(q2 + 1) * 512], bZ[q2][:, :], Zw[:, q2 * 512:(q2 + 1) * 512], ALU.add, [bZ[q2], Zw], [Zw])
                        yield
                    No, Nn = N0b[n0i[0] % 2], N0b[(n0i[0] + 1) % 2]
                    n0i[0] += 1
                    bR = banks[6]
                    mm(bR[:, :], zerob.ap, identb.ap.to_broadcast([128, 128]) if False else strips[:, 0, 0:512], True, False, [zerob, strips], [bR])
                    for c in range(4):
                        mm(bR[:, hb128(c)], alT[:, c, tc], No[:, hb128(c)], False, False, [alT, No], [bR])
                    for h in range(8):
                        mm(bR[:, hs64(h)], AagT[:, hb128(h)], vK[:, j, hs64(h)], False, True, [AagT, vK], [bR])
                    cp("act", Rb.ap, bR[:, :], [bR], [Rb])
                    yield
                    bU = banks[7]
                    for h in range(8):
                        mm(bU[:, hs64(h)], Zw[:, hb128(h)], Rb[:, hs64(h)], True, True, [Zw, Rb], [bU])
                    cp("act", Ub.ap, bU[:, :], [bU], [Ub])
                    yield
                    bY = banks[1]
                    mm(bY[:, :], zerob.ap, strips[:, 0, 0:512], True, False, [zerob, strips], [bY])
                    for c in range(4):
                        mm(bY[:, hb128(c)], rhT[:, c, tc], No[:, hb128(c)], False, False, [rhT, No], [bY])
                    for h in range(8):
                        mm(bY[:, hs64(h)], BrbT[:, hb128(h)], Ub[:, hs64(h)], False, False, [BrbT, Ub], [bY])
                        mm(bY[:, hs64(h)], BrgT[:, hb128(h)], vK[:, j, hs64(h)], False, True, [BrgT, vK], [bY])
                    bN = banks[6]
                    mm(bN[:, :], identf.ap, N0f.ap, True, False, [identf, N0f], [bN])
                    for c in range(4):
                        mm(bN[:, hb128(c)], nbK[:, j, hb128(c)], Ub[:, hb128(c)], False, False, [nbK, Ub], [bN])
                        mm(bN[:, hb128(c)], gaK[:, j, hb128(c)], vK[:, j, hb128(c)], False, True, [gaK, vK], [bN])
                    tt("dve", f1.ap.rearrange("p (c q) -> p c q", c=4), bN[:, :].rearrange("p (c q) -> p c q", c=4),
                       AP(WC.ap.tensor, j, [[4 * TPS, 128], [TPS, 4], [0, 128]]), ALU.mult, [bN, WC], [f1])
                    tt("pool", N0f.ap.rearrange("p (c q) -> p c q", c=4), f1.ap.rearrange("p (c q) -> p c q", c=4), mask4(C_BLK), ALU.mult,
                       [f1, cst], [N0f])
                    cp("act", Nn.ap, N0f.ap, [N0f], [Nn])
                    yield
                    Y3 = bY[:, :].rearrange("p (h i) -> p h i", h=8)
                    add("dve", lambda e: e.tensor_reduce(out=stat[:, 16:24], in_=Y3, axis=AX.X, op=ALU.add), reads=bs(bY), writes=bs(stat))
                    act(f1.ap, bY[:, :], AF.Square, [bY], [f1])
                    add("dve", lambda e: e.tensor_reduce(out=stat[:, 24:32], in_=f1.ap.rearrange("p (h i) -> p h i", h=8), axis=AX.X, op=ALU.add),
                        reads=bs(f1), writes=bs(stat))
                    ts("dve", stat[:, 16:24], stat[:, 16:24], 1.0 / HD, None, ALU.mult, ALU.bypass, [stat], [stat])
                    tt("dve", stat[:, 32:40], stat[:, 16:24], stat[:, 16:24], ALU.mult, [stat], [stat])
                    stt(stat[:, 24:32], stat[:, 24:32], 1.0 / HD, stat[:, 32:40], ALU.mult, ALU.subtract, [stat], [stat])
                    ts("dve", stat[:, 24:32], stat[:, 24:32], GN_EPS, None, ALU.add, ALU.bypass, [stat], [stat])
                    act(stat[:, 24:32], stat[:, 24:32], AF.Sqrt, [stat], [stat])
                    add("dve", lambda e: e.reciprocal(out=stat[:, 24:32], in_=stat[:, 24:32]), reads=bs(stat), writes=bs(stat))
                    b8 = lambda lo: stat[:, lo:lo + 8].unsqueeze(2).to_broadcast([128, 8, 64])
                    f2_3 = f2.ap.rearrange("p (h i) -> p h i", h=8)
                    tt("dve", f2_3, Y3, b8(16), ALU.subtract, [bY, stat], [f2])
                    tt("pool", f2_3, f2_3, b8(24), ALU.mult, [f2, stat], [f2])
                    tt("dve", f2.ap, f2.ap, LNWb.ap, ALU.mult, [f2, LNWb], [f2])
                    tt("pool", f2.ap, f2.ap, LNBb.ap, ALU.add, [f2, LNBb], [f2])
                    bB = banks[7]
                    for c in range(4):
                        mm(bB[:, c * 2:c * 2 + 2], rkT[:, c, j * 128:(j + 1) * 128], ind2b.ap, True, True, [rkT, ind2b], [bB])
                    cp("act", stat[:, 40:48], bB[:, 0:8], [bB], [stat])
                    f3_3 = f3.ap.rearrange("p (h i) -> p h i", h=8)
                    tt("dve", f3_3, vK[:, j, :].rearrange("p (h i) -> p h i", h=8), b8(40), ALU.mult, [vK, stat], [f3])
                    tt("pool", f2.ap, f2.ap, f3.ap, ALU.add, [f2, f3], [f2])
                    tt("dve", mixed[:, j, 512:1024], f2.ap, grw[:, j, :], ALU.mult, [f2, grw], [mixed])
                    yield
            def attn_gen(j, ti):
                    bN0, bN1 = banks[4], banks[5]
                    SB_ = [banks[2], banks[3]]
                    kts = [kt for kt in range(ti - 16, ti + 1) if kt >= 0]
                    mm(bN0[:, 0:260], zerob.ap, strips[:, 0, 0:260], True, False, [zerob, strips], [bN0])
                    mm(bN1[:, 0:260], zerob.ap, strips[:, 0, 0:260], True, False, [zerob, strips], [bN1])
                    units = [(ki, kt, half) for ki, kt in enumerate(kts) for half in range(2)]

                    def emit_pv(u_, ex_):
                        ki_, kt_, half_ = u_
                        bNh_ = bN0 if half_ == 0 else bN1
                        for hq in range(4):
                            h = half_ * 4 + hq
                            mm(bNh_[:, hq * 65:(hq + 1) * 65], ex_[:, hq, :], Vr[:, kt_ % NRING, h, 0:65], False, ki_ == len(kts) - 1, [ex_, Vr], [bNh_])
                    prev = None
                    for idx, (ki, kt, half) in enumerate(units):
                        o = ti - kt
                        rs_ = kt % NRING
                        ksl = slice(rs_ * 128, (rs_ + 1) * 128)
                        bk = SB_[idx % 2]
                        ex = expT[idx % 2]
                        mm(bk[:, :].rearrange("p (h q) -> p h q", h=4), identb.ap, strips[:, half * 4:half * 4 + 4, o * 128:(o + 1) * 128],
                           True, False, [identb, strips], [bk])
                        for cc in range(2):
                            c = half * 2 + cc
                            mm(bk[:, cc * 256:(cc + 1) * 256].rearrange("p (h q) -> p h q", h=2), kTr[:, c, ksl],
                               qTz[:, 2 * c:2 * c + 2, j * 128:(j + 1) * 128], False, cc == 1, [kTr, qTz], [bk])
                        act(ex[:, 0:4, :], bk[:, :].rearrange("p (h q) -> p h q", h=4), AF.Exp, [bk], [ex])
                        if prev is not None:
                            emit_pv(*prev)
                            yield
                        prev = ((ki, kt, half), ex)
                    emit_pv(*prev)
                    yield
                    for half in range(2):
                        bNh = bN0 if half == 0 else bN1
                        N3 = bNh[:, 0:260].rearrange("p (h e) -> p h e", h=4)
                        add("dve", lambda e, N3=N3, half=half: e.reciprocal(out=stat[:, 48 + half * 4:52 + half * 4].unsqueeze(2), in_=N3[:, :, 64:65]),
                            reads=bs(bNh), writes=bs(stat))
                        tt("dve", mixed[:, j, half * 256:(half + 1) * 256].rearrange("p (h d) -> p h d", h=4), N3[:, :, 0:64],
                           stat[:, 48 + half * 4:52 + half * 4].unsqueeze(2).to_broadcast([128, 4, 64]), ALU.mult, [bNh, stat], [mixed])
                    tt("pool", mixed[:, j, 0:512], mixed[:, j, 0:512], gat[:, j, :], ALU.mult, [mixed, gat], [mixed])
                    yield
            def interleave(ga_, gb_):
                a1 = a2 = True
                while a1 or a2:
                    if a1:
                        a1 = next(ga_, "end") != "end"
                    if a2:
                        a2 = next(gb_, "end") != "end"

            def mix_transposes(j):
                bk = nb(PB)
                for c in range(8):
                    tr(bview(bk)[:, c * 128:(c + 1) * 128], mixed[:, j, c * 128:(c + 1) * 128], [mixed], [bk])
                cp("act", mixT[:, j, :, :].rearrange("p c t -> p (c t)"), bview(bk)[:, :], [bk], [mixT])
            assert TPS == 2
            ti0 = sbi * TPS
            for _ in proj_gen(GRP[0:2] + GRP[7:10]):
                pass
            interleave(proj_gen(GRP[2:7]), attn_gen(0, ti0))
            interleave(rwkv_gen(0, ti0), attn_gen(1, ti0 + 1))
            mix_transposes(0)
            if sbi + 1 < NSB:
                interleave(rwkv_gen(1, ti0 + 1), norm_gen(sbi + 1))
            else:
                for _ in rwkv_gen(1, ti0 + 1):
                    pass
            mix_transposes(1)
            if dbg in ('proj', 'chain', 'sweep', 'ypost', 'attn', 'norm', 'pqk', 'pgf', 'ptm', 'ptmv', 'ptmg', 'mats', 'mats1', 'mats2'):
                continue
            wo = []
            for hf in range(2):
                wt = wst[hf]
                op = dma(wt.ap, wobf_d[:, hf * 512:(hf + 1) * 512].rearrange("(kc p) c -> p kc c", p=128), reads=[], writes=[wt])
                op.deps.extend(wstores)
                wo.append(wt)
            for j in range(TPS):
                bO = [banks[6], banks[7]]
                dma(yo[0].ap, x_d[t0 + j * 128:t0 + (j + 1) * 128, :], writes=[yo[0]])
                for hf in range(2):
                    for fc in range(8):
                        mm(bO[hf][:, :], mixT[:, j, fc, :], wo[hf][:, fc, :], fc == 0, fc == 7, [mixT, wo[hf]], [bO[hf]])
                for hf in range(2):
                    hs = slice(hf * 512, (hf + 1) * 512)
                    tt("dve", xo[:, hs], bO[hf][:, :], GATEb[:, hs], ALU.mult, [bO[hf], GATEb], [xo])
                tt("pool", xo.ap, xo.ap, yo[0].ap, ALU.add, [xo, yo[0]], [xo])
                act(mixT[:, j, :, :].rearrange("p c t -> p (c t)"), xo.ap, AF.Square, [xo], [mixT, stat], accum=stat[:, 56:57])
                ts("dve", stat[:, 57:58], stat[:, 56:57], 1.0 / D, NORM_EPS, ALU.mult, ALU.add, [stat], [stat])
                act(stat[:, 57:58], stat[:, 57:58], AF.Sqrt, [stat], [stat])
                add("dve", lambda e: e.reciprocal(out=stat[:, 58:59], in_=stat[:, 57:58]), reads=bs(stat), writes=bs(stat))
                yy = yo[0]
                stt(yy.ap, xo.ap, stat[:, 58:59], FNWb.ap, ALU.mult, ALU.mult, [xo, stat, FNWb], [yy])
                dma(y_d[t0 + j * 128:t0 + (j + 1) * 128, :], yy.ap, reads=[yy])
            if sbi + 1 < NSB:
                hT_transposes()
        if dbg != 'nosample':
            P = slice(0, NS)
            add("dve", lambda e: e.memset(stat[:, 61:62], 0.0), reads=[], writes=xslot + hslot + bs(xb, hb, stat))
            dma(modsb[0:NS, :], mscr_d, reads=[mscr_w], writes=[modsb])
            ar = strips.ap.rearrange("p a b -> p (a b)").bitcast(F32)
            Sst = View(ar[:, 0:4096], strips.b)
            TMP = View(ar[:, 4096:8192], strips.b)
            vec6 = View(ar[:, 8192:8576], strips.b)
            vrf = Vr.ap.rearrange("p a b c -> p (a b c)")
            zs = View(AP(vrf.tensor, 0, [[NRING * NH * 66, 128], [1, 2 * 4224]]).bitcast(F32)[0:NS, :], Vr.b)
            xs = View(yo[0][0:NS, :], yo[0].b)
            hs = View(mixed[0:NS, 0, :], mixed.b)
            dma(xs.ap, xs_d, writes=[xs])
            act(tmpf[P, :], xs.ap, AF.Square, [xs], [tmpf, stat], accum=stat[P, 0:1])
            ts("dve", stat[P, 1:2], stat[P, 0:1], 1.0 / D, NORM_EPS, ALU.mult, ALU.add, [stat], [stat])
            act(stat[P, 1:2], stat[P, 1:2], AF.Sqrt, [stat], [stat])
            add("dve", lambda e: e.reciprocal(out=stat[P, 2:3], in_=stat[P, 1:2]), reads=bs(stat), writes=bs(stat))
            dma(f1[0:1, :], normw_d[0:1, 0:512], writes=[f1])
            dma(f2[0:1, :], normw_d[0:1, 512:1024], writes=[f2])
            for hlf, ft in enumerate((f1, f2)):
                mm(banks[0][P, :], onesf[0:1, 0:NS], ft[0:1, :], True, True, [onesf, ft], [banks[0]])
                stt(Gb[P, hlf * 512:(hlf + 1) * 512], modsb[P, D + hlf * 512:D + (hlf + 1) * 512], 1.0, banks[0][P, :], ALU.add, ALU.mult,
                    [modsb, banks[0]], [Gb])
            stt(tmpf[P, :], xs.ap, stat[P, 2:3], Gb[P, :], ALU.mult, ALU.mult, [xs, stat, Gb], [tmpf])
            tt("pool", hs.ap, tmpf[P, :], modsb[P, 0:D], ALU.add, [tmpf, modsb], [hs])
            bk = banks[1]
            for c in range(8):
                tr(bview(bk)[:, c * 16:(c + 1) * 16], hs[:, c * 128:(c + 1) * 128], [hs], [bk])
            cp("dve", hsT.ap.rearrange("p c t -> p (c t)"), bview(bk)[:, 0:128], [bk], [hsT])
            for gi, c0 in enumerate(range(0, INW, 512)):
                w = min(512, INW - c0)
                wt = load_w(c0, w)
                bk = PB[gi % 2]
                for kc in range(8):
                    mm(bk[P, 0:w], hsT[:, kc, :], wt[:, kc, 0:w], kc == 0, kc == 7, [hsT, wt], [bk])
                cp("dve", zs[:, c0:c0 + w], bk[P, 0:w], [bk], [zs])
            dma(ks_d, zs[:, 512:1024], reads=[zs])
            dma(vs_d, zs[:, 1024:1536], reads=[zs])
            dma(shs_d, zs[:, 2560:4224], reads=[zs])
            qw = Buf("qscr")
            dma(qscr_d, zs[:, 0:512], reads=[zs], writes=[qw])
            dma(f3[0:8, :], scst_d[:, 0:512], writes=[f3])
            dma(f2[0:8, :], scst_d[:, 512:1024], writes=[f2])
            dma(f1[0:32, 0:384], ohs_d, writes=[f1])
            bS = View(ar[:, 8576:8600], strips.b)
            for br in range(3):
                mm(banks[2][:, br * 8:(br + 1) * 8], f1[0:32, br * 128:(br + 1) * 128], relb[0:32, :], True, True, [f1, relb], [banks[2]])
            cp("dve", bS.ap, banks[2][:, 0:24], [banks[2]], [bS])
            f32v = lambda t_, pat: View(t_.ap.rearrange(pat).bitcast(F32), t_.b)
            kst = [f32v(alT, "p a b -> p (a b)"), f32v(nbT, "p a b -> p (a b)")]
            vst = [f32v(gaT, "p a b -> p (a b)"), f32v(rhT, "p a b -> p (a b)"), f32v(rkT, "p a b -> p (a b)")]
            qb_ = f32v(nbK, "p a b -> p (a b)")
            gk32 = gaK.ap.rearrange("p a b -> p (a b)").bitcast(F32)
            lraw = View(gk32[:, 0:24], gaK.b)
            esb = View(gk32[:, 32:56], gaK.b)
            vk32 = vK.ap.rearrange("p a b -> p (a b)").bitcast(F32)
            m1 = View(vk32[0:8, :], vK.b)
            m2 = View(gk32[0:8, 64:80], gaK.b)
            bATT, bDEN, bP1, bP2 = banks[4], banks[5], banks[6], banks[7]
            mm(bATT[P, :], zerob[:, 0:NS], wst[0][:, 0, :], True, False, [zerob, wst[0]], [bATT])
            mm(bDEN[P, 0:8], zerob[:, 0:NS], wst[0][:, 0, 0:8], True, False, [zerob, wst[0]], [bDEN])
            dist = [1, 4, 16]
            for s_ in range(NS):
                dma(qb_.ap, AP(qscr_d.tensor, s_ * 512, [[0, 128], [1, 512]]), reads=[qw], writes=[qb_])
                for br in range(3):
                    Dd = dist[br]
                    off = (s_ * WIN + (WIN - 128 * Dd)) * 512
                    kk_ = kst[br % 2]
                    vv_ = vst[br]
                    dma(kk_.ap, AP(ck_d.tensor, off, [[Dd * 512, 128], [1, 512]]), writes=[kk_])
                    dma(vv_.ap, AP(cvv_d.tensor, off, [[Dd * 512, 128], [1, 512]]), writes=[vv_])
                    tt("dve", kk_.ap, kk_.ap, qb_.ap, ALU.mult, [kk_, qb_], [kk_])
                    add("dve", lambda e, kk_=kk_, br=br: e.tensor_reduce(out=lraw[:, br * 8:(br + 1) * 8], in_=kk_.ap.rearrange("p (h d) -> p h d", h=8),
                                                                        axis=AX.X, op=ALU.add), reads=bs(kk_), writes=bs(lraw))
                stt(lraw.ap, lraw.ap, 0.125, bS.ap, ALU.mult, ALU.add, [lraw, bS], [lraw])
                act(esb.ap, lraw.ap, AF.Exp, [lraw], [esb])
                for br in range(3):
                    mm(bP1[0:8, :], esb[:, br * 8:(br + 1) * 8], vst[br].ap, br == 0, br == 2, [esb, vst[br]], [bP1])
                for br in range(3):
                    mm(bP2[0:8, 0:2], esb[:, br * 8:(br + 1) * 8], cst[:, C_ONE:C_ONE + 2], br == 0, br == 2, [esb, cst], [bP2])
                tt("dve", m1.ap, bP1[0:8, :], f3[0:8, :], ALU.mult, [bP1, f3], [m1])
                cp("act", m2[:, 8:9], bP2[0:8, 0:1], [bP2], [m2])
                ts("dve", m2[:, 0:8], f2[0:8, 256:264], m2[:, 8:9], None, ALU.mult, ALU.bypass, [f2, m2], [m2])
                mm(bATT[P, :], f2[0:8, s_ * 16:(s_ + 1) * 16], m1.ap, False, s_ == NS - 1, [f2, m1], [bATT])
                mm(bDEN[P, 0:8], f2[0:8, s_ * 16:(s_ + 1) * 16], m2[:, 0:8], False, s_ == NS - 1, [f2, m2], [bDEN])
            t1 = View(Pw.ap.bitcast(F32)[0:NS, :], Pw.b)
            tt("dve", t1.ap, zs[:, 0:512], zs[:, 512:1024], ALU.mult, [zs], [t1])
            add("dve", lambda e: e.tensor_reduce(out=stat[P, 8:16], in_=t1.ap.rearrange("p (h d) -> p h d", h=8), axis=AX.X, op=ALU.add),
                reads=bs(t1), writes=bs(stat))
            dma(stat[P, 16:24], AP(relb_d.tensor, 0, [[0, NS], [1, 8]]), writes=[stat])
            stt(stat[P, 8:16], stat[P, 8:16], 0.125, stat[P, 16:24], ALU.mult, ALU.add, [stat], [stat])
            act(stat[P, 8:16], stat[P, 8:16], AF.Exp, [stat], [stat])
            ts("dve", stat[P, 8:16], stat[P, 8:16], 3.0, None, ALU.mult, ALU.bypass, [stat], [stat])
            tt("dve", stat[P, 24:32], bDEN[P, 0:8], stat[P, 8:16], ALU.add, [bDEN, stat], [stat])
            add("dve", lambda e: e.reciprocal(out=stat[P, 24:32], in_=stat[P, 24:32]), reads=bs(stat), writes=bs(stat))
            t13 = t1.ap.rearrange("p (h d) -> p h d", h=8)
            tt("dve", t13, zs[:, 1024:1536].rearrange("p (h d) -> p h d", h=8), stat[P, 8:16].unsqueeze(2).to_broadcast([NS, 8, 64]), ALU.mult,
               [zs, stat], [t1])
            tt("dve", t1.ap, t1.ap, bATT[P, :], ALU.add, [t1, bATT], [t1])
            tt("dve", t13, t13, stat[P, 24:32].unsqueeze(2).to_broadcast([NS, 8, 64]), ALU.mult, [t1, stat], [t1])
            act(f4[P, :], zs[:, 1536:1792], AF.Silu, [zs], [f4])
            act(f5[P, :], zs[:, 1792:2048], AF.Silu, [zs], [f5])
            tt("dve", hs[:, 0:256], t1[:, 0:256], f4[P, :], ALU.mult, [t1, f4], [hs])
            tt("dve", hs[:, 256:512], t1[:, 256:512], f5[P, :], ALU.mult, [t1, f5], [hs])
            UO = 2560
            prv = View(ar[0:NS, 4096:4096 + SHW], strips.b)
            mub = View(ar[0:NS, 4096 + SHW:4096 + 2 * SHW], strips.b)
            dma(prv.ap, ssh_d, writes=[prv])
            dma(mub.ap, AP(muperm_d.tensor, 0, [[0, NS], [1, SHW]]), writes=[mub])
            tt("dve", prv.ap, prv.ap, zs[:, UO:UO + SHW], ALU.subtract, [prv, zs], [prv])
            tt("dve", prv.ap, prv.ap, mub.ap, ALU.mult, [prv, mub], [prv])
            tt("dve", prv.ap, prv.ap, zs[:, UO:UO + SHW], ALU.add, [prv, zs], [prv])
            u3 = prv[:, 128:SHW].rearrange("p (c q x) -> p c q x", c=4, q=3)
            r_v, k_v, v_v = u3[:, :, 0, :], u3[:, :, 1, :], u3[:, :, 2, :]
            nat = lambda t_: t_.rearrange("p (c x) -> p c x", c=4)
            hold = [Gb[P, 0:512], Gb[P, 512:1024], SHb[P, 0:512], SHb[P, 512:1024], GATEb[P, 0:512], GATEb[P, 512:1024], tmpf[P, 512:1024]]
            hbufs = [Gb, Gb, SHb, SHb, GATEb, GATEb, tmpf]
            for i_ in range(7):
                dma(hold[i_], AP(prow_d.tensor, i_ * 512, [[0, NS], [1, 512]]), writes=bs(hbufs[i_]))
            W0b, A0b, KKb, KAb, RKb, LWb, LBb = hold
            cp("dve", hs[:, 512:640], prv[:, 0:128], [prv], [hs])
            bk = banks[0]
            tr(bview(bk)[:, 0:NS], hs[:, 512:640], [hs], [bk])
            act(xxT[0:64, :], bview(bk)[0:64, 0:NS], AF.Tanh, [bk], [xxT])
            cp("dve", xxT[64:128, :], bview(bk)[64:128, 0:NS], [bk], [xxT])
            mm(banks[2][P, :], xxT[0:64, :], wlora[0:64, :], True, True, [xxT, wlora], [banks[2]])
            mm(banks[3][P, :], xxT[64:128, :], alora[64:128, :], True, True, [xxT, alora], [banks[3]])
            g1 = View(f1[P, :], f1.b); g2 = View(f2[P, :], f2.b)
            tt("dve", g1.ap, banks[2][P, :], W0b, ALU.add, [banks[2], Gb], [f1])
            act(g1.ap, g1.ap, AF.Sigmoid, [f1], [f1])
            act(g1.ap, g1.ap, AF.Exp, [f1], [f1], scale=-C0)
            tt("dve", g2.ap, banks[3][P, :], A0b, ALU.add, [banks[3], Gb], [f2])
            act(g2.ap, g2.ap, AF.Sigmoid, [f2], [f2])
            V6t = View(Sst[0:NS, 0:3072], strips.b)
            v6 = lambda q_: V6t[:, q_ * 512:(q_ + 1) * 512]
            cp("pool", v6(0), g1.ap, [f1], [V6t])
            kkt = View(f3[P, :], f3.b)
            tt("dve", nat(kkt.ap), k_v, nat(KKb), ALU.mult, [prv, SHb], [f3])
            tt("dve", t1.ap, kkt.ap, kkt.ap, ALU.mult, [f3], [t1])
            add("dve", lambda e: e.tensor_reduce(out=stat[P, 32:40], in_=t1.ap.rearrange("p (h d) -> p h d", h=8), axis=AX.X, op=ALU.add),
                reads=bs(t1), writes=bs(stat))
            ts("dve", stat[P, 32:40], stat[P, 32:40], 1e-24, None, ALU.add, ALU.bypass, [stat], [stat])
            act(stat[P, 32:40], stat[P, 32:40], AF.Sqrt, [stat], [stat])
            add("dve", lambda e: e.reciprocal(out=stat[P, 32:40], in_=stat[P, 32:40]), reads=bs(stat), writes=bs(stat))
            tt("dve", v6(1).rearrange("p (h d) -> p h d", h=8), kkt.ap.rearrange("p (h d) -> p h d", h=8),
               stat[P, 32:40].unsqueeze(2).to_broadcast([NS, 8, 64]), ALU.mult, [f3, stat], [V6t])
            tt("dve", v6(2), v6(1), g2.ap, ALU.mult, [V6t, f2], [V6t])
            tt("dve", t1.ap, g2.ap, KAb, ALU.mult, [f2, SHb], [t1])
            tt("dve", t1.ap, t1.ap, KAb, ALU.subtract, [t1, SHb], [t1])
            ts("dve", t1.ap, t1.ap, 1.0, None, ALU.add, ALU.bypass, [t1], [t1])
            tt("dve", nat(v6(3)), k_v, nat(t1.ap), ALU.mult, [prv, t1], [V6t])
            cp("dve", nat(v6(4)), r_v, [prv], [V6t])
            cp("dve", nat(v6(5)), v_v, [prv], [V6t])
            tt("dve", t1.ap, v6(4), v6(3), ALU.mult, [V6t], [t1])
            tt("dve", t1.ap, t1.ap, RKb, ALU.mult, [t1, GATEb], [t1])
            add("dve", lambda e: e.tensor_reduce(out=stat[P, 40:48], in_=t1.ap.rearrange("p (h d) -> p h d", h=8), axis=AX.X, op=ALU.add),
                reads=bs(t1), writes=bs(stat))
            tt("dve", kkt.ap.rearrange("p (h d) -> p h d", h=8), v6(5).rearrange("p (h d) -> p h d", h=8),
               stat[P, 40:48].unsqueeze(2).to_broadcast([NS, 8, 64]), ALU.mult, [V6t, stat], [f3])
            vw = Buf("vscr")
            for q_ in range(6):
                dma(vscr_d[:, :, q_, :], v6(q_).rearrange("p (h j) -> p h j", h=8), reads=[V6t], writes=[vw])
            dma(vec6.ap, vscr_d.rearrange("s h q j -> (s h) (q j)"), reads=[vw], writes=[vec6])
            dma(Sst.ap, swkv_d, reads=[V6t], writes=[Sst])
            S3 = Sst.ap.rearrange("p (i j) -> p i j", i=64)
            T3 = TMP.ap.rearrange("p (i j) -> p i j", i=64)
            vq = lambda q_: vec6[:, q_ * 64:(q_ + 1) * 64]
            rowb = lambda q_: vq(q_).unsqueeze(1).to_broadcast([128, 64, 64])
            colb = lambda ap_: ap_.unsqueeze(2).to_broadcast([128, 64, 64])
            tt("dve", T3, S3, rowb(1), ALU.mult, [Sst, vec6], [TMP])
            add("dve", lambda e: e.tensor_reduce(out=f6[:, 0:64], in_=T3, axis=AX.X, op=ALU.add),
                reads=bs(TMP), writes=bs(f6))
            tt("pool", S3, S3, rowb(0), ALU.mult, [Sst, vec6], [Sst])
            tt("dve", T3, colb(f6[:, 0:64]), rowb(2), ALU.mult, [f6, vec6], [TMP])
            tt("dve", Sst.ap, Sst.ap, TMP.ap, ALU.subtract, [Sst, TMP], [Sst])
            tt("pool", T3, colb(vq(5)), rowb(3), ALU.mult, [vec6], [TMP])
            tt("dve", Sst.ap, Sst.ap, TMP.ap, ALU.add, [Sst, TMP], [Sst])
            dma(wkvs_d, Sst.ap, reads=[Sst])
            tt("dve", T3, S3, rowb(4), ALU.mult, [Sst, vec6], [TMP])
            add("dve", lambda e: e.tensor_reduce(out=f6[:, 64:128], in_=T3, axis=AX.X, op=ALU.add), reads=bs(TMP), writes=bs(f6))
            yv = f6[:, 64:128]
            add("dve", lambda e: e.tensor_reduce(out=f6[:, 128:129], in_=yv, axis=AX.X, op=ALU.add), reads=bs(f6), writes=bs(f6))
            ts("dve", f6[:, 128:129], f6[:, 128:129], 1.0 / HD, None, ALU.mult, ALU.bypass, [f6], [f6])
            ts("dve", f6[:, 192:256], yv, f6[:, 128:129], None, ALU.subtract, ALU.bypass, [f6], [f6])
            tt("dve", f7[:, 0:64], f6[:, 192:256], f6[:, 192:256], ALU.mult, [f6], [f7])
            add("dve", lambda e: e.tensor_reduce(out=f6[:, 129:130], in_=f7[:, 0:64], axis=AX.X, op=ALU.add), reads=bs(f7), writes=bs(f6))
            ts("dve", f6[:, 129:130], f6[:, 129:130], 1.0 / HD, GN_EPS, ALU.mult, ALU.add, [f6], [f6])
            act(f6[:, 129:130], f6[:, 129:130], AF.Sqrt, [f6], [f6])
            add("dve", lambda e: e.reciprocal(out=f6[:, 130:131], in_=f6[:, 129:130]), reads=bs(f6), writes=bs(f6))
            ts("dve", f7[:, 64:128], f6[:, 192:256], f6[:, 130:131], None, ALU.mult, ALU.bypass, [f6], [f7])
            yw = Buf("yscr")
            dma(yscr_d, f7[:, 64:128], reads=[f7], writes=[yw])
            dma(t1.ap, yscr_d.rearrange("(s h) i -> s (h i)", h=8), reads=[yw], writes=[t1])
            tt("dve", t1.ap, t1.ap, LWb, ALU.mult, [t1, GATEb], [t1])
            tt("dve", t1.ap, t1.ap, LBb, ALU.add, [t1, tmpf], [t1])
            tt("dve", t1.ap, t1.ap, kkt.ap, ALU.add, [t1, f3], [t1])
            act(f4[P, :], zs[:, 2048:2304], AF.Silu, [zs], [f4])
            act(f5[P, :], zs[:, 2304:2560], AF.Silu, [zs], [f5])
            tt("dve", hs[:, 512:768], t1[:, 0:256], f4[P, :], ALU.mult, [t1, f4], [hs])
            tt("dve", hs[:, 768:1024], t1[:, 256:512], f5[P, :], ALU.mult, [t1, f5], [hs])
            bk = banks[1]
            for c in range(8):
                tr(bview(bk)[:, c * 16:(c + 1) * 16], hs[:, c * 128:(c + 1) * 128], [hs], [bk])
            cp("dve", hsT.ap.rearrange("p c t -> p (c t)"), bview(bk)[:, 0:128], [bk], [hsT])
            for hf in range(2):
                wt = wst[hf]
                op = dma(wt.ap, wobf_d[:, hf * 512:(hf + 1) * 512].rearrange("(kc p) c -> p kc c", p=128), reads=[], writes=[wt])
                op.deps.extend(wstores)
                bO_ = banks[2 + hf]
                for fc in range(8):
                    mm(bO_[P, :], hsT[:, fc, :], wt[:, fc, :], fc == 0, fc == 7, [hsT, wt], [bO_])
                hsl = slice(hf * 512, (hf + 1) * 512)
                tt("dve", tmpf[P, hsl] if hf == 0 else f1[P, :], bO_[P, :], modsb[P, 2 * D + hf * 512:2 * D + (hf + 1) * 512], ALU.mult,
                   [bO_, modsb], [tmpf if hf == 0 else f1])
            xo2 = View(Gb[P, :], Gb.b)
            tt("dve", xo2[:, 0:512], tmpf[P, 0:512], xs[:, 0:512], ALU.add, [tmpf, xs], [xo2])
            tt("dve", xo2[:, 512:1024], f1[P, :], xs[:, 512:1024], ALU.add, [f1, xs], [xo2])
            act(hb[P, 0, :], xo2.ap, AF.Square, [xo2], [hb, stat], accum=stat[P, 56:57])
            ts("dve", stat[P, 57:58], stat[P, 56:57], 1.0 / D, NORM_EPS, ALU.mult, ALU.add, [stat], [stat])
            act(stat[P, 57:58], stat[P, 57:58], AF.Sqrt, [stat], [stat])
            add("dve", lambda e: e.reciprocal(out=stat[P, 58:59], in_=stat[P, 57:58]), reads=bs(stat), writes=bs(stat))
            stt(xo2.ap, xo2.ap, stat[P, 58:59], FNWb[P, :], ALU.mult, ALU.mult, [xo2, stat, FNWb], [xo2])
            dma(ys_d, xo2.ap, reads=[xo2])
        bk = banks[0]
        for c in range(4):
            mm(bk[:, c * 128:(c + 1) * 128], N0f[:, c * 128:(c + 1) * 128], identf.ap, True, True, [N0f, identf], [bk])
        cp("dve", f1.ap, bk[:, :], [bk], [f1])
        for h in range(NH):
            c, hh = h // 2, h % 2
            dma(wkv_d[h], f1[64 * hh:64 * hh + 64, c * 128 + 64 * hh:c * 128 + 64 * hh + 64], reads=[f1])
        for ch in range(13):
            dma(shp_d[ch:ch + 1, :].rearrange("o p -> p o"), ulast[:, ch:ch + 1], reads=[ulast])

        with nc.Block() as block:
            S.emit(block)
    return nc


def _tokshift(add, act, tt, stt, cp, bs, bk, ch, dst, uraw, carry, ulast, dtmp, pvec, V_MU, last):
    cp("pool", uraw[:, 0:1], carry[:, ch:ch + 1], [carry], [uraw])
    act(uraw[:, 1:SBW + 1], bk[:, 0:SBW], AF.Copy, [bk], [uraw])
    if last:
        cp("dve", ulast[:, ch:ch + 1], bk[:, SBW - 1:SBW], [bk], [ulast])
    cp("pool", carry[:, ch:ch + 1], uraw[:, SBW:SBW + 1], [uraw], [carry])
    tt("pool", dtmp.ap, uraw[:, 0:SBW], uraw[:, 1:SBW + 1], ALU.subtract, [uraw], [dtmp])
    stt(dst.ap, dtmp.ap, pvec[:, V_MU + ch:V_MU + ch + 1], uraw[:, 1:SBW + 1], ALU.mult, ALU.add, [dtmp, pvec, uraw], [dst])


def _t5_bucket_np(dist):
    dist = np.asarray(dist, np.int32)
    nf = np.maximum(dist, 16).astype(np.float32)
    large = 16 + (np.log(nf / np.float32(16)) / np.float32(math.log(2048 / 16)) * np.float32(16)).astype(np.int32)
    large = np.minimum(large, 31)
    return np.where(dist < 16, dist, large)


def _consts():
    cst = np.zeros((128, 1280), np.float32)
    p = np.arange(128)[:, None]
    s = np.arange(64)[None, :]
    cst[:, 0:128] = np.eye(128, dtype=np.float32)
    s = np.arange(128)[None, :]
    cst[:, 128:256] = (p > s)
    cst[:, 256:384] = (p >= s)
    cst[:, 384:512] = (p < s)
    cst[:, 512:640] = (p <= s)
    cst[:, 640:768] = (p // 64 == (s // 64))
    cst[:, 768:770] = (p // 64 == np.arange(2)[None, :])
    cst[:, 770:1026] = (np.arange(256)[None, :] % 128 != 0)
    cst[:, 1026:1154] = 1.0
    d = np.arange(FLEN) - 127
    valid = (d >= 0) & (d <= 2048)
    m = ((d <= 128).astype(np.int32) + ((d % 4 == 0) & (d <= 512)).astype(np.int32)
         + ((d % 16 == 0) & (d <= 2048)).astype(np.int32))
    m = np.where(valid, m, 0)
    ok = m > 0
    bucket = _t5_bucket_np(np.clip(d, 0, 2048))
    oh = np.zeros((32, FLEN), np.float32)
    oh[bucket[ok], np.nonzero(ok)[0]] = 1.0
    fcr = np.where(ok, np.log(np.maximum(m, 1)).astype(np.float32), np.float32(NEG)).astype(np.float32)
    fc = np.tile(fcr[None, :], (NH, 1)).astype(np.float32)
    ohs = np.zeros((32, 384), np.float32)
    for br, Dd in enumerate((1, 4, 16)):
        dj = Dd * (128 - np.arange(128))
        ohs[_t5_bucket_np(dj), br * 128 + np.arange(128)] = 1.0
    scst = np.zeros((8, 1024), np.float32)
    hp = np.arange(8)[:, None]
    scst[:, 0:512] = (np.arange(512)[None, :] // 64 == hp)
    col = np.arange(256)[None, :]
    scst[:, 512:768] = ((col % 16) == (col // 16)) * np.ones((8, 1), np.float32)
    scst[:, 768:776] = (np.arange(8)[None, :] == hp)
    return cst, oh, fc, ohs, scst


def _perm():
    cols = list(range(0, 2048)) + list(range(3712, 4224)) + list(range(3584, 3712))
    for c in range(4):
        cols += list(range(2048 + 128 * c, 2048 + 128 * c + 128))
        cols += list(range(2560 + 128 * c, 2560 + 128 * c + 128))
        cols += list(range(3072 + 128 * c, 3072 + 128 * c + 128))
    return np.array(cols, np.int64)


def _uchunks():
    ch = [np.arange(1536, 1664)]
    for c in range(4):
        ch.append(np.arange(128 * c, 128 * c + 128))
        ch.append(np.arange(512 + 128 * c, 512 + 128 * c + 128))
        ch.append(np.arange(1024 + 128 * c, 1024 + 128 * c + 128))
    return ch


_NC_CACHE = {}
DBG_MODE = None


def _get_nc(TP, NS, KEEP):
    key = (TP, NS, KEEP)
    if key not in _NC_CACHE:
        _NC_CACHE[key] = build(TP, NS, KEEP, dbg=DBG_MODE)
    return _NC_CACHE[key]


def _core_inputs(inp, b, s0, NS, TP):
    f = lambda a: np.ascontiguousarray(np.asarray(a, dtype=np.float32))
    cst, oh, fc, ohs, scst = _consts()
    perm = _perm()
    uch = _uchunks()
    mu = f(inp["mu_shift"])[0]
    ucat = np.concatenate(uch)
    pvec = np.zeros((128, 40), np.float32)
    for i, idx in enumerate(uch):
        pvec[:, i] = mu[idx]
    for c in range(4):
        sl = slice(128 * c, 128 * c + 128)
        pvec[:, 13 + c] = f(inp["w0"])[0][sl]
        pvec[:, 17 + c] = f(inp["a0"])[0][sl]
        pvec[:, 21 + c] = f(inp["k_k"])[0][sl]
        pvec[:, 25 + c] = f(inp["k_a"])[0][sl]
        pvec[:, 33 + c] = f(inp["r_k"])[0].reshape(-1)[sl]
    m = {
        "x": f(inp["x_prompt"][b][:TP]),
        "cv": f(np.concatenate([np.asarray(inp["c_prompt"])[b:b + 1], np.asarray(inp["c_sample"])[s0:s0 + NS]], 0)),
        "wperm": f(np.asarray(inp["w_in"])[0][:, perm]),
        "adaw": f(inp["ada_w"])[0],
        "adab": f(inp["ada_b"])[0][None, :],
        "normw": f(inp["norm_w"])[0][None, :],
        "fnw": f(inp["final_norm_w"])[None, :],
        "wout": f(inp["w_out"])[0],
        "relb": f(inp["rel_bias"]),
        "oh": oh, "fc": fc, "cst": cst, "pvec": pvec,
        "wlora": f(inp["w_lora_b"])[0],
        "alora": f(inp["a_lora_b"])[0],
        "lnwb": f(np.concatenate([np.asarray(inp["ln_x_w"])[0], np.asarray(inp["ln_x_b"])[0]])[None, :]),
        "xs": f(np.asarray(inp["x_sample"])[s0:s0 + NS, 0]),
        "cvs": f(np.asarray(inp["c_sample"])[s0:s0 + NS]),
        "ck": f(np.asarray(inp["cache_win_k"])[0, s0:s0 + NS]).reshape(NS, WIN, 512),
        "cvv": f(np.asarray(inp["cache_win_v"])[0, s0:s0 + NS]).reshape(NS, WIN, 512),
        "swkv": f(np.asarray(inp["state_wkv"])[0, s0:s0 + NS]).reshape(NS * NH, HD * HD),
        "ssh": f(np.asarray(inp["state_shift"])[0, s0:s0 + NS][:, ucat]),
        "prow": f(np.stack([np.asarray(inp["w0"])[0], np.asarray(inp["a0"])[0], np.asarray(inp["k_k"])[0], np.asarray(inp["k_a"])[0],
                            np.asarray(inp["r_k"])[0].reshape(-1), np.asarray(inp["ln_x_w"])[0], np.asarray(inp["ln_x_b"])[0],
                            np.zeros(512, np.float32)])),
        "muperm": f(mu[ucat][None, :]),
        "ohs": ohs, "scst": scst,
    }
    return m


def kernel(**inp):
    B, TP, _ = np.asarray(inp["x_prompt"]).shape
    NSAMP = np.asarray(inp["x_sample"]).shape[0]
    NS = NSAMP // 8
    KEEP = min(WIN, TP)
    nc = _get_nc(TP, NS, KEEP)
    in_maps = [_core_inputs(inp, i % B, i * NS, NS, TP) for i in range(8)]
    res = run_bass_kernel_spmd(nc, in_maps, core_ids=list(range(8))).results
    uch = _uchunks()
    y_p = np.stack([res[b]["y"] for b in range(B)]).astype(np.float32)
    wk = np.stack([res[b]["wk"].reshape(KEEP, NH, HD) for b in range(B)])[None].astype(np.float32)
    wv = np.stack([res[b]["wv"].reshape(KEEP, NH, HD) for b in range(B)])[None].astype(np.float32)
    wkv = np.stack([res[b]["wkv"] for b in range(B)])[None].astype(np.float32)
    shp = np.zeros((1, B, SHW), np.float32)
    for b in range(B):
        for i, idx in enumerate(uch):
            shp[0, b, idx] = res[b]["shp"][i]
    ucat = np.concatenate(uch)
    y_s = np.concatenate([res[i]["ys"] for i in range(8)], 0).reshape(NSAMP, 1, D).astype(np.float32)
    wks = np.concatenate([res[i]["ks"] for i in range(8)], 0).reshape(1, NSAMP, 1, NH, HD).astype(np.float32)
    wvs = np.concatenate([res[i]["vs"] for i in range(8)], 0).reshape(1, NSAMP, 1, NH, HD).astype(np.float32)
    wkvs = np.concatenate([res[i]["wkvs"] for i in range(8)], 0).reshape(1, NSAMP, NH, HD, HD).astype(np.float32)
    shs = np.zeros((1, NSAMP, SHW), np.float32)
    shs[0][:, ucat] = np.concatenate([res[i]["shs"] for i in range(8)], 0)
    return (y_p, y_s, wk, wv, wks, wvs, wkv, wkvs, shp, shs)
```

```python
import math
import numpy as np
import concourse.bass as bass
import concourse.mybir as mybir
from concourse.ap import AP
from concourse.bass_utils import run_bass_kernel_spmd
from contextlib import ExitStack

F32 = mybir.dt.float32
BF16 = mybir.dt.bfloat16
AF = mybir.ActivationFunctionType
ALU = mybir.AluOpType
AX = mybir.AxisListType

D = 1024
NH = 8
HD = 64
INW = 4224
SHW = 1664
WIN = 2048
NKT = 17
STRIPW = NKT * 128
FLEN = STRIPW + 128
C0 = math.exp(-0.5)
NEG = -30000.0
NSQ = 6
SBW = 256
TPS = SBW // 128
NORM_EPS = 1e-6
GN_EPS = HD * 1e-5


class Buf:
    __slots__ = ("name", "w", "rs")

    def __init__(self, name):
        self.name = name
        self.w = None
        self.rs = []


class Op:
    __slots__ = ("eng", "fn", "deps", "sig", "val", "dma", "idx")


class Sched:
    ENGS = ["pe", "act", "dve", "pool", "sp"]

    def __init__(self, nc, stack, n_dma=40):
        self.nc = nc
        self.ops = {e: [] for e in self.ENGS}
        self.sem = {e: stack.enter_context(nc.semaphore("s_" + e)) for e in ["pe", "act", "dve", "pool"]}
        self.dsem = [stack.enter_context(nc.semaphore("d%d" % i)) for i in range(n_dma)]
        self.dcnt = [0] * n_dma
        self.dlast = [None] * n_dma
        self.drr = 0
        self.n = 0

    def add(self, eng, fn, reads=(), writes=(), dma=False):
        op = Op()
        op.eng = eng
        op.fn = fn
        op.sig = False
        op.val = None
        op.dma = None
        op.idx = self.n
        self.n += 1
        deps = {}
        for b in reads:
            if b.w is not None:
                deps[b.w.idx] = b.w
        for b in writes:
            if b.w is not None:
                deps[b.w.idx] = b.w
            for r in b.rs:
                deps[r.idx] = r
        if dma:
            k = self.drr
            self.drr = (self.drr + 1) % len(self.dsem)
            if self.dlast[k] is not None:
                deps[self.dlast[k].idx] = self.dlast[k]
            self.dcnt[k] += 16
            op.dma = k
            op.val = self.dcnt[k]
            self.dlast[k] = op
        dl = []
        for d in deps.values():
            if d is op:
                continue
            if d.dma is None and d.eng == "pe" and eng == "pe":
                continue
            if d.dma is None:
                d.sig = True
            dl.append(d)
        op.deps = dl
        for b in reads:
            b.rs.append(op)
        for b in writes:
            b.w = op
            b.rs = []
        self.ops[eng].append(op)
        return op

    def emit(self, block):
        for e in ["pe", "act", "dve", "pool"]:
            c = 0
            for op in self.ops[e]:
                if op.sig:
                    c += 1
                    op.val = c
        fin = [(self.dsem[k], self.dcnt[k]) for k in range(len(self.dsem)) if self.dcnt[k] > 0]

        def run(e, engobj, extra=None):
            waited = {}
            for op in self.ops[e]:
                need = {}
                for d in op.deps:
                    if d.dma is not None:
                        s = self.dsem[d.dma]
                        key = ("d", d.dma)
                    else:
                        s = self.sem[d.eng]
                        key = ("e", d.eng)
                    if need.get(key, (None, 0))[1] < d.val:
                        need[key] = (s, d.val)
                for key, (s, v) in need.items():
                    if waited.get(key, 0) >= v:
                        continue
                    engobj.wait_ge(s, v)
                    waited[key] = v
                ins = op.fn(engobj)
                if op.dma is not None:
                    ins.then_inc(self.dsem[op.dma], 16)
                elif op.sig:
                    ins.then_inc(self.sem[e], 1)
            if extra:
                for s, v in extra:
                    engobj.wait_ge(s, v)

        @block.tensor
        def _(eng):
            run("pe", eng)

        @block.scalar
        def _(eng):
            run("act", eng)

        @block.vector
        def _(eng):
            run("dve", eng)

        @block.gpsimd
        def _(eng):
            run("pool", eng)

        @block.sync
        def _(eng):
            run("sp", eng, fin)


class T:
    def __init__(self, t, name):
        self.t = t
        self.b = Buf(name)

    def __getitem__(self, k):
        return self.t[k]

    @property
    def ap(self):
        return self.t[:]


class View:
    def __init__(self, ap, buf):
        self._ap = ap
        self.b = buf

    def __getitem__(self, k):
        return self._ap[k]

    @property
    def ap(self):
        return self._ap


def bs(*ts):
    return [x.b if hasattr(x, "b") else x for x in ts]


def build(TP, NS, KEEP, dbg=None):
    assert TP % SBW == 0
    NSB = TP // SBW
    nc = bass.Bass("TRN2", target_bir_lowering=False)

    def din(name, shape, dt=F32):
        return nc.dram_tensor(name, list(shape), dt, kind="ExternalInput").ap()

    def dout(name, shape, dt=F32):
        return nc.dram_tensor(name, list(shape), dt, kind="ExternalOutput").ap()

    x_d = din("x", [TP, D])
    cv_d = din("cv", [1 + NS, D])
    wperm_d = din("wperm", [D, INW])
    adaw_d = din("adaw", [D, 3 * D])
    adab_d = din("adab", [1, 3 * D])
    normw_d = din("normw", [1, D])
    fnw_d = din("fnw", [1, D])
    wout_d = din("wout", [D, D])
    relb_d = din("relb", [32, NH])
    oh_d = din("oh", [32, FLEN])
    fc_d = din("fc", [NH, FLEN])
    cst_d = din("cst", [128, 1280])
    pvec_d = din("pvec", [128, 40])
    wlora_d = din("wlora", [64, 512])
    alora_d = din("alora", [64, 512])
    lnwb_d = din("lnwb", [1, 1024])
    xs_d = din("xs", [NS, D])
    ck_d = din("ck", [NS, WIN, 512])
    cvv_d = din("cvv", [NS, WIN, 512])
    swkv_d = din("swkv", [NS * NH, HD * HD])
    ssh_d = din("ssh", [NS, SHW])
    prow_d = din("prow", [8, 512])
    muperm_d = din("muperm", [1, SHW])
    ohs_d = din("ohs", [32, 384])
    scst_d = din("scst", [8, 1024])
    cvs_d = din("cvs", [NS, D])

    y_d = dout("y", [TP, D])
    wk_d = dout("wk", [KEEP, 512])
    wv_d = dout("wv", [KEEP, 512])
    wkv_d = dout("wkv", [NH, HD, HD])
    shp_d = dout("shp", [13, 128])
    ys_d = dout("ys", [NS, D])
    ks_d = dout("ks", [NS, 512])
    vs_d = dout("vs", [NS, 512])
    wkvs_d = dout("wkvs", [NS * NH, HD * HD])
    shs_d = dout("shs", [NS, SHW])
    dbg_d = dout("dbgmix", [TP, D], BF16) if dbg == "mix" else None

    wbf_d = nc.dram_tensor("wbf", [D, INW], BF16, kind="Internal").ap()
    wobf_d = nc.dram_tensor("wobf", [D, D], BF16, kind="Internal").ap()
    fscr_d = nc.dram_tensor("fscr", [NH, FLEN], F32, kind="Internal").ap()
    qscr_d = nc.dram_tensor("qscr", [NS, 512], F32, kind="Internal").ap()
    mscr_d = nc.dram_tensor("mscr", [NS, 3 * D], F32, kind="Internal").ap()
    vscr_d = nc.dram_tensor("vscr", [NS, NH, 6, HD], F32, kind="Internal").ap()
    yscr_d = nc.dram_tensor("yscr", [NS * NH, HD], F32, kind="Internal").ap()
    fscr2_d = nc.dram_tensor("fscr2", [NH * 128, FLEN], F32, kind="Internal").ap()

    with ExitStack() as st:
        S = Sched(nc, st)
        add = S.add

        def sb(name, shape, dt):
            return T(st.enter_context(nc.sbuf_tensor("sb_" + name, list(shape), dt)), name)

        def psb(name, shape, dt):
            return T(st.enter_context(nc.psum_tensor("ps_" + name, list(shape), dt)), name)

        rr = [0]

        def anyeng():
            rr[0] += 1
            return "dve" if rr[0] % 2 else "pool"

        banks = [psb("bk%d" % i, [128, 512], F32) for i in range(8)]
        cst = sb("cst", [128, 1280], F32)
        pvec = sb("pvec", [128, 40], F32)
        identb = sb("identb", [128, 128], BF16)
        blkb = sb("blkb", [128, 128], BF16)
        zerob = sb("zerob", [128, 128], BF16)
        ind2b = sb("ind2b", [128, 2], BF16)
        xb = sb("xb", [128, TPS, D], F32)
        hb = sb("hb", [128, TPS, D], BF16)
        tmpf = sb("tmpf", [128, D], F32)
        hT = sb("hT", [128, 8, SBW], BF16)
        wst = [sb("wst%d" % i, [128, 8, 512], BF16) for i in range(2)]
        Gb = sb("Gb", [128, D], F32)
        SHb = sb("SHb", [128, D], F32)
        GATEb = sb("GATEb", [128, D], F32)
        FNWb = sb("FNWb", [128, D], F32)
        LNWb = sb("LNWb", [128, 512], F32)
        LNBb = sb("LNBb", [128, 512], F32)
        strips = sb("strips", [128, NH, STRIPW], BF16)
        NRING = 18
        kTr = sb("kTr", [128, 4, NRING * 128], BF16)
        Vr = sb("Vr", [128, NRING, NH, 66], BF16)
        qTz = sb("qTz", [128, NH, SBW], BF16)
        gat = sb("gat", [128, TPS, 512], BF16)
        grw = sb("grw", [128, TPS, 512], BF16)
        stat = sb("stat", [128, 64], F32)
        uraw = sb("uraw", [128, SBW + 8], BF16)
        carry = sb("carry", [128, 16], BF16)
        ulast = sb("ulast", [128, 16], F32)
        dtmp = sb("dtmp", [128, SBW], BF16)
        um = [sb("um%d" % i, [128, SBW], BF16) for i in range(3)]
        txw = sb("txw", [128, SBW], BF16)
        wlora = sb("wlora", [128, 512], BF16)
        alora = sb("alora", [128, 512], BF16)
        f1 = sb("f1", [128, 512 if 1 <= 3 else SBW], F32)
        f2 = sb("f2", [128, 512 if 2 <= 3 else SBW], F32)
        f3 = sb("f3", [128, 512 if 3 <= 3 else SBW], F32)
        f4 = sb("f4", [128, 512 if 4 <= 3 else SBW], F32)
        f5 = sb("f5", [128, 512 if 5 <= 3 else SBW], F32)
        f6 = sb("f6", [128, 512 if 6 <= 3 else SBW], F32)
        f7 = sb("f7", [128, 512 if 7 <= 3 else SBW], F32)
        f8 = sb("f8", [128, 512 if 8 <= 3 else SBW], F32)
        alT = sb("alT", [128, 4, SBW], BF16)
        nbT = sb("nbT", [128, 4, SBW], BF16)
        gaT = sb("gaT", [128, 4, SBW], BF16)
        rhT = sb("rhT", [128, 4, SBW], BF16)
        rkT = sb("rkT", [128, 4, SBW], BF16)
        WC = sb("WC", [128, 4, TPS], F32)
        nbK = sb("nbK", [128, TPS, 512], BF16)
        gaK = sb("gaK", [128, TPS, 512], BF16)
        vK = sb("vK", [128, TPS, 512], BF16)
        Pw = sb("Pw", [128, 1024], BF16)
        PTw = sb("PTw", [128, 1024], BF16)
        Zw = sb("Zw", [128, 1024], BF16)
        AagT = sb("AagT", [128, 1024], BF16)
        BrbT = sb("BrbT", [128, 1024], BF16)
        BrgT = sb("BrgT", [128, 1024], BF16)
        Rb = sb("Rb", [128, 512], BF16)
        Ub = sb("Ub", [128, 512], BF16)
        N0f = sb("N0f", [128, 512], F32)
        N0b = [sb("N0b%d" % i, [128, 512], BF16) for i in range(2)]
        identf = sb("identf", [128, 128], F32)
        mixed = sb("mixed", [128, TPS, D], BF16)
        mixT = sb("mixT", [128, TPS, 8, 128], BF16)
        expT = [sb("expT%d" % i, [128, 4, 128], BF16) for i in range(2)]
        attf = f3
        xo = tmpf
        yo = [sb("yo0", [128, D], F32)]
        kvo = [View(yo[0][:, 0:512], yo[0].b)]
        csil = View(mixed[0:32, 0, :], mixed.b)
        csT = sb("csT", [128, 8, 32], BF16)
        hsT = sb("hsT", [128, 8, 16], BF16)
        xxT = sb("xxT", [128, 16], BF16)
        relb = sb("relb", [32, NH], F32)
        onesf = sb("onesf", [1, 128], F32)
        modsb = View(kTr.ap.rearrange("p a b -> p (a b)").bitcast(F32)[0:32, 0:3 * D], kTr.b)
        frow = View(strips.ap.rearrange("p a b -> p (a b)").bitcast(F32)[0:8, 0:FLEN], strips.b)
        rowtmp = View(yo[0][0:1, :], yo[0].b)

        C_I = 0
        C_ML = 128
        C_MLI = 256
        C_MU = 384
        C_MUI = 512
        C_BLK = 640
        C_IND = 768
        C_SEG = 770
        C_ONE = 1026
        V_MU = 0
        V_W0 = 13
        V_A0 = 17
        V_KK = 21
        V_KA = 25
        V_OMKA = 29
        V_RK = 33

        def dma(out, in_, reads=(), writes=()):
            return add("sp", lambda e: e.dma_start(out=out, in_=in_), reads=bs(*reads), writes=bs(*writes), dma=True)

        def mm(out, lhsT, rhs, start, stop, reads, writes):
            return add("pe", lambda e: e.matmul(out, lhsT=lhsT, rhs=rhs, start=start, stop=stop, skip_group_check=True),
                       reads=bs(*reads), writes=bs(*writes))

        def tr(out, in_, reads, writes, idn=None):
            idt = identb
            return add("pe", lambda e: e.transpose(out, in_, idt[0:in_.shape[0], 0:in_.shape[0]]),
                       reads=bs(idt, *reads), writes=bs(*writes))

        def act(out, in_, func, reads, writes, scale=1.0, bias=0.0, accum=None):
            kw = {}
            if accum is not None:
                kw["accum_out"] = accum
            return add("act", lambda e: e.activation(out=out, in_=in_, func=func, scale=scale, bias=bias, **kw),
                       reads=bs(*reads), writes=bs(*writes))

        def tt(eng, out, in0, in1, op, reads, writes):
            return add(eng, lambda e: e.tensor_tensor(out=out, in0=in0, in1=in1, op=op), reads=bs(*reads), writes=bs(*writes))

        def ts(eng, out, in0, s1, s2, op0, op1, reads, writes):
            return add(eng, lambda e: e.tensor_scalar(out=out, in0=in0, scalar1=s1, scalar2=s2, op0=op0, op1=op1),
                       reads=bs(*reads), writes=bs(*writes))

        def stt(out, in0, scalar, in1, op0, op1, reads, writes):
            return add("dve", lambda e: e.scalar_tensor_tensor(out=out, in0=in0, scalar=scalar, in1=in1, op0=op0, op1=op1),
                       reads=bs(*reads), writes=bs(*writes))

        def cp(eng, out, in_, reads, writes):
            if eng == "act":
                return act(out, in_, AF.Copy, reads, writes)
            return add(eng, lambda e: e.tensor_copy(out=out, in_=in_), reads=bs(*reads), writes=bs(*writes))

        def bview(bank, dt=BF16):
            return bank.t[:].bitcast(dt)

        dma(cst.ap, cst_d, writes=[cst])
        dma(pvec.ap, pvec_d, writes=[pvec])
        dma(relb.ap, relb_d, writes=[relb])
        cp("dve", identb.ap, cst[:, C_I:C_I + 128], [cst], [identb])
        cp("dve", identf.ap, cst[:, C_I:C_I + 128], [cst], [identf])
        cp("pool", blkb.ap, cst[:, C_BLK:C_BLK + 128], [cst], [blkb])
        cp("pool", ind2b.ap, cst[:, C_IND:C_IND + 2], [cst], [ind2b])
        cp("pool", onesf.ap, cst[0:1, C_ONE:C_ONE + 128], [cst], [onesf])
        add("pool", lambda e: e.memset(carry.ap, 0.0), writes=bs(carry))
        add("pool", lambda e: e.memset(zerob.ap, 0.0), writes=bs(zerob))
        ts("dve", pvec[:, V_OMKA:V_OMKA + 4], pvec[:, V_KA:V_KA + 4], -1.0, 1.0, ALU.mult, ALU.add, [pvec], [pvec])
        add("pool", lambda e: e.memset(N0f.ap, 0.0), writes=bs(N0f))
        add("pool", lambda e: e.memset(N0b[0].ap, 0.0), writes=bs(N0b[0]))
        add("pool", lambda e: e.memset(qTz.ap, 0.0), writes=bs(qTz))
        dma(f1[0:64, :], wlora_d, writes=[f1])
        dma(f1[64:128, :], alora_d, writes=[f1])
        cp("dve", wlora[0:64, :], f1[0:64, :], [f1], [wlora])
        cp("dve", alora[64:128, :], f1[64:128, :], [f1], [alora])

        xbf = xb.ap.rearrange("p a b -> p (a b)")
        hbf = hb.ap.rearrange("p a b -> p (a b)")
        NSL = TPS
        xslot = [Buf("xs%d" % i) for i in range(NSL)]
        hslot = [Buf("hs%d" % i) for i in range(NSL)]
        pieces = []
        wstores = []
        for kc in range(8):
            for c0 in range(0, INW, 1024):
                pieces.append((kc, c0, min(1024, INW - c0)))
        for i, (kc, c0, w) in enumerate(pieces):
            sl = i % NSL
            dma(xbf[:, sl * 1024:sl * 1024 + w], wperm_d[kc * 128:(kc + 1) * 128, c0:c0 + w], writes=[xslot[sl]])
            e = ["act", "dve", "pool"][i % 3]
            cp(e, hbf[:, sl * 1024:sl * 1024 + w], xbf[:, sl * 1024:sl * 1024 + w], [xslot[sl]], [hslot[sl]])
            wstores.append(dma(wbf_d[kc * 128:(kc + 1) * 128, c0:c0 + w], hbf[:, sl * 1024:sl * 1024 + w], reads=[hslot[sl]], writes=[]))
        for kc in range(8):
            sl = kc % NSL
            dma(xbf[:, sl * 1024:(sl + 1) * 1024], wout_d[kc * 128:(kc + 1) * 128, :], writes=[xslot[sl]])
            e = ["act", "dve", "pool"][kc % 3]
            cp(e, hbf[:, sl * 1024:(sl + 1) * 1024], xbf[:, sl * 1024:(sl + 1) * 1024], [xslot[sl]], [hslot[sl]])
            wstores.append(dma(wobf_d[kc * 128:(kc + 1) * 128, :], hbf[:, sl * 1024:(sl + 1) * 1024], reads=[hslot[sl]], writes=[]))

        def compute_mod(cv_ap, NR):
            dma(f2[0:NR, :], cv_ap[:, 0:512], writes=[f2])
            dma(f3[0:NR, :], cv_ap[:, 512:1024], writes=[f3])
            act(csil[0:NR, 0:512], f2[0:NR, :], AF.Silu, [f2], [csil])
            act(csil[0:NR, 512:1024], f3[0:NR, :], AF.Silu, [f3], [csil])
            pb = banks[0]
            for c in range(8):
                tr(bview(pb)[:, c * 32:c * 32 + NR], csil[0:NR, c * 128:(c + 1) * 128], [csil], [pb])
            cp("dve", csT.ap.rearrange("p a b -> p (a b)")[:, 0:256], bview(pb)[:, 0:256], [pb], [csT])
            for g in range(6):
                for kc in range(8):
                    sl = kc % NSL
                    dma(xbf[:, sl * 1024:sl * 1024 + 512], adaw_d[kc * 128:(kc + 1) * 128, g * 512:(g + 1) * 512], writes=[xslot[sl]])
                    e = ["act", "dve", "pool"][kc % 3]
                    cp(e, hbf[:, sl * 1024:sl * 1024 + 512], xbf[:, sl * 1024:sl * 1024 + 512], [xslot[sl]], [hslot[sl]])
                    mm(banks[1][0:NR, :], csT[:, kc, 0:NR], hbf[:, sl * 1024:sl * 1024 + 512], kc == 0, kc == 7,
                       [csT, hslot[sl]], [banks[1]])
                cp("dve", modsb[0:NR, g * 512:(g + 1) * 512], banks[1][0:NR, :], [banks[1]], [modsb])
            for g in range(6):
                dma(rowtmp[0:1, 0:512], adab_d[0:1, g * 512:(g + 1) * 512], writes=[rowtmp])
                mm(banks[0][0:NR, :], onesf[0:1, 0:NR], rowtmp[0:1, 0:512], True, True, [onesf, rowtmp], [banks[0]])
                tt("dve", modsb[0:NR, g * 512:(g + 1) * 512], banks[0][0:NR, :], modsb[0:NR, g * 512:(g + 1) * 512], ALU.add,
                   [banks[0], modsb], [modsb])
        compute_mod(cv_d, 1 + NS)
        mscr_w = Buf("mscr")
        dma(mscr_d, modsb[1:1 + NS, :], reads=[modsb], writes=[mscr_w])
        def bcast_row(dst, src_row_ap, srcbufs, bank):
            for hlf in range(2):
                mm(bank[:, :], onesf[0:1, :], src_row_ap[:, hlf * 512:(hlf + 1) * 512], True, True, [onesf] + srcbufs, [bank])
                cp("dve", dst[:, hlf * 512:(hlf + 1) * 512], bank[:, :], [bank], [dst])
        bcast_row(SHb, modsb[0:1, 0:D], [modsb], banks[0])
        bcast_row(Gb, modsb[0:1, D:2 * D], [modsb], banks[1])
        bcast_row(GATEb, modsb[0:1, 2 * D:3 * D], [modsb], banks[0])
        dma(rowtmp.ap, normw_d, writes=[rowtmp])
        bcast_row(tmpf, rowtmp.ap, [rowtmp], banks[1])
        stt(Gb.ap, Gb.ap, 1.0, tmpf.ap, ALU.add, ALU.mult, [Gb, tmpf], [Gb])
        dma(rowtmp.ap, fnw_d, writes=[rowtmp])
        bcast_row(FNWb, rowtmp.ap, [rowtmp], banks[0])
        dma(rowtmp.ap, lnwb_d, writes=[rowtmp])
        mm(banks[1][:, :], onesf[0:1, :], rowtmp[0:1, 0:512], True, True, [onesf, rowtmp], [banks[1]])
        cp("dve", LNWb.ap, banks[1][:, :], [banks[1]], [LNWb])
        mm(banks[0][:, :], onesf[0:1, :], rowtmp[0:1, 512:1024], True, True, [onesf, rowtmp], [banks[0]])
        cp("dve", LNBb.ap, banks[0][:, :], [banks[0]], [LNBb])

        if dbg == 'nostrip':
            pass
        for pc in range(0, FLEN, 512):
            w = min(512, FLEN - pc)
            dma(f1[0:32, 0:w], oh_d[:, pc:pc + w], writes=[f1])
            dma(f2[0:8, 0:w], fc_d[:, pc:pc + w], writes=[f2])
            mm(banks[1][0:8, 0:w], relb[0:32, :], f1[0:32, 0:w], True, True, [relb, f1], [banks[1]])
            tt("dve", frow[0:8, pc:pc + w], banks[1][0:8, 0:w], f2[0:8, 0:w], ALU.add, [banks[1], f2], [frow])
        fs_w = Buf("fs_w")
        dma(fscr_d, frow.ap, reads=[frow], writes=[fs_w])
        for h in range(NH):
            dma(fscr2_d[h * 128:(h + 1) * 128, :], AP(fscr_d.tensor, h * FLEN, [[0, 128], [1, FLEN]]), reads=[fs_w], writes=[fs_w])
        pi = 0
        for h in range(NH):
            for q4 in range(4):
                src = AP(fscr2_d.tensor, h * 128 * FLEN + 127 + q4 * 544, [[FLEN - 1, 128], [1, 544]])
                sl = pi % NSL
                pi += 1
                stg = xbf[:, sl * 1024:sl * 1024 + 544]
                dma(stg, src, reads=[fs_w], writes=[xslot[sl]])
                cp(["act", "dve"][pi % 2], strips[:, h, q4 * 544:(q4 + 1) * 544], stg, [xslot[sl]], [strips])
        add("dve", lambda e: e.memset(stat[:, 60:61], 0.0), reads=xslot + hslot, writes=bs(xb, hb, stat))

        add("pool", lambda e: e.memset(kTr.ap, 0.0), writes=bs(kTr))
        add("pool", lambda e: e.memset(Vr.ap, 0.0), writes=bs(Vr))
        add("pool", lambda e: e.memset(Vr[:, :, :, 64:65], 1.0), reads=bs(Vr), writes=bs(Vr))
        bank_rr = [0]

        def nb(lst):
            bank_rr[0] += 1
            return lst[bank_rr[0] % len(lst)]

        GRP = [("q", 0, 512), ("k", 512, 512), ("gf", 2560, 128)] + [("gr%d" % c, 2688 + 384 * c, 384) for c in range(4)] + \
              [("v", 1024, 512), ("ga", 1536, 512), ("gw", 2048, 512)]
        wsti = [0]
        PB = [banks[0], banks[1]]

        def load_w(c0, w):
            wsti[0] += 1
            wt = wst[wsti[0] % 2]
            op = dma(wt[:, :, 0:w], wbf_d[:, c0:c0 + w].rearrange("(kc p) c -> p kc c", p=128), reads=[], writes=[wt])
            op.deps.extend(wstores)
            return wt

        def proj_fm(wt, cc):
            bk = nb(PB)
            for kc in range(8):
                mm(bk[:, 0:SBW], wt[:, kc, cc * 128:(cc + 1) * 128], hT[:, kc, :], kc == 0, kc == 7, [wt, hT], [bk])
            return bk

        n0i = [0]
        for sbi in range(NSB if dbg != 'setup' else 0):
            t0 = sbi * SBW
            last = sbi == NSB - 1
            def norm_gen(nsb):
                for j in range(TPS):
                    act(tmpf.ap, xb[:, j, :], AF.Square, [xb], [tmpf, stat], accum=stat[:, j:j + 1])
                    yield
                ts("dve", stat[:, 4:8], stat[:, 0:4], 1.0 / D, NORM_EPS, ALU.mult, ALU.add, [stat], [stat])
                act(stat[:, 4:8], stat[:, 4:8], AF.Sqrt, [stat], [stat])
                add("dve", lambda e: e.reciprocal(out=stat[:, 8:12], in_=stat[:, 4:8]), reads=bs(stat), writes=bs(stat))
                yield
                for j in range(TPS):
                    stt(tmpf.ap, xb[:, j, :], stat[:, 8 + j:9 + j], Gb.ap, ALU.mult, ALU.mult, [xb, stat, Gb], [tmpf])
                    tt("pool", hb[:, j, :], tmpf.ap, SHb.ap, ALU.add, [tmpf, SHb], [hb])
                    yield
                if nsb + 1 < NSB:
                    dma(xb.ap, x_d[(nsb + 1) * SBW:(nsb + 2) * SBW, :].rearrange("(j p) d -> p j d", p=128), writes=[xb])
                yield

            def hT_transposes():
                for j in range(TPS):
                    bk = nb(PB)
                    for c in range(8):
                        tr(bview(bk)[:, c * 128:(c + 1) * 128], hb[:, j, c * 128:(c + 1) * 128], [hb], [bk])
                    cp("act", hT[:, :, j * 128:(j + 1) * 128], bview(bk).rearrange("p (c t) -> p c t", c=8), [bk], [hT])
            if sbi == 0:
                dma(xb.ap, x_d[t0:t0 + SBW, :].rearrange("(j p) d -> p j d", p=128), writes=[xb])
                for _ in norm_gen(0):
                    pass
                hT_transposes()
            slot = (sbi * TPS) % NRING
            def proj_gen(GL):
                for (gname, c0, w) in GL:
                    if dbg == 'norm':
                        break
                    if dbg == 'pqk' and gname not in ('q', 'k'):
                        continue
                    if dbg == 'pgf' and gname not in ('q', 'k', 'gf'):
                        continue
                    if dbg == 'ptm' and gname not in ('q', 'k', 'v', 'ga', 'gw'):
                        continue
                    if dbg == 'ptmv' and gname not in ('q', 'k', 'v'):
                        continue
                    if dbg == 'ptmg' and gname not in ('q', 'k', 'ga'):
                        continue
                    wt = load_w(c0, w)
                    if gname == "q":
                        for cc in range(4):
                            bk = proj_fm(wt, cc)
                            act(qTz[0:64, 2 * cc, :], bk[0:64, 0:SBW], AF.Copy, [bk], [qTz], scale=0.125)
                            act(qTz[64:128, 2 * cc + 1, :], bk[64:128, 0:SBW], AF.Copy, [bk], [qTz], scale=0.125)
                    elif gname == "k":
                        for cc in range(4):
                            bk = proj_fm(wt, cc)
                            cp("act", kTr[:, cc, slot * 128:slot * 128 + SBW], bk[:, 0:SBW], [bk], [kTr])
                        if t0 + SBW > TP - KEEP:
                            for j in range(TPS):
                                bk = nb(PB)
                                for kc in range(8):
                                    mm(bk[:, :], hT[:, kc, j * 128:(j + 1) * 128], wt[:, kc, :], kc == 0, kc == 7, [wt, hT], [bk])
                                ko = nb(kvo)
                                cp("dve", ko.ap, bk[:, :], [bk], [ko])
                                r0 = t0 + j * 128 - (TP - KEEP)
                                dma(wk_d[r0:r0 + 128, :], ko.ap, reads=[ko])
                    elif gname in ("v", "ga", "gw"):
                        for j in range(TPS):
                            bk = nb(PB)
                            for kc in range(8):
                                mm(bk[:, :], hT[:, kc, j * 128:(j + 1) * 128], wt[:, kc, :], kc == 0, kc == 7, [wt, hT], [bk])
                            if gname == "v":
                                rs_ = (sbi * TPS + j) % NRING
                                cp("dve", Vr[:, rs_, :, 0:64], bk[:, :].rearrange("p (h d) -> p h d", h=NH), [bk], [Vr])
                                if t0 + SBW > TP - KEEP:
                                    ko = nb(kvo)
                                    cp("dve", ko.ap, bk[:, :], [bk], [ko])
                                    r0 = t0 + j * 128 - (TP - KEEP)
                                    dma(wv_d[r0:r0 + 128, :], ko.ap, reads=[ko])
                            elif gname == "ga":
                                act(gat[:, j, :], bk[:, :], AF.Silu, [bk], [gat])
                            else:
                                act(grw[:, j, :], bk[:, :], AF.Silu, [bk], [grw])
                    elif gname == "gf":
                        bk = proj_fm(wt, 0)
                        _tokshift(add, act, tt, stt, cp, bs, bk, 0, um[0], uraw, carry, ulast, dtmp, pvec, V_MU, last)
                        act(txw[0:64, :], um[0][0:64, :], AF.Tanh, [um[0]], [txw])
                        cp("pool", txw[64:128, :], um[0][64:128, :], [um[0]], [txw])
                    else:
                        c = int(gname[2])
                        for q3 in range(3):
                            bk = proj_fm(wt, q3)
                            _tokshift(add, act, tt, stt, cp, bs, bk, 1 + 3 * c + q3, um[q3], uraw, carry, ulast, dtmp, pvec, V_MU, last)
                            yield
                        umr, umk, umv = um
                        W_ = slice(0, SBW)
                        g1, g2, g3 = f1[:, W_], f2[:, W_], f3[:, W_]
                        bw = banks[6]
                        ba = banks[7]
                        mm(bw[:, W_], wlora[0:64, c * 128:(c + 1) * 128], txw[0:64, :], True, True, [wlora, txw], [bw])
                        mm(ba[:, W_], alora[64:128, c * 128:(c + 1) * 128], txw[64:128, :], True, True, [alora, txw], [ba])
                        act(g1, bw[:, W_], AF.Sigmoid, [bw, pvec], [f1], bias=pvec[:, V_W0 + c:V_W0 + c + 1])
                        act(g2, ba[:, W_], AF.Sigmoid, [ba, pvec], [f2], bias=pvec[:, V_A0 + c:V_A0 + c + 1])
                        add("dve", lambda e, g1=g1, g3=g3: e.tensor_tensor_scan(out=g3, data0=cst[:, C_SEG:C_SEG + SBW], data1=g1, initial=0.0,
                                                                   op0=ALU.mult, op1=ALU.add), reads=bs(cst, f1), writes=bs(f3))
                        tt("pool", f4.ap, g3, g1, ALU.subtract, [f3, f1], [f4])
                        act(f5.ap, g3, AF.Exp, [f3], [f5], scale=-C0)
                        act(f6.ap, g3, AF.Exp, [f3], [f6], scale=C0)
                        act(f4.ap, f4.ap, AF.Exp, [f4], [f4], scale=-C0)
                        act(WC[:, c, 0:TPS], AP(f3.ap.tensor, 127, [[512, 128], [128, TPS]]), AF.Exp, [f3], [WC], scale=-C0)
                        yield
                        tt("pool", rhT[:, c, :], umr.ap, f5.ap, ALU.mult, [umr, f5], [rhT])
                        act(dtmp.ap, umk.ap, AF.Square, [umk, pvec], [dtmp], scale=pvec[:, V_KK + c:V_KK + c + 1])
                        bn_ = banks[6]
                        mm(bn_[:, W_], blkb.ap, dtmp.ap, True, True, [blkb, dtmp], [bn_])
                        ts("dve", f7.ap, bn_[:, W_], 1e-24, None, ALU.add, ALU.bypass, [bn_], [f7])
                        act(f7.ap, f7.ap, AF.Sqrt, [f7], [f7])
                        add("dve", lambda e: e.reciprocal(out=f7.ap, in_=f7.ap), reads=bs(f7), writes=bs(f7))
                        stt(f8.ap, umk.ap, pvec[:, V_KK + c:V_KK + c + 1], f7.ap, ALU.mult, ALU.mult, [umk, pvec, f7], [f8])
                        yield
                        tt("pool", alT[:, c, :], f8.ap, f4.ap, ALU.mult, [f8, f4], [alT])
                        tt("dve", f8.ap, f8.ap, g2, ALU.mult, [f8, f2], [f8])
                        stt(nbT[:, c, :], f8.ap, -1.0, f6.ap, ALU.mult, ALU.mult, [f8, f6], [nbT])
                        yield
                        ts("pool", f7.ap, g2, pvec[:, V_KA + c:V_KA + c + 1], pvec[:, V_OMKA + c:V_OMKA + c + 1], ALU.mult, ALU.add,
                           [f2, pvec], [f7])
                        tt("dve", f7.ap, f7.ap, umk.ap, ALU.mult, [f7, umk], [f7])
                        tt("pool", gaT[:, c, :], f7.ap, f6.ap, ALU.mult, [f7, f6], [gaT])
                        stt(rkT[:, c, :], umr.ap, pvec[:, V_RK + c:V_RK + c + 1], f7.ap, ALU.mult, ALU.mult, [umr, pvec, f7], [rkT])
                        yield
                        for (srcap, srcb, dstK, ee) in ((nbT[:, c, :], nbT, nbK, "dve"), (gaT[:, c, :], gaT, gaK, "dve"), (umv.ap, umv, vK, "dve")):
                            bk = nb(PB)
                            for j in range(TPS):
                                tr(bview(bk)[:, j * 128:(j + 1) * 128], srcap[:, j * 128:(j + 1) * 128], [srcb], [bk])
                            cp(ee, dstK[:, :, c * 128:(c + 1) * 128], bview(bk)[:, 0:TPS * 128].rearrange("p (j t) -> p j t", j=TPS), [bk], [dstK])
                    yield
            def rwkv_gen(j, ti):
                    tc = slice(j * 128, (j + 1) * 128)
                    mask4 = lambda c0_: AP(cst.ap.tensor, c0_, [[1280, 128], [0, 4], [1, 128]])
                    par = lambda t_, hh_: AP(t_.ap.tensor, hh_ * 128, [[1024, 128], [256, 4], [1, 128]])
                    hb128 = lambda h_: slice(h_ * 128, (h_ + 1) * 128)
                    hs64 = lambda h_: slice(h_ * 64, (h_ + 1) * 64)
                    specs = [(gaT, alT, C_MU, AagT), (gaT, rhT, C_MUI, BrgT), (nbT, rhT, C_MUI, BrbT), (nbT, alT, C_MU, PTw), (alT, nbT, C_ML, Pw)]
                    RP = [(banks[6], banks[7]), (banks[0], banks[1])]
                    for (La, Ra, mk, dst) in specs:
                        bp = nb(RP)
                        for hh in range(2):
                            for c in range(4):
                                mm(bp[hh][:, c * 128:(c + 1) * 128], La[64 * hh:64 * hh + 64, c, tc], Ra[64 * hh:64 * hh + 64, c, tc], True, True,
                                   [La, Ra], [bp[hh]])
                        yield
                        for hh in range(2):
                            tt("dve", par(dst, hh), bp[hh][:, :].rearrange("p (c s) -> p c s", c=4), mask4(mk), ALU.mult, [bp[hh], cst], [dst])
                    tt("pool", Zw.ap.rearrange("p (h s) -> p h s", h=8), PTw.ap.rearrange("p (h s) -> p h s", h=8),
                       AP(cst.ap.tensor, C_I, [[1280, 128], [0, 8], [1, 128]]), ALU.add, [PTw, cst], [Zw])
                    for lv in range(NSQ):
                        bP = nb(RP)
                        for h in range(8):
                            mm(bP[h // 4][:, hb128(h % 4)], PTw[:, hb128(h)], Pw[:, hb128(h)], True, True, [PTw, Pw], [bP[h // 4]])
                        if lv < NSQ - 1:
                            bT = nb(RP)
                            for h in range(8):
                                mm(bT[h // 4][:, hb128(h % 4)], Pw[:, hb128(h)], PTw[:, hb128(h)], True, True, [PTw, Pw], [bT[h // 4]])
                        yield
                        for q2 in range(2):
                            cp("act", Pw[:, q2 * 512:(q2 + 1) * 512], bP[q2][:, :], [bP[q2]], [Pw])
                        if lv < NSQ - 1:
                            for q2 in range(2):
                                cp("dve", PTw[:, q2 * 512:(q2 + 1) * 512], bT[q2][:, :], [bT[q2]], [PTw])
                        bZ = nb(RP)
                        for h in range(8):
                            mm(bZ[h // 4][:, hb128(h % 4)], Pw[:, hb128(h)], Zw[:, hb128(h)], True, True, [Pw, Zw], [bZ[h // 4]])
                        for q2 in range(2):
                            tt("dve", Zw[:, q2 * 512:(q2 + 1) * 512], bZ[q2][:, :], Zw[:, q2 * 512:(q2 + 1) * 512], ALU.add, [bZ[q2], Zw], [Zw])
                        yield
                    No, Nn = N0b[n0i[0] % 2], N0b[(n0i[0] + 1) % 2]
                    n0i[0] += 1
                    bR = banks[6]
                    mm(bR[:, :], zerob.ap, identb.ap.to_broadcast([128, 128]) if False else strips[:, 0, 0:512], True, False, [zerob, strips], [bR])
                    for c in range(4):
                        mm(bR[:, hb128(c)], alT[:, c, tc], No[:, hb128(c)], False, False, [alT, No], [bR])
                    for h in range(8):
                        mm(bR[:, hs64(h)], AagT[:, hb128(h)], vK[:, j, hs64(h)], False, True, [AagT, vK], [bR])
                    cp("act", Rb.ap, bR[:, :], [bR], [Rb])
                    yield
                    bU = banks[7]
                    for h in range(8):
                        mm(bU[:, hs64(h)], Zw[:, hb128(h)], Rb[:, hs64(h)], True, True, [Zw, Rb], [bU])
                    cp("act", Ub.ap, bU[:, :], [bU], [Ub])
                    yield
                    bY = banks[1]
                    mm(bY[:, :], zerob.ap, strips[:, 0, 0:512], True, False, [zerob, strips], [bY])
                    for c in range(4):
                        mm(bY[:, hb128(c)], rhT[:, c, tc], No[:, hb128(c)], False, False, [rhT, No], [bY])
                    for h in range(8):
                        mm(bY[:, hs64(h)], BrbT[:, hb128(h)], Ub[:, hs64(h)], False, False, [BrbT, Ub], [bY])
                        mm(bY[:, hs64(h)], BrgT[:, hb128(h)], vK[:, j, hs64(h)], False, True, [BrgT, vK], [bY])
                    bN = banks[6]
                    mm(bN[:, :], identf.ap, N0f.ap, True, False, [identf, N0f], [bN])
                    for c in range(4):
                        mm(bN[:, hb128(c)], nbK[:, j, hb128(c)], Ub[:, hb128(c)], False, False, [nbK, Ub], [bN])
                        mm(bN[:, hb128(c)], gaK[:, j, hb128(c)], vK[:, j, hb128(c)], False, True, [gaK, vK], [bN])
                    tt("dve", f1.ap.rearrange("p (c q) -> p c q", c=4), bN[:, :].rearrange("p (c q) -> p c q", c=4),
                       AP(WC.ap.tensor, j, [[4 * TPS, 128], [TPS, 4], [0, 128]]), ALU.mult, [bN, WC], [f1])
                    tt("pool", N0f.ap.rearrange("p (c q) -> p c q", c=4), f1.ap.rearrange("p (c q) -> p c q", c=4), mask4(C_BLK), ALU.mult,
                       [f1, cst], [N0f])
                    cp("act", Nn.ap, N0f.ap, [N0f], [Nn])
                    yield
                    Y3 = bY[:, :].rearrange("p (h i) -> p h i", h=8)
                    add("dve", lambda e: e.tensor_reduce(out=stat[:, 16:24], in_=Y3, axis=AX.X, op=ALU.add), reads=bs(bY), writes=bs(stat))
                    act(f1.ap, bY[:, :], AF.Square, [bY], [f1])
                    add("dve", lambda e: e.tensor_reduce(out=stat[:, 24:32], in_=f1.ap.rearrange("p (h i) -> p h i", h=8), axis=AX.X, op=ALU.add),
                        reads=bs(f1), writes=bs(stat))
                    ts("dve", stat[:, 16:24], stat[:, 16:24], 1.0 / HD, None, ALU.mult, ALU.bypass, [stat], [stat])
                    tt("dve", stat[:, 32:40], stat[:, 16:24], stat[:, 16:24], ALU.mult, [stat], [stat])
                    stt(stat[:, 24:32], stat[:, 24:32], 1.0 / HD, stat[:, 32:40], ALU.mult, ALU.subtract, [stat], [stat])
                    ts("dve", stat[:, 24:32], stat[:, 24:32], GN_EPS, None, ALU.add, ALU.bypass, [stat], [stat])
                    act(stat[:, 24:32], stat[:, 24:32], AF.Sqrt, [stat], [stat])
                    add("dve", lambda e: e.reciprocal(out=stat[:, 24:32], in_=stat[:, 24:32]), reads=bs(stat), writes=bs(stat))
                    b8 = lambda lo: stat[:, lo:lo + 8].unsqueeze(2).to_broadcast([128, 8, 64])
                    f2_3 = f2.ap.rearrange("p (h i) -> p h i", h=8)
                    tt("dve", f2_3, Y3, b8(16), ALU.subtract, [bY, stat], [f2])
                    tt("pool", f2_3, f2_3, b8(24), ALU.mult, [f2, stat], [f2])
                    tt("dve", f2.ap, f2.ap, LNWb.ap, ALU.mult, [f2, LNWb], [f2])
                    tt("pool", f2.ap, f2.ap, LNBb.ap, ALU.add, [f2, LNBb], [f2])
                    bB = banks[7]
                    for c in range(4):
                        mm(bB[:, c * 2:c * 2 + 2], rkT[:, c, j * 128:(j + 1) * 128], ind2b.ap, True, True, [rkT, ind2b], [bB])
                    cp("act", stat[:, 40:48], bB[:, 0:8], [bB], [stat])
                    f3_3 = f3.ap.rearrange("p (h i) -> p h i", h=8)
                    tt("dve", f3_3, vK[:, j, :].rearrange("p (h i) -> p h i", h=8), b8(40), ALU.mult, [vK, stat], [f3])
                    tt("pool", f2.ap, f2.ap, f3.ap, ALU.add, [f2, f3], [f2])
                    tt("dve", mixed[:, j, 512:1024], f2.ap, grw[:, j, :], ALU.mult, [f2, grw], [mixed])
                    yield
            def attn_gen(j, ti):
                    bN0, bN1 = banks[4], banks[5]
                    SB_ = [banks[2], banks[3]]
                    kts = [kt for kt in range(ti - 16, ti + 1) if kt >= 0]
                    mm(bN0[:, 0:260], zerob.ap, strips[:, 0, 0:260], True, False, [zerob, strips], [bN0])
                    mm(bN1[:, 0:260], zerob.ap, strips[:, 0, 0:260], True, False, [zerob, strips], [bN1])
                    units = [(ki, kt, half) for ki, kt in enumerate(kts) for half in range(2)]

                    def emit_pv(u_, ex_):
                        ki_, kt_, half_ = u_
                        bNh_ = bN0 if half_ == 0 else bN1
                        for hq in range(4):
                            h = half_ * 4 + hq
                            mm(bNh_[:, hq * 65:(hq + 1) * 65], ex_[:, hq, :], Vr[:, kt_ % NRING, h, 0:65], False, ki_ == len(kts) - 1, [ex_, Vr], [bNh_])
                    prev = None
                    for idx, (ki, kt, half) in enumerate(units):
                        o = ti - kt
                        rs_ = kt % NRING
                        ksl = slice(rs_ * 128, (rs_ + 1) * 128)
                        bk = SB_[idx % 2]
                        ex = expT[idx % 2]
                        mm(bk[:, :].rearrange("p (h q) -> p h q", h=4), identb.ap, strips[:, half * 4:half * 4 + 4, o * 128:(o + 1) * 128],
                           True, False, [identb, strips], [bk])
                        for cc in range(2):
                            c = half * 2 + cc
                            mm(bk[:, cc * 256:(cc + 1) * 256].rearrange("p (h q) -> p h q", h=2), kTr[:, c, ksl],
                               qTz[:, 2 * c:2 * c + 2, j * 128:(j + 1) * 128], False, cc == 1, [kTr, qTz], [bk])
                        act(ex[:, 0:4, :], bk[:, :].rearrange("p (h q) -> p h q", h=4), AF.Exp, [bk], [ex])
                        if prev is not None:
                            emit_pv(*prev)
                            yield
                        prev = ((ki, kt, half), ex)
                    emit_pv(*prev)
                    yield
                    for half in range(2):
                        bNh = bN0 if half == 0 else bN1
                        N3 = bNh[:, 0:260].rearrange("p (h e) -> p h e", h=4)
                        add("dve", lambda e, N3=N3, half=half: e.reciprocal(out=stat[:, 48 + half * 4:52 + half * 4].unsqueeze(2), in_=N3[:, :, 64:65]),
                            reads=bs(bNh), writes=bs(stat))
                        tt("dve", mixed[:, j, half * 256:(half + 1) * 256].rearrange("p (h d) -> p h d", h=4), N3[:, :, 0:64],
                           stat[:, 48 + half * 4:52 + half * 4].unsqueeze(2).to_broadcast([128, 4, 64]), ALU.mult, [bNh, stat], [mixed])
                    tt("pool", mixed[:, j, 0:512], mixed[:, j, 0:512], gat[:, j, :], ALU.mult, [mixed, gat], [mixed])
                    yield
            def interleave(ga_, gb_):
                a1 = a2 = True
                while a1 or a2:
                    if a1:
                        a1 = next(ga_, "end") != "end"
                    if a2:
                        a2 = next(gb_, "end") != "end"

            def mix_transposes(j):
                bk = nb(PB)
                for c in range(8):
                    tr(bview(bk)[:, c * 128:(c + 1) * 128], mixed[:, j, c * 128:(c + 1) * 128], [mixed], [bk])
                cp("act", mixT[:, j, :, :].rearrange("p c t -> p (c t)"), bview(bk)[:, :], [bk], [mixT])
            assert TPS == 2
            ti0 = sbi * TPS
            for _ in proj_gen(GRP[0:2] + GRP[7:10]):
                pass
            interleave(proj_gen(GRP[2:7]), attn_gen(0, ti0))
            interleave(rwkv_gen(0, ti0), attn_gen(1, ti0 + 1))
            mix_transposes(0)
            if sbi + 1 < NSB:
                interleave(rwkv_gen(1, ti0 + 1), norm_gen(sbi + 1))
            else:
                for _ in rwkv_gen(1, ti0 + 1):
                    pass
            mix_transposes(1)
            if dbg in ('proj', 'chain', 'sweep', 'ypost', 'attn', 'norm', 'pqk', 'pgf', 'ptm', 'ptmv', 'ptmg', 'mats', 'mats1', 'mats2'):
                continue
            wo = []
            for hf in range(2):
                wt = wst[hf]
                op = dma(wt.ap, wobf_d[:, hf * 512:(hf + 1) * 512].rearrange("(kc p) c -> p kc c", p=128), reads=[], writes=[wt])
                op.deps.extend(wstores)
                wo.append(wt)
            for j in range(TPS):
                bO = [banks[6], banks[7]]
                dma(f1.ap, x_d[t0 + j * 128:t0 + (j + 1) * 128, 0:512], writes=[f1])
                dma(f2.ap, x_d[t0 + j * 128:t0 + (j + 1) * 128, 512:1024], writes=[f2])
                for hf in range(2):
                    for fc in range(8):
                        mm(bO[hf][:, :], mixT[:, j, fc, :], wo[hf][:, fc, :], fc == 0, fc == 7, [mixT, wo[hf]], [bO[hf]])
                for hf in range(2):
                    hs = slice(hf * 512, (hf + 1) * 512)
                    tt("dve", xo[:, hs], bO[hf][:, :], GATEb[:, hs], ALU.mult, [bO[hf], GATEb], [xo])
                tt("pool", xo[:, 0:512], xo[:, 0:512], f1.ap, ALU.add, [xo, f1], [xo])
                tt("pool", xo[:, 512:1024], xo[:, 512:1024], f2.ap, ALU.add, [xo, f2], [xo])
                act(mixT[:, j, :, :].rearrange("p c t -> p (c t)"), xo.ap, AF.Square, [xo], [mixT, stat], accum=stat[:, 56:57])
                ts("dve", stat[:, 57:58], stat[:, 56:57], 1.0 / D, NORM_EPS, ALU.mult, ALU.add, [stat], [stat])
                act(stat[:, 57:58], stat[:, 57:58], AF.Sqrt, [stat], [stat])
                add("dve", lambda e: e.reciprocal(out=stat[:, 58:59], in_=stat[:, 57:58]), reads=bs(stat), writes=bs(stat))
                yy = yo[0]
                stt(yy.ap, xo.ap, stat[:, 58:59], FNWb.ap, ALU.mult, ALU.mult, [xo, stat, FNWb], [yy])
                dma(y_d[t0 + j * 128:t0 + (j + 1) * 128, :], yy.ap, reads=[yy])
            if sbi + 1 < NSB:
                hT_transposes()
        if dbg != 'nosample':
            P = slice(0, NS)
            add("dve", lambda e: e.memset(stat[:, 61:62], 0.0), reads=[], writes=xslot + hslot + bs(xb, hb, stat))
            dma(modsb[0:NS, :], mscr_d, reads=[mscr_w], writes=[modsb])
            ar = strips.ap.rearrange("p a b -> p (a b)").bitcast(F32)
            Sst = View(ar[:, 0:4096], strips.b)
            TMP = View(ar[:, 4096:8192], strips.b)
            vec6 = View(ar[:, 8192:8576], strips.b)
            vrf = Vr.ap.rearrange("p a b c -> p (a b c)")
            zs = View(AP(vrf.tensor, 0, [[NRING * NH * 66, 128], [1, 2 * 4224]]).bitcast(F32)[0:NS, :], Vr.b)
            xs = View(yo[0][0:NS, :], yo[0].b)
            hs = View(mixed[0:NS, 0, :], mixed.b)
            dma(xs.ap, xs_d, writes=[xs])
            act(tmpf[P, :], xs.ap, AF.Square, [xs], [tmpf, stat], accum=stat[P, 0:1])
            ts("dve", stat[P, 1:2], stat[P, 0:1], 1.0 / D, NORM_EPS, ALU.mult, ALU.add, [stat], [stat])
            act(stat[P, 1:2], stat[P, 1:2], AF.Sqrt, [stat], [stat])
            add("dve", lambda e: e.reciprocal(out=stat[P, 2:3], in_=stat[P, 1:2]), reads=bs(stat), writes=bs(stat))
            dma(f1[0:1, :], normw_d[0:1, 0:512], writes=[f1])
            dma(f2[0:1, :], normw_d[0:1, 512:1024], writes=[f2])
            for hlf, ft in enumerate((f1, f2)):
                mm(banks[0][P, :], onesf[0:1, 0:NS], ft[0:1, :], True, True, [onesf, ft], [banks[0]])
                stt(Gb[P, hlf * 512:(hlf + 1) * 512], modsb[P, D + hlf * 512:D + (hlf + 1) * 512], 1.0, banks[0][P, :], ALU.add, ALU.mult,
                    [modsb, banks[0]], [Gb])
            stt(tmpf[P, :], xs.ap, stat[P, 2:3], Gb[P, :], ALU.mult, ALU.mult, [xs, stat, Gb], [tmpf])
            tt("pool", hs.ap, tmpf[P, :], modsb[P, 0:D], ALU.add, [tmpf, modsb], [hs])
            bk = banks[1]
            for c in range(8):
                tr(bview(bk)[:, c * 16:(c + 1) * 16], hs[:, c * 128:(c + 1) * 128], [hs], [bk])
            cp("dve", hsT.ap.rearrange("p c t -> p (c t)"), bview(bk)[:, 0:128], [bk], [hsT])
            for gi, c0 in enumerate(range(0, INW, 512)):
                w = min(512, INW - c0)
                wt = load_w(c0, w)
                bk = PB[gi % 2]
                for kc in range(8):
                    mm(bk[P, 0:w], hsT[:, kc, :], wt[:, kc, 0:w], kc == 0, kc == 7, [hsT, wt], [bk])
                cp("dve", zs[:, c0:c0 + w], bk[P, 0:w], [bk], [zs])
            dma(ks_d, zs[:, 512:1024], reads=[zs])
            dma(vs_d, zs[:, 1024:1536], reads=[zs])
            dma(shs_d, zs[:, 2560:4224], reads=[zs])
            qw = Buf("qscr")
            dma(qscr_d, zs[:, 0:512], reads=[zs], writes=[qw])
            dma(f3[0:8, :], scst_d[:, 0:512], writes=[f3])
            dma(f2[0:8, :], scst_d[:, 512:1024], writes=[f2])
            dma(f1[0:32, 0:384], ohs_d, writes=[f1])
            bS = View(ar[:, 8576:8600], strips.b)
            for br in range(3):
                mm(banks[2][:, br * 8:(br + 1) * 8], f1[0:32, br * 128:(br + 1) * 128], relb[0:32, :], True, True, [f1, relb], [banks[2]])
            cp("dve", bS.ap, banks[2][:, 0:24], [banks[2]], [bS])
            f32v = lambda t_, pat: View(t_.ap.rearrange(pat).bitcast(F32), t_.b)
            kst = [f32v(alT, "p a b -> p (a b)"), f32v(nbT, "p a b -> p (a b)")]
            vst = [f32v(gaT, "p a b -> p (a b)"), f32v(rhT, "p a b -> p (a b)"), f32v(rkT, "p a b -> p (a b)")]
            qb_ = f32v(nbK, "p a b -> p (a b)")
            gk32 = gaK.ap.rearrange("p a b -> p (a b)").bitcast(F32)
            lraw = View(gk32[:, 0:24], gaK.b)
            esb = View(gk32[:, 32:56], gaK.b)
            vk32 = vK.ap.rearrange("p a b -> p (a b)").bitcast(F32)
            m1 = View(vk32[0:8, :], vK.b)
            m2 = View(gk32[0:8, 64:80], gaK.b)
            bATT, bDEN, bP1, bP2 = banks[4], banks[5], banks[6], banks[7]
            mm(bATT[P, :], zerob[:, 0:NS], wst[0][:, 0, :], True, False, [zerob, wst[0]], [bATT])
            mm(bDEN[P, 0:8], zerob[:, 0:NS], wst[0][:, 0, 0:8], True, False, [zerob, wst[0]], [bDEN])
            dist = [1, 4, 16]
            for s_ in range(NS):
                dma(qb_.ap, AP(qscr_d.tensor, s_ * 512, [[0, 128], [1, 512]]), reads=[qw], writes=[qb_])
                for br in range(3):
                    Dd = dist[br]
                    off = (s_ * WIN + (WIN - 128 * Dd)) * 512
                    kk_ = kst[br % 2]
                    vv_ = vst[br]
                    dma(kk_.ap, AP(ck_d.tensor, off, [[Dd * 512, 128], [1, 512]]), writes=[kk_])
                    dma(vv_.ap, AP(cvv_d.tensor, off, [[Dd * 512, 128], [1, 512]]), writes=[vv_])
                    tt("dve", kk_.ap, kk_.ap, qb_.ap, ALU.mult, [kk_, qb_], [kk_])
                    add("dve", lambda e, kk_=kk_, br=br: e.tensor_reduce(out=lraw[:, br * 8:(br + 1) * 8], in_=kk_.ap.rearrange("p (h d) -> p h d", h=8),
                                                                        axis=AX.X, op=ALU.add), reads=bs(kk_), writes=bs(lraw))
                stt(lraw.ap, lraw.ap, 0.125, bS.ap, ALU.mult, ALU.add, [lraw, bS], [lraw])
                act(esb.ap, lraw.ap, AF.Exp, [lraw], [esb])
                for br in range(3):
                    mm(bP1[0:8, :], esb[:, br * 8:(br + 1) * 8], vst[br].ap, br == 0, br == 2, [esb, vst[br]], [bP1])
                for br in range(3):
                    mm(bP2[0:8, 0:2], esb[:, br * 8:(br + 1) * 8], cst[:, C_ONE:C_ONE + 2], br == 0, br == 2, [esb, cst], [bP2])
                tt("dve", m1.ap, bP1[0:8, :], f3[0:8, :], ALU.mult, [bP1, f3], [m1])
                cp("act", m2[:, 8:9], bP2[0:8, 0:1], [bP2], [m2])
                ts("dve", m2[:, 0:8], f2[0:8, 256:264], m2[:, 8:9], None, ALU.mult, ALU.bypass, [f2, m2], [m2])
                mm(bATT[P, :], f2[0:8, s_ * 16:(s_ + 1) * 16], m1.ap, False, s_ == NS - 1, [f2, m1], [bATT])
                mm(bDEN[P, 0:8], f2[0:8, s_ * 16:(s_ + 1) * 16], m2[:, 0:8], False, s_ == NS - 1, [f2, m2], [bDEN])
            t1 = View(Pw.ap.bitcast(F32)[0:NS, :], Pw.b)
            tt("dve", t1.ap, zs[:, 0:512], zs[:, 512:1024], ALU.mult, [zs], [t1])
            add("dve", lambda e: e.tensor_reduce(out=stat[P, 8:16], in_=t1.ap.rearrange("p (h d) -> p h d", h=8), axis=AX.X, op=ALU.add),
                reads=bs(t1), writes=bs(stat))
            dma(stat[P, 16:24], AP(relb_d.tensor, 0, [[0, NS], [1, 8]]), writes=[stat])
            stt(stat[P, 8:16], stat[P, 8:16], 0.125, stat[P, 16:24], ALU.mult, ALU.add, [stat], [stat])
            act(stat[P, 8:16], stat[P, 8:16], AF.Exp, [stat], [stat])
            ts("dve", stat[P, 8:16], stat[P, 8:16], 3.0, None, ALU.mult, ALU.bypass, [stat], [stat])
            tt("dve", stat[P, 24:32], bDEN[P, 0:8], stat[P, 8:16], ALU.add, [bDEN, stat], [stat])
            add("dve", lambda e: e.reciprocal(out=stat[P, 24:32], in_=stat[P, 24:32]), reads=bs(stat), writes=bs(stat))
            t13 = t1.ap.rearrange("p (h d) -> p h d", h=8)
            tt("dve", t13, zs[:, 1024:1536].rearrange("p (h d) -> p h d", h=8), stat[P, 8:16].unsqueeze(2).to_broadcast([NS, 8, 64]), ALU.mult,
               [zs, stat], [t1])
            tt("dve", t1.ap, t1.ap, bATT[P, :], ALU.add, [t1, bATT], [t1])
            tt("dve", t13, t13, stat[P, 24:32].unsqueeze(2).to_broadcast([NS, 8, 64]), ALU.mult, [t1, stat], [t1])
            act(f4[P, :], zs[:, 1536:1792], AF.Silu, [zs], [f4])
            act(f5[P, :], zs[:, 1792:2048], AF.Silu, [zs], [f5])
            tt("dve", hs[:, 0:256], t1[:, 0:256], f4[P, :], ALU.mult, [t1, f4], [hs])
            tt("dve", hs[:, 256:512], t1[:, 256:512], f5[P, :], ALU.mult, [t1, f5], [hs])
            UO = 2560
            prv = View(ar[0:NS, 4096:4096 + SHW], strips.b)
            mub = View(ar[0:NS, 4096 + SHW:4096 + 2 * SHW], strips.b)
            dma(prv.ap, ssh_d, writes=[prv])
            dma(mub.ap, AP(muperm_d.tensor, 0, [[0, NS], [1, SHW]]), writes=[mub])
            tt("dve", prv.ap, prv.ap, zs[:, UO:UO + SHW], ALU.subtract, [prv, zs], [prv])
            tt("dve", prv.ap, prv.ap, mub.ap, ALU.mult, [prv, mub], [prv])
            tt("dve", prv.ap, prv.ap, zs[:, UO:UO + SHW], ALU.add, [prv, zs], [prv])
            u3 = prv[:, 128:SHW].rearrange("p (c q x) -> p c q x", c=4, q=3)
            r_v, k_v, v_v = u3[:, :, 0, :], u3[:, :, 1, :], u3[:, :, 2, :]
            nat = lambda t_: t_.rearrange("p (c x) -> p c x", c=4)
            hold = [Gb[P, 0:512], Gb[P, 512:1024], SHb[P, 0:512], SHb[P, 512:1024], GATEb[P, 0:512], GATEb[P, 512:1024], tmpf[P, 512:1024]]
            hbufs = [Gb, Gb, SHb, SHb, GATEb, GATEb, tmpf]
            for i_ in range(7):
                dma(hold[i_], AP(prow_d.tensor, i_ * 512, [[0, NS], [1, 512]]), writes=bs(hbufs[i_]))
            W0b, A0b, KKb, KAb, RKb, LWb, LBb = hold
            cp("dve", hs[:, 512:640], prv[:, 0:128], [prv], [hs])
            bk = banks[0]
            tr(bview(bk)[:, 0:NS], hs[:, 512:640], [hs], [bk])
            act(xxT[0:64, :], bview(bk)[0:64, 0:NS], AF.Tanh, [bk], [xxT])
            cp("dve", xxT[64:128, :], bview(bk)[64:128, 0:NS], [bk], [xxT])
            mm(banks[2][P, :], xxT[0:64, :], wlora[0:64, :], True, True, [xxT, wlora], [banks[2]])
            mm(banks[3][P, :], xxT[64:128, :], alora[64:128, :], True, True, [xxT, alora], [banks[3]])
            g1 = View(f1[P, :], f1.b); g2 = View(f2[P, :], f2.b)
            tt("dve", g1.ap, banks[2][P, :], W0b, ALU.add, [banks[2], Gb], [f1])
            act(g1.ap, g1.ap, AF.Sigmoid, [f1], [f1])
            act(g1.ap, g1.ap, AF.Exp, [f1], [f1], scale=-C0)
            tt("dve", g2.ap, banks[3][P, :], A0b, ALU.add, [banks[3], Gb], [f2])
            act(g2.ap, g2.ap, AF.Sigmoid, [f2], [f2])
            V6t = View(Sst[0:NS, 0:3072], strips.b)
            v6 = lambda q_: V6t[:, q_ * 512:(q_ + 1) * 512]
            cp("pool", v6(0), g1.ap, [f1], [V6t])
            kkt = View(f3[P, :], f3.b)
            tt("dve", nat(kkt.ap), k_v, nat(KKb), ALU.mult, [prv, SHb], [f3])
            tt("dve", t1.ap, kkt.ap, kkt.ap, ALU.mult, [f3], [t1])
            add("dve", lambda e: e.tensor_reduce(out=stat[P, 32:40], in_=t1.ap.rearrange("p (h d) -> p h d", h=8), axis=AX.X, op=ALU.add),
                reads=bs(t1), writes=bs(stat))
            ts("dve", stat[P, 32:40], stat[P, 32:40], 1e-24, None, ALU.add, ALU.bypass, [stat], [stat])
            act(stat[P, 32:40], stat[P, 32:40], AF.Sqrt, [stat], [stat])
            add("dve", lambda e: e.reciprocal(out=stat[P, 32:40], in_=stat[P, 32:40]), reads=bs(stat), writes=bs(stat))
            tt("dve", v6(1).rearrange("p (h d) -> p h d", h=8), kkt.ap.rearrange("p (h d) -> p h d", h=8),
               stat[P, 32:40].unsqueeze(2).to_broadcast([NS, 8, 64]), ALU.mult, [f3, stat], [V6t])
            tt("dve", v6(2), v6(1), g2.ap, ALU.mult, [V6t, f2], [V6t])
            tt("dve", t1.ap, g2.ap, KAb, ALU.mult, [f2, SHb], [t1])
            tt("dve", t1.ap, t1.ap, KAb, ALU.subtract, [t1, SHb], [t1])
            ts("dve", t1.ap, t1.ap, 1.0, None, ALU.add, ALU.bypass, [t1], [t1])
            tt("dve", nat(v6(3)), k_v, nat(t1.ap), ALU.mult, [prv, t1], [V6t])
            cp("dve", nat(v6(4)), r_v, [prv], [V6t])
            cp("dve", nat(v6(5)), v_v, [prv], [V6t])
            tt("dve", t1.ap, v6(4), v6(3), ALU.mult, [V6t], [t1])
            tt("dve", t1.ap, t1.ap, RKb, ALU.mult, [t1, GATEb], [t1])
            add("dve", lambda e: e.tensor_reduce(out=stat[P, 40:48], in_=t1.ap.rearrange("p (h d) -> p h d", h=8), axis=AX.X, op=ALU.add),
                reads=bs(t1), writes=bs(stat))
            tt("dve", kkt.ap.rearrange("p (h d) -> p h d", h=8), v6(5).rearrange("p (h d) -> p h d", h=8),
               stat[P, 40:48].unsqueeze(2).to_broadcast([NS, 8, 64]), ALU.mult, [V6t, stat], [f3])
            vw = Buf("vscr")
            for q_ in range(6):
                dma(vscr_d[:, :, q_, :], v6(q_).rearrange("p (h j) -> p h j", h=8), reads=[V6t], writes=[vw])
            dma(vec6.ap, vscr_d.rearrange("s h q j -> (s h) (q j)"), reads=[vw], writes=[vec6])
            dma(Sst.ap, swkv_d, reads=[V6t], writes=[Sst])
            S3 = Sst.ap.rearrange("p (i j) -> p i j", i=64)
            T3 = TMP.ap.rearrange("p (i j) -> p i j", i=64)
            vq = lambda q_: vec6[:, q_ * 64:(q_ + 1) * 64]
            rowb = lambda q_: vq(q_).unsqueeze(1).to_broadcast([128, 64, 64])
            colb = lambda ap_: ap_.unsqueeze(2).to_broadcast([128, 64, 64])
            tt("dve", T3, S3, rowb(1), ALU.mult, [Sst, vec6], [TMP])
            add("dve", lambda e: e.tensor_reduce(out=f6[:, 0:64], in_=T3, axis=AX.X, op=ALU.add),
                reads=bs(TMP), writes=bs(f6))
            tt("pool", S3, S3, rowb(0), ALU.mult, [Sst, vec6], [Sst])
            tt("dve", T3, colb(f6[:, 0:64]), rowb(2), ALU.mult, [f6, vec6], [TMP])
            tt("dve", Sst.ap, Sst.ap, TMP.ap, ALU.subtract, [Sst, TMP], [Sst])
            tt("pool", T3, colb(vq(5)), rowb(3), ALU.mult, [vec6], [TMP])
            tt("dve", Sst.ap, Sst.ap, TMP.ap, ALU.add, [Sst, TMP], [Sst])
            dma(wkvs_d, Sst.ap, reads=[Sst])
            tt("dve", T3, S3, rowb(4), ALU.mult, [Sst, vec6], [TMP])
            add("dve", lambda e: e.tensor_reduce(out=f6[:, 64:128], in_=T3, axis=AX.X, op=ALU.add), reads=bs(TMP), writes=bs(f6))
            yv = f6[:, 64:128]
            add("dve", lambda e: e.tensor_reduce(out=f6[:, 128:129], in_=yv, axis=AX.X, op=ALU.add), reads=bs(f6), writes=bs(f6))
            ts("dve", f6[:, 128:129], f6[:, 128:129], 1.0 / HD, None, ALU.mult, ALU.bypass, [f6], [f6])
            ts("dve", f6[:, 192:256], yv, f6[:, 128:129], None, ALU.subtract, ALU.bypass, [f6], [f6])
            tt("dve", f7[:, 0:64], f6[:, 192:256], f6[:, 192:256], ALU.mult, [f6], [f7])
            add("dve", lambda e: e.tensor_reduce(out=f6[:, 129:130], in_=f7[:, 0:64], axis=AX.X, op=ALU.add), reads=bs(f7), writes=bs(f6))
            ts("dve", f6[:, 129:130], f6[:, 129:130], 1.0 / HD, GN_EPS, ALU.mult, ALU.add, [f6], [f6])
            act(f6[:, 129:130], f6[:, 129:130], AF.Sqrt, [f6], [f6])
            add("dve", lambda e: e.reciprocal(out=f6[:, 130:131], in_=f6[:, 129:130]), reads=bs(f6), writes=bs(f6))
            ts("dve", f7[:, 64:128], f6[:, 192:256], f6[:, 130:131], None, ALU.mult, ALU.bypass, [f6], [f7])
            yw = Buf("yscr")
            dma(yscr_d, f7[:, 64:128], reads=[f7], writes=[yw])
            dma(t1.ap, yscr_d.rearrange("(s h) i -> s (h i)", h=8), reads=[yw], writes=[t1])
            tt("dve", t1.ap, t1.ap, LWb, ALU.mult, [t1, GATEb], [t1])
            tt("dve", t1.ap, t1.ap, LBb, ALU.add, [t1, tmpf], [t1])
            tt("dve", t1.ap, t1.ap, kkt.ap, ALU.add, [t1, f3], [t1])
            act(f4[P, :], zs[:, 2048:2304], AF.Silu, [zs], [f4])
            act(f5[P, :], zs[:, 2304:2560], AF.Silu, [zs], [f5])
            tt("dve", hs[:, 512:768], t1[:, 0:256], f4[P, :], ALU.mult, [t1, f4], [hs])
            tt("dve", hs[:, 768:1024], t1[:, 256:512], f5[P, :], ALU.mult, [t1, f5], [hs])
            bk = banks[1]
            for c in range(8):
                tr(bview(bk)[:, c * 16:(c + 1) * 16], hs[:, c * 128:(c + 1) * 128], [hs], [bk])
            cp("dve", hsT.ap.rearrange("p c t -> p (c t)"), bview(bk)[:, 0:128], [bk], [hsT])
            for hf in range(2):
                wt = wst[hf]
                op = dma(wt.ap, wobf_d[:, hf * 512:(hf + 1) * 512].rearrange("(kc p) c -> p kc c", p=128), reads=[], writes=[wt])
                op.deps.extend(wstores)
                bO_ = banks[2 + hf]
                for fc in range(8):
                    mm(bO_[P, :], hsT[:, fc, :], wt[:, fc, :], fc == 0, fc == 7, [hsT, wt], [bO_])
                hsl = slice(hf * 512, (hf + 1) * 512)
                tt("dve", tmpf[P, hsl] if hf == 0 else f1[P, :], bO_[P, :], modsb[P, 2 * D + hf * 512:2 * D + (hf + 1) * 512], ALU.mult,
                   [bO_, modsb], [tmpf if hf == 0 else f1])
            xo2 = View(Gb[P, :], Gb.b)
            tt("dve", xo2[:, 0:512], tmpf[P, 0:512], xs[:, 0:512], ALU.add, [tmpf, xs], [xo2])
            tt("dve", xo2[:, 512:1024], f1[P, :], xs[:, 512:1024], ALU.add, [f1, xs], [xo2])
            act(hb[P, 0, :], xo2.ap, AF.Square, [xo2], [hb, stat], accum=stat[P, 56:57])
            ts("dve", stat[P, 57:58], stat[P, 56:57], 1.0 / D, NORM_EPS, ALU.mult, ALU.add, [stat], [stat])
            act(stat[P, 57:58], stat[P, 57:58], AF.Sqrt, [stat], [stat])
            add("dve", lambda e: e.reciprocal(out=stat[P, 58:59], in_=stat[P, 57:58]), reads=bs(stat), writes=bs(stat))
            stt(xo2.ap, xo2.ap, stat[P, 58:59], FNWb[P, :], ALU.mult, ALU.mult, [xo2, stat, FNWb], [xo2])
            dma(ys_d, xo2.ap, reads=[xo2])
        bk = banks[0]
        for c in range(4):
            mm(bk[:, c * 128:(c + 1) * 128], N0f[:, c * 128:(c + 1) * 128], identf.ap, True, True, [N0f, identf], [bk])
        cp("dve", f1.ap, bk[:, :], [bk], [f1])
        for h in range(NH):
            c, hh = h // 2, h % 2
            dma(wkv_d[h], f1[64 * hh:64 * hh + 64, c * 128 + 64 * hh:c * 128 + 64 * hh + 64], reads=[f1])
        for ch in range(13):
            dma(shp_d[ch:ch + 1, :].rearrange("o p -> p o"), ulast[:, ch:ch + 1], reads=[ulast])

        with nc.Block() as block:
            S.emit(block)
    return nc


def _tokshift(add, act, tt, stt, cp, bs, bk, ch, dst, uraw, carry, ulast, dtmp, pvec, V_MU, last):
    cp("pool", uraw[:, 0:1], carry[:, ch:ch + 1], [carry], [uraw])
    act(uraw[:, 1:SBW + 1], bk[:, 0:SBW], AF.Copy, [bk], [uraw])
    if last:
        cp("dve", ulast[:, ch:ch + 1], bk[:, SBW - 1:SBW], [bk], [ulast])
    cp("pool", carry[:, ch:ch + 1], uraw[:, SBW:SBW + 1], [uraw], [carry])
    tt("pool", dtmp.ap, uraw[:, 0:SBW], uraw[:, 1:SBW + 1], ALU.subtract, [uraw], [dtmp])
    stt(dst.ap, dtmp.ap, pvec[:, V_MU + ch:V_MU + ch + 1], uraw[:, 1:SBW + 1], ALU.mult, ALU.add, [dtmp, pvec, uraw], [dst])


def _t5_bucket_np(dist):
    dist = np.asarray(dist, np.int32)
    nf = np.maximum(dist, 16).astype(np.float32)
    large = 16 + (np.log(nf / np.float32(16)) / np.float32(math.log(2048 / 16)) * np.float32(16)).astype(np.int32)
    large = np.minimum(large, 31)
    return np.where(dist < 16, dist, large)


def _consts():
    cst = np.zeros((128, 1280), np.float32)
    p = np.arange(128)[:, None]
    s = np.arange(64)[None, :]
    cst[:, 0:128] = np.eye(128, dtype=np.float32)
    s = np.arange(128)[None, :]
    cst[:, 128:256] = (p > s)
    cst[:, 256:384] = (p >= s)
    cst[:, 384:512] = (p < s)
    cst[:, 512:640] = (p <= s)
    cst[:, 640:768] = (p // 64 == (s // 64))
    cst[:, 768:770] = (p // 64 == np.arange(2)[None, :])
    cst[:, 770:1026] = (np.arange(256)[None, :] % 128 != 0)
    cst[:, 1026:1154] = 1.0
    d = np.arange(FLEN) - 127
    valid = (d >= 0) & (d <= 2048)
    m = ((d <= 128).astype(np.int32) + ((d % 4 == 0) & (d <= 512)).astype(np.int32)
         + ((d % 16 == 0) & (d <= 2048)).astype(np.int32))
    m = np.where(valid, m, 0)
    ok = m > 0
    bucket = _t5_bucket_np(np.clip(d, 0, 2048))
    oh = np.zeros((32, FLEN), np.float32)
    oh[bucket[ok], np.nonzero(ok)[0]] = 1.0
    fcr = np.where(ok, np.log(np.maximum(m, 1)).astype(np.float32), np.float32(NEG)).astype(np.float32)
    fc = np.tile(fcr[None, :], (NH, 1)).astype(np.float32)
    ohs = np.zeros((32, 384), np.float32)
    for br, Dd in enumerate((1, 4, 16)):
        dj = Dd * (128 - np.arange(128))
        ohs[_t5_bucket_np(dj), br * 128 + np.arange(128)] = 1.0
    scst = np.zeros((8, 1024), np.float32)
    hp = np.arange(8)[:, None]
    scst[:, 0:512] = (np.arange(512)[None, :] // 64 == hp)
    col = np.arange(256)[None, :]
    scst[:, 512:768] = ((col % 16) == (col // 16)) * np.ones((8, 1), np.float32)
    scst[:, 768:776] = (np.arange(8)[None, :] == hp)
    return cst, oh, fc, ohs, scst


def _perm():
    cols = list(range(0, 2048)) + list(range(3712, 4224)) + list(range(3584, 3712))
    for c in range(4):
        cols += list(range(2048 + 128 * c, 2048 + 128 * c + 128))
        cols += list(range(2560 + 128 * c, 2560 + 128 * c + 128))
        cols += list(range(3072 + 128 * c, 3072 + 128 * c + 128))
    return np.array(cols, np.int64)


def _uchunks():
    ch = [np.arange(1536, 1664)]
    for c in range(4):
        ch.append(np.arange(128 * c, 128 * c + 128))
        ch.append(np.arange(512 + 128 * c, 512 + 128 * c + 128))
        ch.append(np.arange(1024 + 128 * c, 1024 + 128 * c + 128))
    return ch


_NC_CACHE = {}
DBG_MODE = None


def _get_nc(TP, NS, KEEP):
    key = (TP, NS, KEEP)
    if key not in _NC_CACHE:
        _NC_CACHE[key] = build(TP, NS, KEEP, dbg=DBG_MODE)
    return _NC_CACHE[key]


def _core_inputs(inp, b, s0, NS, TP):
    f = lambda a: np.ascontiguousarray(np.asarray(a, dtype=np.float32))
    cst, oh, fc, ohs, scst = _consts()
    perm = _perm()
    uch = _uchunks()
    mu = f(inp["mu_shift"])[0]
    ucat = np.concatenate(uch)
    pvec = np.zeros((128, 40), np.float32)
    for i, idx in enumerate(uch):
        pvec[:, i] = mu[idx]
    for c in range(4):
        sl = slice(128 * c, 128 * c + 128)
        pvec[:, 13 + c] = f(inp["w0"])[0][sl]
        pvec[:, 17 + c] = f(inp["a0"])[0][sl]
        pvec[:, 21 + c] = f(inp["k_k"])[0][sl]
        pvec[:, 25 + c] = f(inp["k_a"])[0][sl]
        pvec[:, 33 + c] = f(inp["r_k"])[0].reshape(-1)[sl]
    m = {
        "x": f(inp["x_prompt"][b][:TP]),
        "cv": f(np.concatenate([np.asarray(inp["c_prompt"])[b:b + 1], np.asarray(inp["c_sample"])[s0:s0 + NS]], 0)),
        "wperm": f(np.asarray(inp["w_in"])[0][:, perm]),
        "adaw": f(inp["ada_w"])[0],
        "adab": f(inp["ada_b"])[0][None, :],
        "normw": f(inp["norm_w"])[0][None, :],
        "fnw": f(inp["final_norm_w"])[None, :],
        "wout": f(inp["w_out"])[0],
        "relb": f(inp["rel_bias"]),
        "oh": oh, "fc": fc, "cst": cst, "pvec": pvec,
        "wlora": f(inp["w_lora_b"])[0],
        "alora": f(inp["a_lora_b"])[0],
        "lnwb": f(np.concatenate([np.asarray(inp["ln_x_w"])[0], np.asarray(inp["ln_x_b"])[0]])[None, :]),
        "xs": f(np.asarray(inp["x_sample"])[s0:s0 + NS, 0]),
        "cvs": f(np.asarray(inp["c_sample"])[s0:s0 + NS]),
        "ck": f(np.asarray(inp["cache_win_k"])[0, s0:s0 + NS]).reshape(NS, WIN, 512),
        "cvv": f(np.asarray(inp["cache_win_v"])[0, s0:s0 + NS]).reshape(NS, WIN, 512),
        "swkv": f(np.asarray(inp["state_wkv"])[0, s0:s0 + NS]).reshape(NS * NH, HD * HD),
        "ssh": f(np.asarray(inp["state_shift"])[0, s0:s0 + NS][:, ucat]),
        "prow": f(np.stack([np.asarray(inp["w0"])[0], np.asarray(inp["a0"])[0], np.asarray(inp["k_k"])[0], np.asarray(inp["k_a"])[0],
                            np.asarray(inp["r_k"])[0].reshape(-1), np.asarray(inp["ln_x_w"])[0], np.asarray(inp["ln_x_b"])[0],
                            np.zeros(512, np.float32)])),
        "muperm": f(mu[ucat][None, :]),
        "ohs": ohs, "scst": scst,
    }
    return m


def kernel(**inp):
    B, TP, _ = np.asarray(inp["x_prompt"]).shape
    NSAMP = np.asarray(inp["x_sample"]).shape[0]
    NS = NSAMP // 8
    KEEP = min(WIN, TP)
    nc = _get_nc(TP, NS, KEEP)
    in_maps = [_core_inputs(inp, i % B, i * NS, NS, TP) for i in range(8)]
    res = run_bass_kernel_spmd(nc, in_maps, core_ids=list(range(8))).results
    uch = _uchunks()
    y_p = np.stack([res[b]["y"] for b in range(B)]).astype(np.float32)
    wk = np.stack([res[b]["wk"].reshape(KEEP, NH, HD) for b in range(B)])[None].astype(np.float32)
    wv = np.stack([res[b]["wv"].reshape(KEEP, NH, HD) for b in range(B)])[None].astype(np.float32)
    wkv = np.stack([res[b]["wkv"] for b in range(B)])[None].astype(np.float32)
    shp = np.zeros((1, B, SHW), np.float32)
    for b in range(B):
        for i, idx in enumerate(uch):
            shp[0, b, idx] = res[b]["shp"][i]
    ucat = np.concatenate(uch)
    y_s = np.concatenate([res[i]["ys"] for i in range(8)], 0).reshape(NSAMP, 1, D).astype(np.float32)
    wks = np.concatenate([res[i]["ks"] for i in range(8)], 0).reshape(1, NSAMP, 1, NH, HD).astype(np.float32)
    wvs = np.concatenate([res[i]["vs"] for i in range(8)], 0).reshape(1, NSAMP, 1, NH, HD).astype(np.float32)
    wkvs = np.concatenate([res[i]["wkvs"] for i in range(8)], 0).reshape(1, NSAMP, NH, HD, HD).astype(np.float32)
    shs = np.zeros((1, NSAMP, SHW), np.float32)
    shs[0][:, ucat] = np.concatenate([res[i]["shs"] for i in range(8)], 0)
    return (y_p, y_s, wk, wv, wks, wvs, wkv, wkvs, shp, shs)
```
